# Optimizing a Trainium2 kernel written in Bass

```python
import jax, jax.numpy as jnp
from jax import lax
import numpy as np

D_MODEL = 1024
BATCH = 2
SEQ = 8192
DEPTH = 4
DEC_BATCH = 16
DEC_SEQ = 64
PAST_LEN = 2048

CHUNK = 64
N_MIXERS = 3
LAYER_MIXER = tuple(i % N_MIXERS for i in range(DEPTH))
EPS = 1e-6

D_RNN = D_MODEL
LRU_BLOCKS = 8
LRU_BW = D_RNN // LRU_BLOCKS
LRU_CONV = 4
LRU_C = 8.0
N_LRU_LAYERS = len([i for i in range(DEPTH) if i % N_MIXERS == 0])

D_INNER = 2 * D_MODEL
SSD_HEAD_DIM = 64
SSD_HEADS = D_INNER // SSD_HEAD_DIM
SSD_GROUPS = 4
SSD_HPG = SSD_HEADS // SSD_GROUPS
SSD_STATE = 128
SSD_CONV = 4
SSD_CONV_DIM = D_INNER + 2 * SSD_GROUPS * SSD_STATE
SSD_IN_DIM = D_INNER + SSD_CONV_DIM + SSD_HEADS
SSD_CHUNK = CHUNK

FOX_HEADS = 16
FOX_HEAD_DIM = D_MODEL // FOX_HEADS
FOX_QBLOCK = 128
FOX_SCALE = FOX_HEAD_DIM ** -0.5
FOX_BF_LO = 2.2
FOX_BF_HI = 6.9

D_FF = 2816
FFN_CONV = 3

kernel_name = 'hybrid_streaming_encoder_step'


def _rmsnorm(x, g):
    xf = x.astype(jnp.float32)
    y = xf * lax.rsqrt(jnp.mean(xf * xf, axis=-1, keepdims=True) + EPS)
    return (y * g.astype(jnp.float32)).astype(x.dtype)


def _ada(c, w, b):
    m = jax.nn.silu(c) @ w + b
    return jnp.split(m[:, None, :], 6, axis=-1)


def _causal_dwconv(x, prev, w, b):
    width = w.shape[0]
    T = x.shape[1]
    xx = jnp.concatenate([prev.astype(x.dtype), x], axis=1)
    y = b + w[0] * xx[:, 0:T]
    for k in range(1, width):
        y = y + w[k] * xx[:, k:k + T]
    return y, xx[:, T:]


def _linear_scan(a, b, h0):
    def comb(l, r):
        return l[0] * r[0], r[0] * l[1] + r[1]
    A, Bc = lax.associative_scan(comb, (a, b), axis=1)
    return A * h0.astype(jnp.float32)[:, None] + Bc


def _rglru_mixer(h, conv_prev, h_prev, w_x, b_x, w_y, b_y, conv_w, conv_b, w_a, b_a, w_i, b_i, lam, w_o, b_o):
    Bn, T, _ = h.shape
    gate = jax.nn.gelu(h @ w_y + b_y)
    u, conv_new = _causal_dwconv(h @ w_x + b_x, conv_prev, conv_w, conv_b)
    ub = u.reshape(Bn, T, LRU_BLOCKS, LRU_BW)
    r = jax.nn.sigmoid(jnp.einsum('btnk,nkj->btnj', ub, w_a).reshape(Bn, T, D_RNN) + b_a)
    i = jax.nn.sigmoid(jnp.einsum('btnk,nkj->btnj', ub, w_i).reshape(Bn, T, D_RNN) + b_i)
    log_a = -LRU_C * r.astype(jnp.float32) * jax.nn.softplus(-lam.astype(jnp.float32))
    a = jnp.exp(log_a)
    bterm = jnp.sqrt(-jnp.expm1(2.0 * log_a)) * (i * u).astype(jnp.float32)
    hs = _linear_scan(a, bterm, h_prev)
    y = (hs.astype(h.dtype) * gate) @ w_o + b_o
    return y, conv_new, hs[:, -1]


def _ssd_scan(x, dt, A, Bm, Cm, s0, L):
    b, T = x.shape[:2]
    nc = T // L
    x = x.reshape(b, nc, L, SSD_GROUPS, SSD_HPG, SSD_HEAD_DIM)
    dt = dt.reshape(b, nc, L, SSD_GROUPS, SSD_HPG)
    Bm = Bm.reshape(b, nc, L, SSD_GROUPS, SSD_STATE)
    Cm = Cm.reshape(b, nc, L, SSD_GROUPS, SSD_STATE)
    cum = jnp.cumsum(dt * A, axis=2)
    seg = cum[:, :, :, None] - cum[:, :, None, :]
    causal = jnp.tril(jnp.ones((L, L), bool))[:, :, None, None]
    decay = jnp.exp(jnp.where(causal, seg, -jnp.inf))
    xdt = x * dt[..., None]
    cb = jnp.einsum('bclgn,bcsgn->bclsg', Cm, Bm)
    y_diag = jnp.einsum('bclsg,bclsge,bcsgep->bclgep', cb, decay, xdt)
    decay_end = jnp.exp(cum[:, :, -1:] - cum)
    states = jnp.einsum('bclgn,bclge,bclgep->bcgepn', Bm, decay_end, xdt)
    chunk_decay = jnp.exp(cum[:, :, -1])

    def step(s, inp):
        dec, st = inp
        return s * dec[..., None, None] + st, s

    s_final, s_in = lax.scan(step, s0, (jnp.moveaxis(chunk_decay, 1, 0), jnp.moveaxis(states, 1, 0)))
    s_in = jnp.moveaxis(s_in, 0, 1)
    y_off = jnp.einsum('bclgn,bcgepn,bclge->bclgep', Cm, s_in, jnp.exp(cum))
    y = (y_diag + y_off).reshape(b, T, SSD_GROUPS, SSD_HPG, SSD_HEAD_DIM)
    return y, s_final


def _ssd_mixer(h, conv_prev, ssm_prev, w_in, conv_w, conv_b, dt_bias, a_log, d_skip, norm_g, w_out):
    Bn, T, _ = h.shape
    zxbcdt = h @ w_in
    z = zxbcdt[..., :D_INNER]
    xbc = zxbcdt[..., D_INNER:D_INNER + SSD_CONV_DIM]
    dt_raw = zxbcdt[..., D_INNER + SSD_CONV_DIM:]
    xbc, conv_new = _causal_dwconv(xbc, conv_prev, conv_w, conv_b)
    xbc = jax.nn.silu(xbc).astype(jnp.float32)
    gn = SSD_GROUPS * SSD_STATE
    xs = xbc[..., :D_INNER].reshape(Bn, T, SSD_GROUPS, SSD_HPG, SSD_HEAD_DIM)
    Bm = xbc[..., D_INNER:D_INNER + gn].reshape(Bn, T, SSD_GROUPS, SSD_STATE)
    Cm = xbc[..., D_INNER + gn:].reshape(Bn, T, SSD_GROUPS, SSD_STATE)
    dt = jax.nn.softplus(dt_raw.astype(jnp.float32) + dt_bias.astype(jnp.float32)).reshape(Bn, T, SSD_GROUPS, SSD_HPG)
    A = -jnp.exp(a_log.astype(jnp.float32)).reshape(SSD_GROUPS, SSD_HPG)
    s0 = ssm_prev.astype(jnp.float32).reshape(Bn, SSD_GROUPS, SSD_HPG, SSD_HEAD_DIM, SSD_STATE)
    L = SSD_CHUNK if T % SSD_CHUNK == 0 else T
    y, s_new = _ssd_scan(xs, dt, A, Bm, Cm, s0, L)
    y = y + d_skip.astype(jnp.float32).reshape(SSD_GROUPS, SSD_HPG)[:, :, None] * xs
    y = y.reshape(Bn, T, D_INNER)
    y = _rmsnorm(y * jax.nn.silu(z.astype(jnp.float32)), norm_g)
    return y.astype(h.dtype) @ w_out, conv_new, s_new.reshape(Bn, SSD_HEADS, SSD_HEAD_DIM, SSD_STATE)


def _fox_block(q_blk, fq_blk, qpos_blk, k_all, v_all, fk_t, kpos):
    s = jnp.einsum('bqhd,bkhd->bhqk', q_blk, k_all).astype(jnp.float32) * FOX_SCALE
    s = s + jnp.swapaxes(fq_blk, 1, 2)[..., :, None] - fk_t[..., None, :]
    s = jnp.where(kpos[None, :] <= qpos_blk[:, None], s, -jnp.inf)
    p = jax.nn.softmax(s, axis=-1)
    return jnp.einsum('bhqk,bkhd->bqhd', p.astype(v_all.dtype), v_all)


def _fox_mixer(h, k_past, v_past, lf_past, w_qkv, w_f, b_f, w_o):
    Bn, T, _ = h.shape
    qkv = (h @ w_qkv).reshape(Bn, T, 3, FOX_HEADS, FOX_HEAD_DIM)
    q, k, v = qkv[:, :, 0], qkv[:, :, 1], qkv[:, :, 2]
    lf = jax.nn.log_sigmoid((h @ w_f).astype(jnp.float32) + b_f.astype(jnp.float32))
    if k_past is None:
        k_all, v_all, lf_all = k, v, lf
    else:
        k_all = jnp.concatenate([k_past.astype(k.dtype), k], axis=1)
        v_all = jnp.concatenate([v_past.astype(v.dtype), v], axis=1)
        lf_all = jnp.concatenate([lf_past.astype(jnp.float32), lf], axis=1)
    P = k_all.shape[1] - T
    F = jnp.cumsum(lf_all, axis=1)
    fk_t = jnp.swapaxes(F, 1, 2)
    fq = F[:, P:]
    kpos = jnp.arange(P + T)
    qpos = P + jnp.arange(T)
    if T <= FOX_QBLOCK:
        o = _fox_block(q, fq, qpos, k_all, v_all, fk_t, kpos)
    else:
        nb = T // FOX_QBLOCK
        qs = jnp.swapaxes(q.reshape(Bn, nb, FOX_QBLOCK, FOX_HEADS, FOX_HEAD_DIM), 0, 1)
        fqs = jnp.swapaxes(fq.reshape(Bn, nb, FOX_QBLOCK, FOX_HEADS), 0, 1)
        qps = qpos.reshape(nb, FOX_QBLOCK)
        o = lax.map(lambda a: _fox_block(a[0], a[1], a[2], k_all, v_all, fk_t, kpos), (qs, fqs, qps))
        o = jnp.swapaxes(o, 0, 1)
    y = o.reshape(Bn, T, D_MODEL) @ w_o
    return y, k, v, lf


def _conv_ffn(h, conv_prev, w_gate, w_up, conv_w, conv_b, w_down):
    g, conv_new = _causal_dwconv(h @ w_gate, conv_prev, conv_w, conv_b)
    return (jax.nn.gelu(g) * (h @ w_up)) @ w_down, conv_new


def _run_trunk(x, c, prev, W):
    Bn = x.shape[0]
    fresh = prev is None
    new = {}
    ffn_bufs = []
    for layer in range(DEPTH):
        sh1, sc1, g1, sh2, sc2, g2 = _ada(c, W['ada_w'][layer], W['ada_b'][layer])
        ng = W['norm_g'][layer]
        h = _rmsnorm(x, ng[0]) * (1 + sc1) + sh1
        kind = LAYER_MIXER[layer]
        tag = 'l%d' % layer
        if kind == 0:
            j = layer // N_MIXERS
            conv_prev = jnp.zeros((Bn, LRU_CONV - 1, D_RNN), x.dtype) if fresh else prev[tag + '_conv']
            h_prev = jnp.zeros((Bn, D_RNN), jnp.float32) if fresh else prev[tag + '_h']
            y, cn, hn = _rglru_mixer(h, conv_prev, h_prev, W['lru_w_x'][j], W['lru_b_x'][j], W['lru_w_y'][j], W['lru_b_y'][j],
                                     W['lru_conv_w'][j], W['lru_conv_b'][j], W['lru_w_a'][j], W['lru_b_a'][j],
                                     W['lru_w_i'][j], W['lru_b_i'][j], W['lru_lambda'][j], W['lru_w_o'][j], W['lru_b_o'][j])
            new[tag + '_conv'] = cn
            new[tag + '_h'] = hn
        elif kind == 1:
            conv_prev = jnp.zeros((Bn, SSD_CONV - 1, SSD_CONV_DIM), x.dtype) if fresh else prev[tag + '_conv']
            s_prev = jnp.zeros((Bn, SSD_HEADS, SSD_HEAD_DIM, SSD_STATE), jnp.float32) if fresh else prev[tag + '_ssm']
            y, cn, sn = _ssd_mixer(h, conv_prev, s_prev, W['ssd_w_in'], W['ssd_conv_w'], W['ssd_conv_b'], W['ssd_dt_bias'],
                                   W['ssd_a_log'], W['ssd_d'], W['ssd_norm_g'], W['ssd_w_out'])
            new[tag + '_conv'] = cn
            new[tag + '_ssm'] = sn
        else:
            kp = None if fresh else prev[tag + '_k']
            vp = None if fresh else prev[tag + '_v']
            lp = None if fresh else prev[tag + '_logf']
            y, kn, vn, ln = _fox_mixer(h, kp, vp, lp, W['fox_w_qkv'], W['fox_w_f'], W['fox_b_f'], W['fox_w_o'])
            new[tag + '_k'] = kn
            new[tag + '_v'] = vn
            new[tag + '_logf'] = ln
        x = x + g1 * _rmsnorm(y, ng[1])
        h = _rmsnorm(x, ng[2]) * (1 + sc2) + sh2
        ffn_prev = jnp.zeros((Bn, FFN_CONV - 1, D_FF), x.dtype) if fresh else prev['ffn_conv'][layer]
        y, buf = _conv_ffn(h, ffn_prev, W['ffn_w_gate'][layer], W['ffn_w_up'][layer], W['ffn_conv_w'][layer],
                           W['ffn_conv_b'][layer], W['ffn_w_down'][layer])
        ffn_bufs.append(buf)
        x = x + g2 * _rmsnorm(y, ng[3])
    new['ffn_conv'] = jnp.stack(ffn_bufs)
    return x, new


def setup_inputs(seed: int = 0) -> dict:
    key = jax.random.key(seed)
    ks = iter(jax.random.split(key, 64))
    f32 = jnp.float32

    def nrm(shape, scale):
        return scale * jax.random.normal(next(ks), shape, f32)

    def uni(shape, lo, hi):
        return jax.random.uniform(next(ks), shape, f32, lo, hi)

    D = D_MODEL
    L = N_LRU_LAYERS
    dt0 = jnp.exp(uni((SSD_HEADS,), float(np.log(1e-3)), float(np.log(1e-1))))
    return {
        'x_prompt': nrm((BATCH, SEQ, D), 1.0),
        'x_sample': nrm((DEC_BATCH, DEC_SEQ, D), 1.0),
        'c_prompt': nrm((BATCH, D), 1.0),
        'c_sample': nrm((DEC_BATCH, D), 1.0),
        'state_l0_conv': nrm((DEC_BATCH, LRU_CONV - 1, D_RNN), 1.0),
        'state_l0_h': nrm((DEC_BATCH, D_RNN), 0.3),
        'state_l1_conv': nrm((DEC_BATCH, SSD_CONV - 1, SSD_CONV_DIM), 1.0),
        'state_l1_ssm': nrm((DEC_BATCH, SSD_HEADS, SSD_HEAD_DIM, SSD_STATE), 0.1),
        'cache_l2_k': nrm((DEC_BATCH, PAST_LEN, FOX_HEADS, FOX_HEAD_DIM), 1.0),
        'cache_l2_v': nrm((DEC_BATCH, PAST_LEN, FOX_HEADS, FOX_HEAD_DIM), 1.0),
        'cache_l2_logf': jax.nn.log_sigmoid(uni((DEC_BATCH, PAST_LEN, FOX_HEADS), FOX_BF_LO, FOX_BF_HI)),
        'state_l3_conv': nrm((DEC_BATCH, LRU_CONV - 1, D_RNN), 1.0),
        'state_l3_h': nrm((DEC_BATCH, D_RNN), 0.3),
        'state_ffn_conv': nrm((DEPTH, DEC_BATCH, FFN_CONV - 1, D_FF), 1.0),
        'ada_w': nrm((DEPTH, D, 6 * D), D ** -0.5),
        'ada_b': nrm((DEPTH, 6 * D), 0.02),
        'norm_g': 1.0 + nrm((DEPTH, 4, D), 0.02),
        'lru_w_x': nrm((L, D, D_RNN), D ** -0.5),
        'lru_b_x': nrm((L, D_RNN), 0.02),
        'lru_w_y': nrm((L, D, D_RNN), D ** -0.5),
        'lru_b_y': nrm((L, D_RNN), 0.02),
        'lru_conv_w': nrm((L, LRU_CONV, D_RNN), LRU_CONV ** -0.5),
        'lru_conv_b': nrm((L, D_RNN), 0.02),
        'lru_w_a': nrm((L, LRU_BLOCKS, LRU_BW, LRU_BW), LRU_BW ** -0.5),
        'lru_b_a': nrm((L, D_RNN), 0.02),
        'lru_w_i': nrm((L, LRU_BLOCKS, LRU_BW, LRU_BW), LRU_BW ** -0.5),
        'lru_b_i': nrm((L, D_RNN), 0.02),
        'lru_lambda': uni((L, D_RNN), 4.3, 9.0),
        'lru_w_o': nrm((L, D_RNN, D), D_RNN ** -0.5),
        'lru_b_o': nrm((L, D), 0.02),
        'ssd_w_in': nrm((D, SSD_IN_DIM), D ** -0.5),
        'ssd_conv_w': nrm((SSD_CONV, SSD_CONV_DIM), SSD_CONV ** -0.5),
        'ssd_conv_b': nrm((SSD_CONV_DIM,), 0.02),
        'ssd_dt_bias': dt0 + jnp.log(-jnp.expm1(-dt0)),
        'ssd_a_log': jnp.log(uni((SSD_HEADS,), 1.0, 16.0)),
        'ssd_d': 1.0 + nrm((SSD_HEADS,), 0.1),
        'ssd_norm_g': 1.0 + nrm((D_INNER,), 0.02),
        'ssd_w_out': nrm((D_INNER, D), D_INNER ** -0.5),
        'fox_w_qkv': nrm((D, 3 * D), D ** -0.5),
        'fox_w_f': nrm((D, FOX_HEADS), 0.1 * D ** -0.5),
        'fox_b_f': uni((FOX_HEADS,), FOX_BF_LO, FOX_BF_HI),
        'fox_w_o': nrm((D, D), D ** -0.5),
        'ffn_w_gate': nrm((DEPTH, D, D_FF), D ** -0.5),
        'ffn_w_up': nrm((DEPTH, D, D_FF), D ** -0.5),
        'ffn_conv_w': nrm((DEPTH, FFN_CONV, D_FF), FFN_CONV ** -0.5),
        'ffn_conv_b': nrm((DEPTH, D_FF), 0.02),
        'ffn_w_down': nrm((DEPTH, D_FF, D), D_FF ** -0.5),
    }


def reference(x_prompt, x_sample, c_prompt, c_sample,
              state_l0_conv, state_l0_h, state_l1_conv, state_l1_ssm,
              cache_l2_k, cache_l2_v, cache_l2_logf, state_l3_conv, state_l3_h, state_ffn_conv,
              ada_w, ada_b, norm_g,
              lru_w_x, lru_b_x, lru_w_y, lru_b_y, lru_conv_w, lru_conv_b, lru_w_a, lru_b_a,
              lru_w_i, lru_b_i, lru_lambda, lru_w_o, lru_b_o,
              ssd_w_in, ssd_conv_w, ssd_conv_b, ssd_dt_bias, ssd_a_log, ssd_d, ssd_norm_g, ssd_w_out,
              fox_w_qkv, fox_w_f, fox_b_f, fox_w_o,
              ffn_w_gate, ffn_w_up, ffn_conv_w, ffn_conv_b, ffn_w_down):
    W = dict(ada_w=ada_w, ada_b=ada_b, norm_g=norm_g,
             lru_w_x=lru_w_x, lru_b_x=lru_b_x, lru_w_y=lru_w_y, lru_b_y=lru_b_y,
             lru_conv_w=lru_conv_w, lru_conv_b=lru_conv_b, lru_w_a=lru_w_a, lru_b_a=lru_b_a,
             lru_w_i=lru_w_i, lru_b_i=lru_b_i, lru_lambda=lru_lambda, lru_w_o=lru_w_o, lru_b_o=lru_b_o,
             ssd_w_in=ssd_w_in, ssd_conv_w=ssd_conv_w, ssd_conv_b=ssd_conv_b, ssd_dt_bias=ssd_dt_bias,
             ssd_a_log=ssd_a_log, ssd_d=ssd_d, ssd_norm_g=ssd_norm_g, ssd_w_out=ssd_w_out,
             fox_w_qkv=fox_w_qkv, fox_w_f=fox_w_f, fox_b_f=fox_b_f, fox_w_o=fox_w_o,
             ffn_w_gate=ffn_w_gate, ffn_w_up=ffn_w_up, ffn_conv_w=ffn_conv_w, ffn_conv_b=ffn_conv_b,
             ffn_w_down=ffn_w_down)
    y_prompt, p = _run_trunk(x_prompt, c_prompt, None, W)
    prev = dict(l0_conv=state_l0_conv, l0_h=state_l0_h, l1_conv=state_l1_conv, l1_ssm=state_l1_ssm,
                l2_k=cache_l2_k, l2_v=cache_l2_v, l2_logf=cache_l2_logf,
                l3_conv=state_l3_conv, l3_h=state_l3_h, ffn_conv=state_ffn_conv)
    y_sample, s = _run_trunk(x_sample, c_sample, prev, W)
    return (y_prompt, y_sample,
            p['l0_conv'], p['l0_h'], p['l1_conv'], p['l1_ssm'], p['l2_k'], p['l2_v'], p['l2_logf'],
            p['l3_conv'], p['l3_h'], p['ffn_conv'],
            s['l0_conv'], s['l0_h'], s['l1_conv'], s['l1_ssm'], s['l2_k'], s['l2_v'], s['l2_logf'],
            s['l3_conv'], s['l3_h'], s['ffn_conv'])
```

```python
import numpy as np
from contextlib import ExitStack
import concourse.bass as bass
import concourse.mybir as mybir
from concourse.bass_utils import run_bass_kernel_spmd

F32 = mybir.dt.float32
BF16 = mybir.dt.bfloat16
ALU = mybir.AluOpType
AF = mybir.ActivationFunctionType

D = 1024
NCH = 8
TP = 2048
TS = 64
T = TP + 2 * TS
SEQS = [(0, TP), (TP, TS), (TP + TS, TS)]
BLKS = [(0, 512), (512, 512), (1024, 512), (1536, 512), (2048, 128)]
DFF = 2816
NFF = 22
EPS = 1e-6
NCORES = 8


class Rec:
    def __getattr__(self, name):
        def f(*a, **kw):
            self.call = (name, a, kw)
            return self
        return f


def _rec(fn):
    r = Rec()
    fn(r)
    name, a, kw = r.call
    return lambda e: getattr(e, name)(*a, **kw)


class Prog:
    def __init__(self, nc, st):
        self.nc = nc
        self.names = ['pe', 'act', 'dve', 'pool', 'sp']
        self.lists = {e: [] for e in self.names}
        self.sem = {e: st.enter_context(nc.semaphore('c_' + e)) for e in ['pe', 'act', 'dve', 'pool']}
        self.cnt = {e: 0 for e in self.sem}
        self.ND = 32
        self.dsem = [st.enter_context(nc.semaphore('d%d' % i)) for i in range(self.ND)]
        self.dcnt = [0] * self.ND
        self.dnext = {'sp': 0, 'pool': 0}
        self.dbase = {'sp': 0, 'pool': 16}
        self.csem = st.enter_context(nc.semaphore('cc'))
        self.ccnt = 0
        self.seen = {e: {} for e in self.names}
        self.res = {}

    def _deps(self, reads, writes):
        deps = {}
        for r in reads:
            s = self.res.get(r)
            if s and s[0]:
                k, v = s[0]
                deps[k] = max(deps.get(k, 0), v)
        for w in writes:
            s = self.res.get(w)
            if s:
                if s[0]:
                    k, v = s[0]
                    deps[k] = max(deps.get(k, 0), v)
                for k, v in s[1].items():
                    deps[k] = max(deps.get(k, 0), v)
        return deps

    def _need(self, eng, deps):
        for k, v in deps.items():
            if k == 'pe' and eng == 'pe':
                continue
            if self.seen[eng].get(k, 0) < v:
                self.lists[eng].append(('wait', k, v))
                self.seen[eng][k] = v

    def _mark(self, tag, reads, writes):
        k, v = tag
        for r in reads:
            s = self.res.setdefault(r, [None, {}])
            s[1][k] = max(s[1].get(k, 0), v)
        for w in writes:
            self.res[w] = [tag, {}]

    def op(self, eng, fn, r=(), w=()):
        self._need(eng, self._deps(r, w))
        self.cnt[eng] += 1
        self.lists[eng].append(('op', _rec(fn), eng))
        self._mark((eng, self.cnt[eng]), r, w)

    def dma(self, q, fn, r=(), w=()):
        i = self.dbase[q] + self.dnext[q]
        self.dnext[q] = (self.dnext[q] + 1) % 16
        deps = self._deps(r, w)
        if self.dcnt[i] > 0:
            deps[('d', i)] = max(deps.get(('d', i), 0), 16 * self.dcnt[i])
        self._need(q, deps)
        self.dcnt[i] += 1
        self.lists[q].append(('dma', _rec(fn), i))
        self._mark((('d', i), 16 * self.dcnt[i]), r, w)

    def coll(self, fn, r=(), w=()):
        self._need('pool', self._deps(r, w))
        self.ccnt += 1
        self.lists['pool'].append(('coll', _rec(fn)))
        self._mark(('cc', self.ccnt), r, w)

    def fence(self):
        allv = {e: self.cnt[e] for e in self.cnt if self.cnt[e]}
        for i in range(self.ND):
            if self.dcnt[i]:
                allv[('d', i)] = 16 * self.dcnt[i]
        if self.ccnt:
            allv['cc'] = self.ccnt
        for e in self.names:
            self._need(e, dict(allv))

    def _semof(self, k):
        if k == 'cc':
            return self.csem
        if isinstance(k, tuple):
            return self.dsem[k[1]]
        return self.sem[k]

    def finish(self):
        final = {('d', i): 16 * self.dcnt[i] for i in range(self.ND) if self.dcnt[i]}
        self._need('sp', final)
        nc = self.nc
        engs = {'pe': 'tensor', 'act': 'scalar', 'dve': 'vector', 'pool': 'gpsimd', 'sp': 'sync'}
        with nc.Block() as block:
            for name in self.names:
                lst = self.lists[name]

                def run(e, lst=lst):
                    for it in lst:
                        if it[0] == 'wait':
                            e.wait_ge(self._semof(it[1]), it[2])
                        elif it[0] == 'op':
                            it[1](e).then_inc(self.sem[it[2]], 1)
                        elif it[0] == 'dma':
                            it[1](e).then_inc(self.dsem[it[2]], 16)
                        else:
                            it[1](e).then_inc(self.csem)
                getattr(block, engs[name])(run)


def keys(name, c, lo, hi, gran=512):
    return [(name, c, b) for b in range(lo // gran, (hi - 1) // gran + 1)]


class K:
    pass


def build(nlayers=4, dbg=False):
    nc = bass.Bass("TRN2", target_bir_lowering=False)
    st = ExitStack()
    P = Prog(nc, st)
    k = K()
    k.nc, k.P, k.st = nc, P, st

    def din(name, shape):
        return nc.dram_tensor(name, list(shape), F32, kind="ExternalInput").ap()

    def dout(name, shape):
        return nc.dram_tensor(name, list(shape), F32, kind="ExternalOutput").ap()

    k.sbtot = 0

    def sb(name, shape, dt=F32):
        nb = int(np.prod(shape[1:])) * (2 if dt == BF16 else 4)
        k.sbtot += nb
        try:
            return st.enter_context(nc.sbuf_tensor(name, list(shape), dt))
        except AssertionError:
            print("SBUF OOM at", name, nb, "total requested so far", k.sbtot)
            raise

    def ps(name, shape, dt=F32):
        return st.enter_context(nc.psum_tensor(name, list(shape), dt))

    xT = din("xT", [128, NCH, T])
    cT = din("cT", [128, NCH, 3])
    msk = din("msk", [128, 16])
    ada_w = [din("ada_w%d" % i, [D, 6 * D]) for i in range(4)]
    ada_bT = din("ada_bT", [128, 4, 48])
    norm_gT = din("norm_gT", [128, 4, 4, NCH])
    lru_w_x = din("lru_w_x", [2, D, D]); lru_w_y = din("lru_w_y", [2, D, D]); lru_w_o = din("lru_w_o", [2, D, D])
    lru_w_a = din("lru_w_a", [2, 8, 128, 128]); lru_w_i = din("lru_w_i", [2, 8, 128, 128])
    lru_vT = din("lru_vT", [128, 2, 11, NCH])
    st_lconv = din("st_lconv", [128, 2, NCH, 2, 3])
    st_lh = din("st_lh", [128, 2, NCH, 2])
    ffn_w_gate = [din("ffn_w_gate%d" % i, [D, DFF]) for i in range(4)]
    ffn_w_up = [din("ffn_w_up%d" % i, [D, DFF]) for i in range(4)]
    ffn_w_down = [din("ffn_w_down%d" % i, [DFF, D]) for i in range(4)]
    ffn_vT = din("ffn_vT", [128, 4, 4, NFF])
    st_ffn = din("st_ffn", [128, 4, NFF, 2, 2])
    o_ffn = dout("o_ffn", [128, 4, NFF, 3, 2])
    ssd_w_in = din("ssd_w_in", [D, 5152]); ssd_w_out = din("ssd_w_out", [2048, D])
    ssd_cvT = din("ssd_cvT", [128, 5, 24])
    ssd_rep = din("ssd_rep", [128, 96])
    ssd_ngT = din("ssd_ngT", [128, 16])
    st_sconv = din("st_sconv", [128, 24, 2, 3])
    st_ssm = din("st_ssm", [128, 2, 2048])
    cst = din("cst", [128, 4, 128])
    o_sconv = dout("o_sconv", [128, 24, 3, 3])
    o_ssm = dout("o_ssm", [128, 3, 2048])
    fox_w_qkv = din("fox_w_qkv", [D, 3 * D]); fox_w_o = din("fox_w_o", [D, D]); fox_w_f = din("fox_w_f", [D, 16])
    fox_bf = din("fox_bf", [128, 16])
    kcT = din("kcT", [128, 2, NCH, 2048])
    vc = din("vc", [2, 2048, D])
    lfc = din("lfc", [128, 2, 16, 16])
    tmk = din("tmk", [128, 4, 512])
    t64 = din("t64", [128, 64])
    o_kT = dout("o_kT", [128, NCH, T])
    o_v = dout("o_v", [T, D])
    o_lf = dout("o_lf", [T, 16])
    kv_in = [nc.dram_tensor("kv_in%d" % c, [128, 4096], BF16) for c in range(NCH)]
    kv_out = [nc.dram_tensor("kv_out%d" % c, [4 * 128, 4096], BF16) for c in range(NCH)]
    zd = nc.dram_tensor("zd", [T, 2048], BF16).ap()
    yT = dout("yT", [128, NCH, T])
    o_lconv = dout("o_lconv", [128, 2, NCH, 3, 3])
    o_lh = dout("o_lh", [128, 2, NCH, 3])
    GROUPS = [[0, 1, 2, 3], [4, 5, 6, 7]]

    xd = nc.dram_tensor("xd", [128, NCH, T], F32).ap()
    h = sb("h", [128, NCH, T], BF16)
    big = sb("big", [128, 24 * T], BF16)
    HW = T + 9
    bufA = big[:, 0:NCH * T].rearrange("p (c t) -> p c t", t=T)
    bufB = big[:, NCH * T:NCH * T + NCH * HW].rearrange("p (c t) -> p c t", t=HW)
    hid = big[:, 0:NFF * T].rearrange("p (c t) -> p c t", t=T)
    xbc = big[:, :].rearrange("p (c t) -> p c t", t=T)
    HOFF = [3, 3 + TP + 3, 3 + TP + 3 + TS + 3]
    wpf = [sb("wp%d" % i, [128, NCH * 512], BF16) for i in range(2)]
    wsm = [sb("wsm%d" % i, [128, 2, 128], BF16) for i in range(2)]
    ones = sb("ones", [128, 128], BF16)
    siluc = sb("siluc", [128, NCH, 3], BF16)
    ctmp = sb("ctmp", [128, NCH, 3])
    ada = sb("ada", [128, 48, 3])
    adab = sb("adab", [128, 4, 48])
    ng = sb("ng", [128, 4, 4, NCH])
    gsc = sb("gsc", [128, 4, NCH, 3])
    lruv = sb("lruv", [128, 2, 11, NCH])
    cfac = sb("cfac", [128, 2, 2, NCH])
    mk = sb("mk", [128, 16])
    rstd = sb("rstd", [128, 512])
    NSQ = 4
    sq = sb("sq", [128, NSQ, 512], BF16)
    tmpf0 = sb("tmpf0", [128, T + 16])
    tmpb = sb("tmpb", [128, T], BF16)
    blk16 = sb("blk16", [128, NCH, 512])
    small = sb("small", [128, 64])
    XW = 80
    xg = sb("xg", [128, 4, XW])
    hin = sb("hin", [128, NCH, 3])
    tails = sb("tails", [128, NCH, 3, 3])
    hl = sb("hl", [128, NCH, 3])
    ffv = sb("ffv", [128, 4, 4, NFF])
    fix = sb("fix", [128, NFF, 4])
    gt = sb("gt", [128, NFF, 3, 2])
    fprev = sb("fprev", [128, NFF, 2])
    ftmp = sb("ftmp", [128, 4, NFF])
    cstf = sb("cstf", [128, 4, 128])
    ksm = sb("ksm", [128, NCH, 128], BF16)
    vsm = sb("vsm", [128, D], BF16)
    lftm = sb("lftm", [128, 17, 16])
    wf = sb("wf", [128, NCH, 16], BF16)
    ones1 = sb("ones1", [128, 64], BF16)
    fbf = sb("fbf", [128, 16])
    t64b = sb("t64b", [128, 64], BF16)
    nbc = sb("nbc", [128, 8])
    ident = sb("ident", [128, 128], BF16)
    scv = sb("scv", [128, 5, 24])
    srep = sb("srep", [128, 96])
    sng = sb("sng", [128, 16])
    wdt = sb("wdt", [128, NCH, 32], BF16)
    stail = sb("stail", [128, 24, 3, 3])
    sfirst = sb("sfirst", [128, 24, 3])
    sprev = sb("sprev", [128, 24, 3])
    sfx = sb("sfx", [128, 4, 24])
    banks = [ps("bank%d" % i, [128, 512]) for i in range(8)]
    k.bank_i = 0

    def tmpf(i):
        return h[:, 2 * (i - 1):2 * (i - 1) + 2, :].rearrange("p a t -> p (a t)").bitcast(F32)

    def tk(i):
        if i == 0:
            return keys('tmpf0', 0, 0, T)
        return keys('h', 2 * (i - 1), 0, T) + keys('h', 2 * (i - 1) + 1, 0, T)

    def bank():
        b = k.bank_i
        k.bank_i = (b + 1) % 8
        return b

    def load(dst, src, rk=(), wk=()):
        P.dma('sp', lambda e: e.dma_start(out=dst, in_=src), r=rk, w=wk)

    for c in range(NCH):
        P.dma('sp', lambda e, c=c: e.dma_start(out=xd[:, c, :], in_=xT[:, c, :]), w=keys('xd', c, 0, T))
    load(ctmp[:, :, :], cT[:, :, :], wk=['ctmp'])
    load(adab[:, :, :], ada_bT[:, :, :], wk=['adab'])
    load(ng[:, :, :, :], norm_gT[:, :, :, :], wk=['ng'])
    load(lruv[:, :, :, :], lru_vT[:, :, :, :], wk=['lruv'])
    load(ffv[:, :, :, :], ffn_vT[:, :, :, :], wk=['ffv'])
    load(mk[:, :], msk[:, :], wk=['mk'])
    load(cstf[:, :, :], cst[:, :, :], wk=['cstf'])
    load(fbf[:, :], fox_bf[:, :], wk=['fbf'])
    P.dma('sp', lambda e: e.dma_start(out=small[:, 48:50], in_=kcT[:, 0, 0, 0:2]), w=['touch'])
    P.dma('sp', lambda e: e.dma_start(out=small[:, 50:52], in_=vc[0, 0:128, 0:2]), w=['touch'])
    P.dma('sp', lambda e: e.dma_start(out=small[:, 52:54], in_=lfc[:, 0, 0, 0:2]), w=['touch'])
    P.dma('sp', lambda e: e.dma_start(out=small[:, 54:56], in_=tmk[:, 0, 0:2]), w=['touch'])
    P.dma('sp', lambda e: e.dma_start(out=small[:, 56:58], in_=fox_w_qkv[0:128, 0:2]), w=['touch'])
    P.dma('sp', lambda e: e.dma_start(out=small[:, 58:60], in_=fox_w_o[0:128, 0:2]), w=['touch'])
    P.dma('sp', lambda e: e.dma_start(out=small[:, 60:62], in_=fox_w_f[0:128, 0:2]), w=['touch'])
    P.dma('pool', lambda e: e.dma_start(out=t64b[:, :], in_=t64[:, :]), w=['t64b'])
    P.op('dve', lambda e: e.memset(ones1[:, :], 1.0), w=['ones1'])
    P.op('dve', lambda e: e.tensor_scalar(out=nbc[:, 0:4], in0=mk[:, 12:16], scalar1=-1.0, scalar2=30000.0, op0=ALU.add, op1=ALU.mult), r=['mk'], w=['nbc'])
    P.op('dve', lambda e: e.tensor_scalar(out=nbc[:, 4:8], in0=mk[:, 4:8], scalar1=-1.0, scalar2=30000.0, op0=ALU.add, op1=ALU.mult), r=['mk'], w=['nbc'])
    load(scv[:, :, :], ssd_cvT[:, :, :], wk=['scv'])
    load(srep[:, :], ssd_rep[:, :], wk=['srep'])
    load(sng[:, :], ssd_ngT[:, :], wk=['sng'])
    P.dma('pool', lambda e: e.dma_start(out=ident[:, :], in_=cst[:, 0, :]), w=['ident'])
    P.op('act', lambda e: e.activation(out=srep[:, 32:64], in_=srep[:, 32:64], func=AF.Exp), r=['srep'], w=['srep'])
    P.op('dve', lambda e: e.tensor_scalar(out=srep[:, 32:64], in0=srep[:, 32:64], scalar1=-1.0, scalar2=None, op0=ALU.mult), r=['srep'], w=['srep'])
    P.op('dve', lambda e: e.memset(ones[:, :], 1.0 / D), w=['ones'])
    P.op('dve', lambda e: e.memset(small[:, 0:1], EPS), w=['small'])
    P.op('dve', lambda e: e.memset(small[:, 1:2], 1.0), w=['small'])
    P.op('act', lambda e: e.activation(out=ada[:, 0:NCH, :], in_=ctmp[:, :, :], func=AF.Sigmoid), r=['ctmp'], w=['ada'])
    P.op('dve', lambda e: e.tensor_tensor(out=siluc[:, :, :], in0=ada[:, 0:NCH, :], in1=ctmp[:, :, :], op=ALU.mult),
         r=['ada', 'ctmp'], w=['siluc'])
    for j in range(2):
        zz = cfac[:, j, 0, :]
        t1 = cfac[:, j, 1, :]
        P.op('act', lambda e, j=j, zz=zz: e.activation(out=zz, in_=lruv[:, j, 9, :], func=AF.Exp, scale=-1.0), r=['lruv'], w=['cfac'])
        P.op('dve', lambda e, zz=zz, t1=t1: e.tensor_scalar(out=t1, in0=zz, scalar1=-1.0 / 3.0, scalar2=0.5, op0=ALU.mult, op1=ALU.add),
             r=['cfac'], w=['cfac'])
        P.op('dve', lambda e, zz=zz, t1=t1: e.tensor_tensor(out=t1, in0=t1, in1=zz, op=ALU.mult), r=['cfac'], w=['cfac'])
        P.op('dve', lambda e, zz=zz, t1=t1: e.tensor_scalar(out=t1, in0=t1, scalar1=-1.0, scalar2=1.0, op0=ALU.mult, op1=ALU.add),
             r=['cfac'], w=['cfac'])
        P.op('dve', lambda e, zz=zz, t1=t1: e.tensor_tensor(out=t1, in0=t1, in1=zz, op=ALU.mult), r=['cfac'], w=['cfac'])
        P.op('dve', lambda e, zz=zz, t1=t1: e.tensor_scalar(out=zz, in0=t1, scalar1=-8.0, scalar2=None, op0=ALU.mult), r=['cfac'], w=['cfac'])
        P.op('dve', lambda e, zz=zz, t1=t1: e.tensor_scalar(out=t1, in0=t1, scalar1=-16.0, scalar2=None, op0=ALU.mult), r=['cfac'], w=['cfac'])

    k.wi = 0

    def load_panel(wap, kc, c0, ncols):
        i = k.wi
        k.wi = (i + 1) % 2
        t = wpf[i][:, 0:kc * ncols].rearrange("p (q n) -> p q n", n=ncols)
        src = wap.rearrange("(kc p) n -> p kc n", p=128)
        key = ('wp', i)
        for q in range(kc):
            P.dma('pool', lambda e, q=q: e.dma_start(out=t[:, q, :], in_=src[:, q, c0:c0 + ncols]), w=[key])
        return t, key

    def ada_layer(l):
        for pn in range(12):
            t, key = load_panel(ada_w[l], NCH, pn * 512, 512)
            b = bank()
            for o in range(4):
                for q in range(NCH):
                    P.op('pe', lambda e, b=b, o=o, q=q, t=t: e.matmul(banks[b][:, o * 4:o * 4 + 3], lhsT=t[:, q, o * 128:(o + 1) * 128],
                                                                         rhs=siluc[:, q, :], start=(q == 0), stop=(q == NCH - 1)),
                         r=[key, 'siluc'], w=[('bank', b)])
            for o in range(4):
                oc = pn * 4 + o
                P.op('act', lambda e, b=b, o=o, oc=oc: e.activation(out=ada[:, oc, :], in_=banks[b][:, o * 4:o * 4 + 3], func=AF.Identity,
                                                                     bias=adab[:, l, oc:oc + 1]),
                     r=[('bank', b), 'adab'], w=['ada'])
        for kind, (gi, ai, plus1) in enumerate([(0, 1, True), (1, 2, False), (2, 4, True), (3, 5, False)]):
            for s in range(3):
                if plus1:
                    P.op('dve', lambda e, kind=kind, gi=gi, ai=ai, s=s: e.scalar_tensor_tensor(
                        out=gsc[:, kind, :, s], in0=ada[:, ai * 8:(ai + 1) * 8, s], scalar=1.0, in1=ng[:, l, gi, :],
                        op0=ALU.add, op1=ALU.mult), r=['ada', 'ng'], w=['gsc'])
                else:
                    P.op('dve', lambda e, kind=kind, gi=gi, ai=ai, s=s: e.tensor_tensor(
                        out=gsc[:, kind, :, s], in0=ada[:, ai * 8:(ai + 1) * 8, s], in1=ng[:, l, gi, :], op=ALU.mult),
                        r=['ada', 'ng'], w=['gsc'])

    def rstd_block(src_fn, nch, src_keys_fn, lo, n, dim):
        b = bank()
        for c in range(nch):
            if c % 2 == 0:
                P.op('act', lambda e, c=c: e.activation(out=sq[:, c % NSQ, 0:n], in_=src_fn(c), func=AF.Square),
                     r=src_keys_fn(c), w=[('sq', c % NSQ)])
            else:
                P.op('dve', lambda e, c=c: e.tensor_tensor(out=sq[:, c % NSQ, 0:n], in0=src_fn(c), in1=src_fn(c), op=ALU.mult),
                     r=src_keys_fn(c), w=[('sq', c % NSQ)])
            P.op('pe', lambda e, c=c, b=b: e.matmul(banks[b][:, 0:n], lhsT=ones[:, :], rhs=sq[:, c % NSQ, 0:n],
                                                     start=(c == 0), stop=(c == nch - 1)),
                 r=['ones', ('sq', c % NSQ)], w=[('bank', b)])
        P.op('act', lambda e, b=b: e.activation(out=rstd[:, 0:n], in_=banks[b][:, 0:n], func=AF.Sqrt,
                                                 scale=float(D) / dim, bias=small[:, 0:1]),
             r=[('bank', b), 'small'], w=['rstd'])
        P.op('dve', lambda e: e.reciprocal(out=rstd[:, 0:n], in_=rstd[:, 0:n]),
             r=['rstd'], w=['rstd'])

    def seq_pieces(lo, n):
        out = []
        for s, (o, ln) in enumerate(SEQS):
            a, b_ = max(lo, o), min(lo + n, o + ln)
            if a < b_:
                out.append((s, a, b_))
        return out

    def prenorm(kind, shift_idx):
        for (lo, n) in BLKS:
            P.dma('sp', lambda e, lo=lo, n=n: e.dma_start(out=blk16[:, :, 0:n], in_=xd[:, :, lo:lo + n]),
                  r=[kk for c in range(NCH) for kk in keys('xd', c, lo, lo + n)], w=[('blk16', c) for c in range(NCH)])
            rstd_block(lambda c: blk16[:, c, 0:n], NCH, lambda c: [('blk16', c)], lo, n, D)
            for c in range(NCH):
                for (s, a, b_) in seq_pieces(lo, n):
                    P.op('dve', lambda e, c=c, s=s, a=a, b_=b_, lo=lo: e.scalar_tensor_tensor(
                        out=tmpf0[:, a:b_], in0=blk16[:, c, a - lo:b_ - lo], scalar=gsc[:, kind, c, s:s + 1], in1=rstd[:, a - lo:b_ - lo],
                        op0=ALU.mult, op1=ALU.mult),
                        r=[('blk16', c), 'gsc'] + ['rstd'], w=keys('tmpf0', 0, a, b_))
                    P.op('act', lambda e, c=c, s=s, a=a, b_=b_: e.activation(
                        out=h[:, c, a:b_], in_=tmpf0[:, a:b_], func=AF.Identity, bias=ada[:, shift_idx * 8 + c, s:s + 1]),
                        r=keys('tmpf0', 0, a, b_) + ['ada'], w=keys('h', c, a, b_))

    def project(wap, kc, nout, src, srcname, evac, pw=512):
        for c0 in range(0, nout, pw):
            ncols = min(pw, nout - c0)
            t, key = load_panel(wap, kc, c0, ncols)
            for o in range(ncols // 128):
                oc = c0 // 128 + o
                for (lo, n) in BLKS:
                    b = bank()
                    for q in range(kc):
                        P.op('pe', lambda e, b=b, o=o, q=q, t=t, lo=lo, n=n: e.matmul(
                            banks[b][:, 0:n], lhsT=t[:, q, o * 128:(o + 1) * 128], rhs=src[:, q, lo:lo + n],
                            start=(q == 0), stop=(q == kc - 1)),
                            r=[key] + keys(srcname, q, lo, lo + n), w=[('bank', b)])
                    evac(oc, lo, n, b)

    def out_proj_residual(wap, kc, src, srcname, gkind, bias_fn=None, pw=512):
        def ev(oc, lo, n, b):
            if bias_fn is not None:
                P.op('act', lambda e: e.activation(out=h[:, oc, lo:lo + n], in_=banks[b][:, 0:n], func=AF.Identity, bias=bias_fn(oc)),
                     r=[('bank', b), 'lruv'], w=keys('h', oc, lo, lo + n))
            else:
                P.op('act', lambda e: e.activation(out=h[:, oc, lo:lo + n], in_=banks[b][:, 0:n], func=AF.Identity),
                     r=[('bank', b)], w=keys('h', oc, lo, lo + n))
        project(wap, kc, D, src, srcname, ev, pw=pw)
        for (lo, n) in BLKS:
            rstd_block(lambda c: h[:, c, lo:lo + n], NCH, lambda c: keys('h', c, lo, lo + n), lo, n, D)
            for c in range(NCH):
                for (s, a, b_) in seq_pieces(lo, n):
                    P.op('dve', lambda e, c=c, s=s, a=a, b_=b_, lo=lo: e.scalar_tensor_tensor(
                        out=blk16[:, c, a - lo:b_ - lo], in0=h[:, c, a:b_], scalar=gsc[:, gkind, c, s:s + 1],
                        in1=rstd[:, a - lo:b_ - lo], op0=ALU.mult, op1=ALU.mult),
                        r=keys('h', c, a, b_) + ['gsc'] + ['rstd'], w=[('blk16', c)])
            P.dma('pool', lambda e, lo=lo, n=n: e.dma_start(out=xd[:, :, lo:lo + n], in_=blk16[:, :, 0:n], accum_op=ALU.add),
                  r=[('blk16', c) for c in range(NCH)],
                  w=[kk for c in range(NCH) for kk in keys('xd', c, lo, lo + n)])

    k.xi = 0

    def exchange(src_ap, ncols, src_keys, dst_view=None, W=None, gdst=None, gkeys=None):
        i = k.xi
        k.xi += 1
        W = W or XW
        gdst = xg if gdst is None else gdst
        gkeys = ['xg'] if gkeys is None else gkeys
        xin = nc.dram_tensor("xb_in%d" % i, [128, W], F32)
        xout = nc.dram_tensor("xb_out%d" % i, [4 * 128, W], F32)
        dst = xin.ap()[:, 0:ncols]
        if dst_view is not None:
            dst = dst_view(dst)
        P.dma('pool', lambda e: e.dma_start(out=dst, in_=src_ap), r=src_keys, w=['xb_in%d' % i])
        P.coll(lambda e: e.collective_compute("AllGather", ALU.bypass, replica_groups=GROUPS,
                                              ins=[xin.ap().opt()], outs=[xout.ap().opt()]),
               r=['xb_in%d' % i], w=['xb_out%d' % i])
        P.dma('pool', lambda e: e.dma_start(out=gdst[:, :, 0:W], in_=xout.ap().rearrange("(r p) n -> p r n", p=128)),
              r=['xb_out%d' % i], w=gkeys)

    def masked_sum(dst, dkeys, ncols, mcol0):
        P.op('dve', lambda e: e.tensor_scalar(out=dst, in0=xg[:, 0, 0:ncols], scalar1=mk[:, mcol0:mcol0 + 1], scalar2=None, op0=ALU.mult),
             r=['xg', 'mk'], w=dkeys)
        for r_ in range(1, 4):
            P.op('dve', lambda e, r_=r_: e.scalar_tensor_tensor(out=dst, in0=xg[:, r_, 0:ncols], scalar=mk[:, mcol0 + r_:mcol0 + r_ + 1],
                                                               in1=dst, op0=ALU.mult, op1=ALU.add),
                 r=['xg', 'mk'] + dkeys, w=dkeys)

    def lru_layer(l, j):
        V = lambda idx, c: lruv[:, j, idx, c:c + 1]

        def evac_x(oc, lo, n, b):
            for (s, a, b_) in seq_pieces(lo, n):
                off = HOFF[s] - SEQS[s][0]
                P.op('act', lambda e, a=a, b_=b_, off=off: e.activation(out=bufB[:, oc, a + off:b_ + off], in_=banks[b][:, a - lo:b_ - lo],
                                                                         func=AF.Identity, bias=V(0, oc)),
                     r=[('bank', b), 'lruv'], w=[('bufB', oc)])
                if b_ == SEQS[s][0] + SEQS[s][1]:
                    P.op('act', lambda e, b_=b_, s=s: e.activation(out=tails[:, oc, s, :], in_=banks[b][:, b_ - 3 - lo:b_ - lo],
                                                                  func=AF.Identity, bias=V(0, oc)),
                         r=[('bank', b), 'lruv'], w=['tails'])
        project(lru_w_x[j], NCH, D, h, 'h', evac_x)

        def evac_y(oc, lo, n, b):
            P.op('act', lambda e: e.activation(out=bufA[:, oc, lo:lo + n], in_=banks[b][:, 0:n], func=AF.Gelu, bias=V(1, oc)),
                 r=[('bank', b), 'lruv'], w=keys('bufA', oc, lo, lo + n))
        project(lru_w_y[j], NCH, D, h, 'h', evac_y)
        P.dma('sp', lambda e: e.dma_start(out=o_lconv[:, j, :, :, :], in_=tails[:, :, :, :]), r=['tails'], w=['o_lconv%d' % j])
        P.dma('sp', lambda e: e.dma_start(out=tmpf0[:, 0:NCH * 6], in_=st_lconv[:, j, :, :, :].rearrange("p c s k -> p (c s k)")),
              w=tk(0))
        for c in range(NCH):
            for s in (1, 2):
                P.op('dve', lambda e, c=c, s=s: e.tensor_copy(out=bufB[:, c, HOFF[s] - 3:HOFF[s]],
                                                               in_=tmpf0[:, c * 6 + (s - 1) * 3:c * 6 + (s - 1) * 3 + 3]),
                     r=tk(0), w=[('bufB', c)])
        exchange(tails[:, :, 0, :], 24, ['tails'], dst_view=lambda d: d.rearrange('p (c k) -> p c k', k=3))
        masked_sum(small[:, 32:56], ['pt'], 24, 0)
        for c in range(NCH):
            P.op('dve', lambda e, c=c: e.tensor_copy(out=bufB[:, c, 0:3], in_=small[:, 32 + c * 3:32 + c * 3 + 3]),
                 r=['pt'], w=[('bufB', c)])
        P.dma('sp', lambda e: e.dma_start(out=hin[:, :, 1:3], in_=st_lh[:, j, :, :]), w=['hin'])
        P.op('dve', lambda e: e.memset(hin[:, :, 0:1], 0.0), w=['hin'])
        u, rr, ig, aa, sc_ = tmpf0[:, 0:T], tmpf(1), tmpf(2), tmpf(3), tmpf(4)
        zeros = sq[:, :, :].rearrange('p c n -> p (c n)')
        ZK = [('sq', i_) for i_ in range(NSQ)]
        P.op('dve', lambda e: e.memset(zeros[:, :], 0.0), w=ZK)
        summ = small
        for c in range(NCH):
            wt = wsm[c % 2]
            wkey = ('wsm', c % 2)
            P.dma('pool', lambda e, c=c, wt=wt: e.dma_start(out=wt[:, 0, :], in_=lru_w_a[j, c]), w=[wkey])
            P.dma('pool', lambda e, c=c, wt=wt: e.dma_start(out=wt[:, 1, :], in_=lru_w_i[j, c]), w=[wkey])
            for (s, (o, ln)) in enumerate(SEQS):
                ho = HOFF[s]
                P.op('dve', lambda e, c=c, o=o, ln=ln, ho=ho: e.tensor_scalar(
                    out=u[:, o:o + ln], in0=bufB[:, c, ho - 3:ho - 3 + ln], scalar1=V(2, c), scalar2=V(6, c), op0=ALU.mult, op1=ALU.add),
                    r=[('bufB', c), 'lruv'], w=tk(0))
                for kk in (1, 2, 3):
                    P.op('dve', lambda e, c=c, o=o, ln=ln, ho=ho, kk=kk: e.scalar_tensor_tensor(
                        out=u[:, o:o + ln], in0=bufB[:, c, ho - 3 + kk:ho - 3 + kk + ln], scalar=V(2 + kk, c), in1=u[:, o:o + ln],
                        op0=ALU.mult, op1=ALU.add),
                        r=[('bufB', c), 'lruv'] + tk(0), w=tk(0))
            P.op('act', lambda e: e.activation(out=tmpb[:, :], in_=u[:, :], func=AF.Identity), r=tk(0), w=['tmpb'])
            for (lo, n) in BLKS:
                for gi, (dst, di, bidx) in enumerate([(rr, 1, 7), (ig, 2, 8)]):
                    b = bank()
                    P.op('pe', lambda e, b=b, gi=gi, wt=wt, lo=lo, n=n: e.matmul(banks[b][:, 0:n], lhsT=wt[:, gi, :], rhs=tmpb[:, lo:lo + n],
                                                                                 start=True, stop=True),
                         r=[wkey, 'tmpb'], w=[('bank', b)])
                    P.op('act', lambda e, b=b, dst=dst, bidx=bidx, c=c, lo=lo, n=n: e.activation(
                        out=dst[:, lo:lo + n], in_=banks[b][:, 0:n], func=AF.Sigmoid, bias=V(bidx, c)),
                        r=[('bank', b), 'lruv'], w=tk(di))
            P.op('act', lambda e, c=c: e.activation(out=aa[:, :], in_=rr[:, :], func=AF.Exp, scale=cfac[:, j, 1, c:c + 1]),
                 r=tk(1) + ['cfac'], w=tk(3))
            P.op('act', lambda e: e.activation(out=aa[:, :], in_=aa[:, :], func=AF.Sqrt, scale=-1.0, bias=small[:, 1:2]),
                 r=tk(3) + ['small'], w=tk(3))
            P.op('dve', lambda e: e.tensor_tensor(out=ig[:, :], in0=ig[:, :], in1=u[:, :], op=ALU.mult), r=tk(2) + tk(0), w=tk(2))
            P.op('dve', lambda e: e.tensor_tensor(out=ig[:, :], in0=ig[:, :], in1=aa[:, :], op=ALU.mult), r=tk(2) + tk(3), w=tk(2))
            P.op('act', lambda e, c=c: e.activation(out=aa[:, :], in_=rr[:, :], func=AF.Exp, scale=cfac[:, j, 0, c:c + 1]),
                 r=tk(1) + ['cfac'], w=tk(3))
            P.op('dve', lambda e, c=c: e.tensor_tensor_scan(out=sc_[:, 0:TP], data0=aa[:, 0:TP], data1=zeros[:, :], initial=1.0,
                                                            op0=ALU.mult, op1=ALU.add), r=tk(3) + ZK, w=tk(4))
            P.op('dve', lambda e, c=c: e.tensor_copy(out=summ[:, 8 + c:9 + c], in_=sc_[:, TP - 1:TP]), r=tk(4), w=['summ'])
            P.op('dve', lambda e, c=c: e.tensor_tensor(out=bufB[:, c, 0:TP], in0=bufA[:, c, 0:TP], in1=sc_[:, 0:TP], op=ALU.mult),
                 r=keys('bufA', c, 0, TP) + tk(4), w=[('bufB', c)])
            for (s, (o, ln)) in enumerate(SEQS):
                P.op('dve', lambda e, c=c, s=s, o=o, ln=ln: e.tensor_tensor_scan(
                    out=sc_[:, o:o + ln], data0=aa[:, o:o + ln], data1=ig[:, o:o + ln], initial=hin[:, c, s:s + 1],
                    op0=ALU.mult, op1=ALU.add), r=tk(3) + tk(2) + ['hin'], w=tk(4))
            for (s, (o, ln)) in enumerate(SEQS):
                P.op('dve', lambda e, c=c, s=s, o=o, ln=ln: e.tensor_copy(out=hl[:, c, s:s + 1], in_=sc_[:, o + ln - 1:o + ln]),
                     r=tk(4), w=['hl'])
            P.op('dve', lambda e, c=c: e.tensor_tensor(out=bufA[:, c, :], in0=bufA[:, c, :], in1=sc_[:, :], op=ALU.mult),
                 r=keys('bufA', c, 0, T) + tk(4), w=keys('bufA', c, 0, T))
        P.op('dve', lambda e: e.tensor_copy(out=summ[:, 16:24], in_=hl[:, :, 0]), r=['hl'], w=['summ'])
        exchange(summ[:, 8:24], 16, ['summ'])
        hp = small[:, 32:48]
        P.op('dve', lambda e: e.memset(hp[:, 0:8], 0.0), w=['pt'])
        for r_ in range(4):
            P.op('dve', lambda e, r_=r_: e.tensor_tensor(out=hp[:, 8:16], in0=hp[:, 0:8], in1=xg[:, r_, 0:8], op=ALU.mult),
                 r=['xg', 'pt'], w=['pt'])
            P.op('dve', lambda e, r_=r_: e.tensor_tensor(out=hp[:, 8:16], in0=hp[:, 8:16], in1=xg[:, r_, 8:16], op=ALU.add),
                 r=['xg', 'pt'], w=['pt'])
            P.op('dve', lambda e, r_=r_: e.tensor_tensor(out=hp[:, 8:16], in0=hp[:, 8:16], in1=hp[:, 0:8], op=ALU.subtract),
                 r=['pt'], w=['pt'])
            P.op('dve', lambda e, r_=r_: e.scalar_tensor_tensor(out=hp[:, 0:8], in0=hp[:, 8:16], scalar=mk[:, 4 + r_:5 + r_], in1=hp[:, 0:8],
                                                               op0=ALU.mult, op1=ALU.add), r=['pt', 'mk'], w=['pt'])
        P.op('dve', lambda e: e.tensor_tensor(out=hp[:, 8:16], in0=hp[:, 0:8], in1=summ[:, 8:16], op=ALU.mult), r=['pt', 'summ'], w=['pt'])
        P.op('dve', lambda e: e.tensor_tensor(out=hl[:, :, 0], in0=hl[:, :, 0], in1=hp[:, 8:16], op=ALU.add), r=['pt', 'hl'], w=['hl'])
        P.dma('sp', lambda e: e.dma_start(out=o_lh[:, j, :, :], in_=hl[:, :, :]), r=['hl'], w=['o_lh%d' % j])
        for c in range(NCH):
            P.op('dve', lambda e, c=c: e.scalar_tensor_tensor(out=bufA[:, c, 0:TP], in0=bufB[:, c, 0:TP], scalar=hp[:, c:c + 1],
                                                             in1=bufA[:, c, 0:TP], op0=ALU.mult, op1=ALU.add),
                 r=[('bufB', c), 'pt'] + keys('bufA', c, 0, TP), w=keys('bufA', c, 0, TP))
        out_proj_residual(lru_w_o[j], NCH, bufA, 'bufA', 1, bias_fn=lambda oc: V(10, oc))


    def exchange_big(src_ap, ncols, src_keys, name):
        xin = nc.dram_tensor("xB_in_" + name, [128, ncols], F32)
        xout = nc.dram_tensor("xB_out_" + name, [4 * 128, ncols], F32)
        P.dma('pool', lambda e: e.dma_start(out=xin.ap()[:, :], in_=src_ap), r=src_keys, w=['xB_in_' + name])
        import os
        if not os.environ.get('SSD_NOCOLL'):
            P.coll(lambda e: e.collective_compute("AllGather", ALU.bypass, replica_groups=GROUPS,
                                                  ins=[xin.ap().opt()], outs=[xout.ap().opt()]),
                   r=['xB_in_' + name], w=['xB_out_' + name])
        return xout.ap(), 'xB_out_' + name

    def ssd_layer(l):
        NCK = T // 64
        hflat = h[:, :, :].rearrange("p c t -> p (c t)")
        bflat = blk16[:, :, :].rearrange("p c n -> p (c n)")
        BKALL = [('blk16', c) for c in range(NCH)]
        zst = bflat[:, 0:256].bitcast(BF16)
        zsg = bflat[:, 256:768]
        for cg in range(4):
            t, key = load_panel(ssd_w_in, NCH, cg * 512, 512)
            for tt in range(T // 128):
                b = bank()
                for q in range(NCH):
                    P.op('pe', lambda e: e.matmul(banks[b][:, 0:512], lhsT=h[:, q, tt * 128:(tt + 1) * 128], rhs=t[:, q, :],
                                                  start=(q == 0), stop=(q == NCH - 1)),
                         r=[key] + keys('h', q, tt * 128, tt * 128 + 128), w=[('bank', b)])
                P.op('act', lambda e: e.activation(out=zsg, in_=banks[b][:, 0:512], func=AF.Sigmoid), r=[('bank', b)], w=['zsg'])
                P.op('dve', lambda e: e.tensor_tensor(out=zst, in0=zsg, in1=banks[b][:, 0:512], op=ALU.mult), r=[('bank', b), 'zsg'], w=['zst'])
                P.dma('sp', lambda e: e.dma_start(out=zd[tt * 128:(tt + 1) * 128, cg * 512:(cg + 1) * 512], in_=zst), r=['zst'], w=['zd'])
        P.dma('pool', lambda e: e.dma_start(out=wdt[:, :, :], in_=ssd_w_in.rearrange("(kc p) n -> p kc n", p=128)[:, :, 5120:5152]), w=['wdt'])
        dttm = tmpb[:, 0:T].bitcast(F32)
        dtv = dttm.rearrange("p (c e) -> p c e", e=32)
        for c0 in range(0, NCK, 16):
            ncq = min(16, NCK - c0)
            b = bank()
            for ci in range(c0, c0 + ncq):
                for q in range(NCH):
                    P.op('pe', lambda e: e.matmul(banks[b][0:64, (ci - c0) * 32:(ci - c0 + 1) * 32], lhsT=h[:, q, ci * 64:(ci + 1) * 64],
                                                  rhs=wdt[:, q, :], start=(q == 0), stop=(q == NCH - 1)),
                         r=['wdt'] + keys('h', q, ci * 64, ci * 64 + 64), w=[('bank', b)])
            P.op('dve', lambda e: e.tensor_tensor(out=dtv[0:64, c0:c0 + ncq, :], in0=banks[b][0:64, 0:ncq * 32].rearrange("p (c e) -> p c e", e=32),
                                                  in1=srep[0:64, 0:32].unsqueeze(1).to_broadcast([64, ncq, 32]), op=ALU.add),
                 r=[('bank', b), 'srep'], w=['dttm'])
        sp1 = bflat[0:64, 0:1088]
        sp2 = bflat[0:64, 1088:2176]
        P.op('act', lambda e: e.activation(out=sp1, in_=dttm[0:64, :], func=AF.Abs), r=['dttm', 'zsg', 'zst'], w=['sp1'])
        P.op('act', lambda e: e.activation(out=sp1, in_=sp1, func=AF.Exp, scale=-1.0), r=['sp1'], w=['sp1'])
        P.op('act', lambda e: e.activation(out=sp1, in_=sp1, func=AF.Ln, bias=small[0:64, 1:2]), r=['sp1', 'small'], w=['sp1'])
        P.op('dve', lambda e: e.tensor_scalar_max(out=sp2, in0=dttm[0:64, :], scalar1=0.0), r=['dttm'], w=['sp2'])
        P.op('dve', lambda e: e.tensor_tensor(out=dttm[0:64, :], in0=sp1, in1=sp2, op=ALU.add), r=['sp1', 'sp2'], w=['dttm'])
        P.fence()
        pre = tmpf0[:, 0:HW]
        acc = bflat[:, 0:T]
        sig = bflat[:, T:2 * T] if 2 * T <= 4096 else None
        sig = rstd
        SV = lambda idx, oc: scv[:, idx, oc:oc + 1]
        sst = sfx
        shal = bflat[:, 2 * 1088:2 * 1088 + 144]
        P.dma('sp', lambda e: e.dma_start(out=shal, in_=st_sconv[:, :, :, :].rearrange("p c s k -> p (c s k)")), w=['shal'])

        def evac_xbc(oc, lo, n, b):
            for (s_, a, b_) in seq_pieces(lo, n):
                off = HOFF[s_] - SEQS[s_][0]
                P.op('act', lambda e: e.activation(out=pre[:, a + off:b_ + off], in_=banks[b][:, a - lo:b_ - lo], func=AF.Identity),
                     r=[('bank', b)], w=['pre'])
            if lo + n == T:
                for (s_, (o, ln)) in enumerate(SEQS):
                    ho = HOFF[s_]
                    P.op('dve', lambda e: e.tensor_copy(out=stail[:, oc, s_, :], in_=pre[:, ho + ln - 3:ho + ln]), r=['pre'], w=['stail'])
                    if s_ == 0:
                        P.op('dve', lambda e: e.tensor_copy(out=sfirst[:, oc, :], in_=pre[:, ho:ho + 3]), r=['pre'], w=['sfirst'])
                        P.op('dve', lambda e: e.memset(pre[:, 0:3], 0.0), r=['pre'], w=['pre'])
                    else:
                        P.op('dve', lambda e: e.tensor_copy(out=pre[:, ho - 3:ho], in_=shal[:, oc * 6 + (s_ - 1) * 3:oc * 6 + (s_ - 1) * 3 + 3]),
                             r=['pre', 'shal'], w=['pre'])
                for (s_, (o, ln)) in enumerate(SEQS):
                    ho = HOFF[s_]
                    P.op('dve', lambda e: e.tensor_scalar(out=acc[:, o:o + ln], in0=pre[:, ho - 3:ho - 3 + ln], scalar1=SV(0, oc), scalar2=SV(4, oc),
                                                          op0=ALU.mult, op1=ALU.add), r=['pre', 'scv'], w=['acc'])
                    for kk in (1, 2, 3):
                        P.op('dve', lambda e: e.scalar_tensor_tensor(out=acc[:, o:o + ln], in0=pre[:, ho - 3 + kk:ho - 3 + kk + ln], scalar=SV(kk, oc),
                                                                     in1=acc[:, o:o + ln], op0=ALU.mult, op1=ALU.add),
                             r=['pre', 'scv', 'acc'], w=['acc'])
                sgt = hflat[:, 0:2 * T].bitcast(F32)
                P.op('act', lambda e: e.activation(out=pre[:, 0:T], in_=acc, func=AF.Sigmoid), r=['acc', 'pre'], w=['pre'])
                P.op('dve', lambda e: e.tensor_tensor(out=xbc[:, oc, :], in0=pre[:, 0:T], in1=acc, op=ALU.mult), r=['pre', 'acc'],
                     w=keys('xbc', oc, 0, T))
        project(ssd_w_in[:, 2048:5120], NCH, 3072, h, 'h', evac_xbc)
        P.dma('sp', lambda e: e.dma_start(out=o_sconv[:, :, :, :], in_=stail[:, :, :, :]), r=['stail'], w=['o_sconv'])
        exchange(stail[:, :, 0, :], 72, ['stail'], dst_view=lambda d: d.rearrange('p (c k) -> p c k', k=3))
        masked_sum(sprev[:, :, :].rearrange("p c k -> p (c k)"), ['sprev'], 72, 0)
        Wk = lambda kk: scv[:, kk, :]
        U = lambda i: sfx[:, i, :]
        tt_ = sfx[:, 3, :]
        def TT2(out, a, b_, op):
            P.op('dve', lambda e: e.tensor_tensor(out=out, in0=a, in1=b_, op=op), r=['sfx', 'sprev', 'sfirst', 'scv'], w=['sfx'])
        terms = [[(0, ('p', 0)), (1, ('p', 1)), (2, ('p', 2)), (3, ('x', 0))],
                 [(0, ('p', 1)), (1, ('p', 2)), (2, ('x', 0)), (3, ('x', 1))],
                 [(0, ('p', 2)), (1, ('x', 0)), (2, ('x', 1)), (3, ('x', 2))]]
        for ti in range(3):
            for n_, (kk, (src, idx)) in enumerate(terms[ti]):
                v = sprev[:, :, idx] if src == 'p' else sfirst[:, :, idx]
                if n_ == 0:
                    TT2(U(ti), Wk(kk), v, ALU.mult)
                    TT2(U(ti), U(ti), scv[:, 4, :], ALU.add)
                else:
                    TT2(tt_, Wk(kk), v, ALU.mult)
                    TT2(U(ti), U(ti), tt_, ALU.add)
            P.op('act', lambda e: e.activation(out=tt_, in_=U(ti), func=AF.Sigmoid), r=['sfx'], w=['sfx'])
            P.op('dve', lambda e: e.tensor_tensor(out=xbc[:, :, ti], in0=U(ti), in1=tt_, op=ALU.mult), r=['sfx'],
                 w=[kk_ for c in range(24) for kk_ in keys('xbc', c, 0, 3)])
        P.fence()
        ST = hflat[:, 0:4096].bitcast(F32)
        ytm = hflat[:, 4096:8192].bitcast(F32)
        xtm = hflat[:, 8192:10240]
        xdt = hflat[:, 10240:12288]
        xdd = hflat[:, 12288:14336]
        Sb = hflat[:, 14336:16384]
        zch = bflat[:, 0:1024].bitcast(BF16)
        Lt = bflat[:, 1024:1536]
        Mt = bflat[:, 1536:1792].bitcast(BF16)
        Btm = bflat[:, 1792:2048].bitcast(BF16)
        mG = bflat[:, 2048:2112]
        dgx = bflat[:, 2112:2624]
        sm = bflat[:, 2624:2944]
        dA, cum, dend, ecum = sm[:, 0:32], sm[:, 32:64], sm[:, 64:96], sm[:, 96:128]
        tot, cd, totall, rs = sm[:, 128:160], sm[:, 160:192], sm[:, 192:224], sm[:, 224:226]
        wn = xdt
        tpb = [banks[i][:, :].bitcast(BF16) for i in range(8)]
        Sx = tmpf0[:, 0:2080]
        idf, trif, onef = cstf[:, 0, :], cstf[:, 1, :], cstf[:, 2, :]
        br3 = lambda ap_, n0, n1, n2: ap_.unsqueeze(2).to_broadcast([n0, n1, n2])

        def chunk(ci, full):
            t0 = ci * 64
            g8 = lambda ap_, g: ap_[0:64, g * 8:(g + 1) * 8]
            P.op('dve', lambda e: e.tensor_tensor(out=dA[0:64, :], in0=dtv[0:64, ci, :], in1=srep[0:64, 32:64], op=ALU.mult),
                 r=['dttm', 'srep'], w=['dA'])
            b = bank()
            P.op('pe', lambda e: e.matmul(banks[b][0:64, 0:32], lhsT=trif[0:64, 0:64], rhs=dA[0:64, :], start=True, stop=True),
                 r=['cstf', 'dA'], w=[('bank', b)])
            P.op('pe', lambda e: e.matmul(banks[b][:, 32:64], lhsT=onef[0:64, 0:128], rhs=dA[0:64, :], start=True, stop=True),
                 r=['cstf', 'dA'], w=[('bank', b)])
            P.op('act', lambda e: e.activation(out=cum[0:64, :], in_=banks[b][0:64, 0:32], func=AF.Identity), r=[('bank', b)], w=['cum'])
            P.op('act', lambda e: e.activation(out=tot[:, :], in_=banks[b][:, 32:64], func=AF.Identity), r=[('bank', b)], w=['tot'])
            P.op('dve', lambda e: e.tensor_tensor(out=dend[0:64, :], in0=tot[0:64, :], in1=cum[0:64, :], op=ALU.subtract), r=['tot', 'cum'], w=['dend'])
            P.op('act', lambda e: e.activation(out=dend[0:64, :], in_=dend[0:64, :], func=AF.Exp), r=['dend'], w=['dend'])
            P.op('act', lambda e: e.activation(out=cd[:, :], in_=tot[:, :], func=AF.Exp), r=['tot'], w=['cd'])
            if full:
                P.op('act', lambda e: e.activation(out=ecum[0:64, :], in_=cum[0:64, :], func=AF.Exp), r=['cum'], w=['ecum'])
            else:
                P.op('dve', lambda e: e.tensor_tensor(out=totall[:, :], in0=totall[:, :], in1=tot[:, :], op=ALU.add), r=['tot', 'totall'], w=['totall'])
            for half in range(2):
                b = bank()
                for i in range(8):
                    xc = half * 8 + i
                    P.op('pe', lambda e: e.transpose(tpb[b][0:64, i * 128:(i + 1) * 128], xbc[:, xc, t0:t0 + 64], ident[:, :]),
                         r=keys('xbc', xc, t0, t0 + 64) + ['ident'], w=[('bank', b)])
                P.op('act', lambda e: e.activation(out=xtm[0:64, half * 1024:(half + 1) * 1024], in_=tpb[b][0:64, 0:1024], func=AF.Identity),
                     r=[('bank', b)], w=['xtm'])
            b = bank()
            for g in range(4):
                P.op('pe', lambda e: e.transpose(tpb[b][0:64, g * 128:(g + 1) * 128], xbc[:, 16 + g, t0:t0 + 64], ident[:, :]),
                     r=keys('xbc', 16 + g, t0, t0 + 64) + ['ident'], w=[('bank', b)])
            P.op('act', lambda e: e.activation(out=Btm[0:64, :], in_=tpb[b][0:64, 0:512], func=AF.Identity), r=[('bank', b)], w=['Btm'])
            x3 = lambda ap_: ap_[0:64, :].rearrange("p (e q) -> p e q", q=64)
            P.op('dve', lambda e: e.tensor_tensor(out=x3(xdt), in0=x3(xtm), in1=br3(dtv[0:64, ci, :], 64, 32, 64), op=ALU.mult),
                 r=['xtm', 'dttm'], w=['xdt'])
            P.op('dve', lambda e: e.tensor_tensor(out=x3(xdd), in0=x3(xdt), in1=br3(dend[0:64, :], 64, 32, 64), op=ALU.mult),
                 r=['xdt', 'dend'], w=['xdd'])
            for g in range(4):
                gs = slice(g * 512, (g + 1) * 512)
                if full:
                    bg = bank()
                    P.op('pe', lambda e: e.matmul(banks[bg][0:64, 0:64], lhsT=xbc[:, 16 + g, t0:t0 + 64], rhs=xbc[:, 20 + g, t0:t0 + 64],
                                                  start=True, stop=True),
                         r=keys('xbc', 16 + g, t0, t0 + 64) + keys('xbc', 20 + g, t0, t0 + 64), w=[('bank', bg)])
                    P.op('dve', lambda e: e.tensor_tensor(out=mG[0:64, :], in0=banks[bg][0:64, 0:64], in1=trif[0:64, 0:64], op=ALU.mult),
                         r=[('bank', bg), 'cstf'], w=['mG'])
                    d3 = dgx[0:64, :].rearrange("p (e q) -> p e q", q=64)
                    P.op('dve', lambda e: e.tensor_tensor(out=d3, in0=idf[0:64, 0:64].unsqueeze(1).to_broadcast([64, 8, 64]),
                                                          in1=br3(g8(cum, g), 64, 8, 64), op=ALU.mult), r=['cstf', 'cum'], w=['dgx'])
                    bc = bank()
                    P.op('pe', lambda e: e.matmul(banks[bc][0:64, 0:512], lhsT=onef[0:64, 0:64], rhs=dgx[0:64, :], start=True, stop=True),
                         r=['cstf', 'dgx'], w=[('bank', bc)])
                    L3 = Lt[0:64, :].rearrange("p (e q) -> p e q", q=64)
                    P.op('dve', lambda e: e.tensor_tensor(out=L3, in0=banks[bc][0:64, 0:512].rearrange("p (e q) -> p e q", q=64),
                                                          in1=br3(g8(cum, g), 64, 8, 64), op=ALU.subtract), r=[('bank', bc), 'cum'], w=['Lt'])
                    P.op('dve', lambda e: e.tensor_scalar_min(out=Lt[0:64, :], in0=Lt[0:64, :], scalar1=0.0), r=['Lt'], w=['Lt'])
                    P.op('act', lambda e: e.activation(out=Lt[0:64, :], in_=Lt[0:64, :], func=AF.Exp), r=['Lt'], w=['Lt'])
                    P.op('dve', lambda e: e.tensor_tensor(out=Mt[0:64, :].rearrange("p (e q) -> p e q", q=64), in0=L3,
                                                          in1=mG[0:64, :].unsqueeze(1).to_broadcast([64, 8, 64]), op=ALU.mult),
                         r=['Lt', 'mG'], w=['Mt'])
                    byd = bank()
                    for e_ in range(8):
                        P.op('pe', lambda e: e.matmul(banks[byd][0:64, e_ * 64:(e_ + 1) * 64], lhsT=Mt[0:64, e_ * 64:(e_ + 1) * 64],
                                                      rhs=xdt[0:64, g * 512 + e_ * 64:g * 512 + (e_ + 1) * 64], start=True, stop=True),
                             r=['Mt', 'xdt'], w=[('bank', byd)])
                    byo = bank()
                    P.op('pe', lambda e: e.matmul(banks[byo][0:64, 0:512], lhsT=xbc[:, 20 + g, t0:t0 + 64], rhs=Sb[:, gs], start=True, stop=True),
                         r=keys('xbc', 20 + g, t0, t0 + 64) + ['Sb'], w=[('bank', byo)])
                    y3 = ytm[0:64, gs].rearrange("p (e q) -> p e q", q=64)
                    P.op('dve', lambda e: e.tensor_tensor(out=y3, in0=banks[byo][0:64, 0:512].rearrange("p (e q) -> p e q", q=64),
                                                          in1=br3(g8(ecum, g), 64, 8, 64), op=ALU.mult), r=[('bank', byo), 'ecum'], w=['ytm'])
                    P.op('dve', lambda e: e.tensor_tensor(out=ytm[0:64, gs], in0=ytm[0:64, gs], in1=banks[byd][0:64, 0:512], op=ALU.add),
                         r=[('bank', byd), 'ytm'], w=['ytm'])
                bs = bank()
                P.op('pe', lambda e: e.matmul(banks[bs][:, 0:512], lhsT=Btm[0:64, g * 128:(g + 1) * 128], rhs=xdd[0:64, gs], start=True, stop=True),
                     r=['Btm', 'xdd'], w=[('bank', bs)])
                S3 = ST[:, gs].rearrange("p (e q) -> p e q", q=64)
                P.op('dve', lambda e: e.tensor_tensor(out=S3, in0=S3, in1=br3(cd[:, g * 8:(g + 1) * 8], 128, 8, 64), op=ALU.mult),
                     r=['ST', 'cd', 'Sb'], w=['ST'])
                P.op('dve', lambda e: e.tensor_tensor(out=ST[:, gs], in0=ST[:, gs], in1=banks[bs][:, 0:512], op=ALU.add),
                     r=['ST', ('bank', bs)], w=['ST'])
            P.op('act', lambda e: e.activation(out=Sb[:, :], in_=ST[:, :], func=AF.Identity), r=['ST', 'Sb'], w=['Sb'])
            if full:
                P.dma('sp', lambda e: e.dma_start(out=zch[0:64, :], in_=zd[t0:t0 + 64, :]), r=['zd', 'zch'], w=['zch'])
                P.op('dve', lambda e: e.tensor_tensor(out=x3(xdd), in0=x3(xtm), in1=br3(srep[0:64, 64:96], 64, 32, 64), op=ALU.mult),
                     r=['xtm', 'srep', 'xdd'], w=['xdd'])
                P.op('dve', lambda e: e.tensor_tensor(out=ytm[0:64, :], in0=ytm[0:64, :], in1=xdd[0:64, :], op=ALU.add), r=['ytm', 'xdd'], w=['ytm'])
                P.op('dve', lambda e: e.tensor_tensor(out=ytm[0:64, :], in0=ytm[0:64, :], in1=zch[0:64, :], op=ALU.mult), r=['ytm', 'zch'], w=['ytm'])
                P.op('act', lambda e: e.activation(out=xdd[0:64, :], in_=ytm[0:64, :], func=AF.Square, accum_out=rs[0:64, 0:1]),
                     r=['ytm', 'xdd'], w=['xdd', 'rs'])
                P.op('act', lambda e: e.activation(out=rs[0:64, 0:1], in_=rs[0:64, 0:1], func=AF.Sqrt, scale=1.0 / 2048.0, bias=small[0:64, 0:1]),
                     r=['rs', 'small'], w=['rs'])
                P.op('dve', lambda e: e.reciprocal(out=rs[0:64, 0:1], in_=rs[0:64, 0:1]), r=['rs'], w=['rs'])
                P.op('dve', lambda e: e.tensor_scalar(out=wn[0:64, :], in0=ytm[0:64, :], scalar1=rs[0:64, 0:1], scalar2=None, op0=ALU.mult),
                     r=['ytm', 'rs', 'xdt'], w=['xdt'])
                for half in range(2):
                    b = bank()
                    for i in range(8):
                        xc = half * 8 + i
                        P.op('pe', lambda e: e.transpose(tpb[b][:, i * 64:(i + 1) * 64], wn[0:64, xc * 128:(xc + 1) * 128], ident[0:64, 0:64]),
                             r=['xdt', 'ident'], w=[('bank', b)])
                    for i in range(8):
                        xc = half * 8 + i
                        P.op('act', lambda e: e.activation(out=xbc[:, xc, t0:t0 + 64], in_=tpb[b][:, i * 64:(i + 1) * 64], func=AF.Identity,
                                                           scale=sng[:, xc:xc + 1]),
                             r=[('bank', b), 'sng'], w=keys('xbc', xc, t0, t0 + 64))

        P.op('dve', lambda e: e.memset(ST[:, :], 0.0), w=['ST'])
        P.op('dve', lambda e: e.memset(totall[:, :], 0.0), w=['totall'])
        P.op('act', lambda e: e.activation(out=Sb[:, :], in_=ST[:, :], func=AF.Identity), r=['ST'], w=['Sb'])
        for ci in range(TP // 64):
            chunk(ci, False)
        P.op('dve', lambda e: e.tensor_copy(out=Sx[:, 0:2048], in_=ST[:, :]), r=['ST'], w=['Sx'])
        gath, gkey = exchange_big(Sx[:, 0:2048], 2048, ['Sx'], 'ssd')
        exchange(totall[:, :], 32, ['totall'])
        Sacc = ytm
        P.op('dve', lambda e: e.memset(Sacc[:, :], 0.0), w=['ytm'])
        for r_ in range(4):
            P.dma('sp', lambda e: e.dma_start(out=Sx[:, 0:2048], in_=gath[r_ * 128:(r_ + 1) * 128, :]), r=[gkey, 'Sx'], w=['Sx'])
            P.op('act', lambda e: e.activation(out=cd[:, :], in_=xg[:, r_, 0:32], func=AF.Exp), r=['xg'], w=['cd'])
            S3 = ST[:, :].rearrange("p (e q) -> p e q", q=64)
            A3 = Sacc[:, :].rearrange("p (e q) -> p e q", q=64)
            P.op('dve', lambda e: e.tensor_tensor(out=S3, in0=A3, in1=br3(cd[:, :], 128, 32, 64), op=ALU.mult), r=['ytm', 'cd', 'ST'], w=['ST'])
            P.op('dve', lambda e: e.tensor_tensor(out=ST[:, :], in0=ST[:, :], in1=Sx[:, 0:2048], op=ALU.add), r=['ST', 'Sx'], w=['ST'])
            P.op('dve', lambda e: e.tensor_tensor(out=ST[:, :], in0=ST[:, :], in1=Sacc[:, :], op=ALU.subtract), r=['ST', 'ytm'], w=['ST'])
            P.op('dve', lambda e: e.scalar_tensor_tensor(out=Sacc[:, :], in0=ST[:, :], scalar=mk[:, 4 + r_:5 + r_], in1=Sacc[:, :],
                                                         op0=ALU.mult, op1=ALU.add), r=['ST', 'mk', 'ytm'], w=['ytm'])
        P.op('dve', lambda e: e.tensor_copy(out=ST[:, :], in_=Sacc[:, :]), r=['ytm'], w=['ST'])
        P.op('act', lambda e: e.activation(out=Sb[:, :], in_=ST[:, :], func=AF.Identity), r=['ST'], w=['Sb'])
        for ci in range(TP // 64):
            chunk(ci, True)
        P.dma('sp', lambda e: e.dma_start(out=o_ssm[:, 0, :], in_=ST[:, :]), r=['ST'], w=['o_ssm0'])
        for s_ in (1, 2):
            P.dma('sp', lambda e: e.dma_start(out=ST[:, :], in_=st_ssm[:, s_ - 1, :]), r=['ST'], w=['ST'])
            P.op('act', lambda e: e.activation(out=Sb[:, :], in_=ST[:, :], func=AF.Identity), r=['ST'], w=['Sb'])
            chunk(TP // 64 + s_ - 1, True)
            P.dma('sp', lambda e: e.dma_start(out=o_ssm[:, s_, :], in_=ST[:, :]), r=['ST'], w=['o_ssm%d' % s_])
        P.fence()
        out_proj_residual(ssd_w_out, 16, xbc, 'xbc', 1, pw=256)


    k.bg = {'s': 0, 'o': 0, 'd': 0}

    def gbank(grp):
        base, n = {'s': (0, 4), 'o': (4, 2), 'd': (6, 2)}[grp]
        i = k.bg[grp]
        k.bg[grp] = (i + 1) % n
        return base + i

    def fox_layer(l):
        hflat = h[:, :, :].rearrange("p c t -> p (c t)")
        bflat = blk16[:, :, :].rearrange("p c n -> p (c n)")
        qT = big[:, 0:NCH * T].rearrange("p (c t) -> p c t", t=T)
        oT = big[:, NCH * T:2 * NCH * T].rearrange("p (c t) -> p c t", t=T)
        kvb = big[:, 2 * NCH * T:3 * NCH * T]
        Kc = [kvb[:, r_ * 2048:(r_ + 1) * 2048] for r_ in range(4)]
        Vc = [kvb[:, 8192 + r_ * 2048:8192 + (r_ + 1) * 2048].rearrange("p (t n) -> p t n", n=128) for r_ in range(4)]
        kst = bflat[:, 0:512]
        kb = bflat[:, 512:768].bitcast(BF16)
        vst = bflat[:, 768:1280]
        vb = bflat[:, 1280:1536].bitcast(BF16)
        lfa = bflat[:, 1536:1808]
        lfb = bflat[:, 1808:2080]
        import os
        FST = float(os.environ.get('FOX_STAGE', '9'))
        if FST <= 0.05:
            return
        def evac_q(oc, lo, n, b):
            P.op('act', lambda e: e.activation(out=qT[:, oc, lo:lo + n], in_=banks[b][:, 0:n], func=AF.Identity), r=[('bank', b)],
                 w=keys('qT', oc, lo, lo + n))
        project(fox_w_qkv[:, 0:D], NCH, D, h, 'h', evac_q)
        if FST <= 0.08:
            return

        def evac_k(oc, lo, n, b):
            P.op('act', lambda e: e.activation(out=kst[:, 0:n], in_=banks[b][:, 0:n], func=AF.Identity), r=[('bank', b)], w=['kst'])
            P.dma('sp', lambda e: e.dma_start(out=o_kT[:, oc, lo:lo + n], in_=kst[:, 0:n]), r=['kst'], w=['o_kT'])
            if FST <= 0.09:
                return
            if lo < TP:
                P.op('act', lambda e: e.activation(out=kb[:, 0:n], in_=banks[b][:, 0:n], func=AF.Identity), r=[('bank', b)], w=['kb'])
                if FST > 0.095:
                    P.dma('sp', lambda e: e.dma_start(out=kv_in[oc].ap()[:, lo:lo + n], in_=kb[:, 0:n]), r=['kb'], w=['kv_in%d' % oc])
            else:
                P.op('act', lambda e: e.activation(out=ksm[:, oc, :], in_=banks[b][:, 0:n], func=AF.Identity), r=[('bank', b)], w=['ksm'])
        project(fox_w_qkv[:, D:2 * D], NCH, D, h, 'h', evac_k)
        if FST <= 0.4:
            return
        for cg in range(2):
            t, key = load_panel(fox_w_qkv[:, 2 * D:3 * D], NCH, cg * 512, 512)
            for tt in range(17):
                b = bank()
                for q in range(NCH):
                    P.op('pe', lambda e: e.matmul(banks[b][:, 0:512], lhsT=h[:, q, tt * 128:(tt + 1) * 128], rhs=t[:, q, :],
                                                  start=(q == 0), stop=(q == NCH - 1)),
                         r=[key] + keys('h', q, tt * 128, tt * 128 + 128), w=[('bank', b)])
                P.op('act', lambda e: e.activation(out=vst, in_=banks[b][:, 0:512], func=AF.Identity), r=[('bank', b)], w=['vst'])
                P.dma('sp', lambda e: e.dma_start(out=o_v[tt * 128:(tt + 1) * 128, cg * 512:(cg + 1) * 512], in_=vst), r=['vst'], w=['o_v'])
                if tt < 16:
                    P.op('act', lambda e: e.activation(out=vb, in_=banks[b][:, 0:512], func=AF.Identity), r=[('bank', b)], w=['vb'])
                    for cl in range(4):
                        c_ = cg * 4 + cl
                        P.dma('sp', lambda e: e.dma_start(out=kv_in[c_].ap()[:, 2048 + tt * 128:2048 + (tt + 1) * 128],
                                                          in_=vb[:, cl * 128:(cl + 1) * 128]), r=['vb'], w=['kv_in%d' % c_])
                else:
                    P.op('act', lambda e: e.activation(out=vsm[:, cg * 512:(cg + 1) * 512], in_=banks[b][:, 0:512], func=AF.Identity), r=[('bank', b)], w=['vsm'])
        if FST <= 0.6:
            return
        P.dma('pool', lambda e: e.dma_start(out=wf[:, :, :], in_=fox_w_f.rearrange("(kc p) n -> p kc n", p=128)), w=['wf'])
        b = bank()
        for tt in range(17):
            for q in range(NCH):
                P.op('pe', lambda e: e.matmul(banks[b][:, tt * 16:(tt + 1) * 16], lhsT=h[:, q, tt * 128:(tt + 1) * 128], rhs=wf[:, q, :],
                                              start=(q == 0), stop=(q == NCH - 1)),
                     r=['wf'] + keys('h', q, tt * 128, tt * 128 + 128), w=[('bank', b)])
        lf2 = lftm[:, :, :].rearrange("p t e -> p (t e)")
        P.op('dve', lambda e: e.tensor_tensor(out=lftm[:, :, :], in0=banks[b][:, 0:272].rearrange("p (t e) -> p t e", e=16),
                                              in1=fbf[:, :].unsqueeze(1).to_broadcast([128, 17, 16]), op=ALU.add), r=[('bank', b), 'fbf'], w=['lftm'])
        P.op('act', lambda e: e.activation(out=lfa, in_=lf2, func=AF.Abs), r=['lftm'], w=['lfa'])
        P.op('act', lambda e: e.activation(out=lfa, in_=lfa, func=AF.Exp, scale=-1.0), r=['lfa'], w=['lfa'])
        P.op('act', lambda e: e.activation(out=lfa, in_=lfa, func=AF.Ln, bias=small[:, 1:2]), r=['lfa', 'small'], w=['lfa'])
        P.op('dve', lambda e: e.tensor_scalar_min(out=lfb, in0=lf2, scalar1=0.0), r=['lftm'], w=['lfb'])
        P.op('dve', lambda e: e.tensor_tensor(out=lf2, in0=lfb, in1=lfa, op=ALU.subtract), r=['lfa', 'lfb'], w=['lftm'])
        P.dma('sp', lambda e: e.dma_start(out=o_lf.rearrange("(t p) e -> p t e", p=128), in_=lftm[:, :, :]), r=['lftm'], w=['o_lf'])
        if FST <= 0.8:
            return
        for c in range(NCH):
            P.coll(lambda e: e.collective_compute("AllGather", ALU.bypass, replica_groups=GROUPS,
                                                  ins=[kv_in[c].ap().opt()], outs=[kv_out[c].ap().opt()]),
                   r=['kv_in%d' % c], w=['kv_out%d' % c])
        P.fence()
        import os
        if FST <= 1:
            return
        idf, trif, onef, tri2 = cstf[:, 0, :], cstf[:, 1, :], cstf[:, 2, :], cstf[:, 3, :]
        HF = hflat[:, :].bitcast(F32)
        biasAll = HF[:, 0:4096].rearrange("p (r q x) -> p r q x", r=4, q=4)
        Mband = hflat[:, 8192:16384].rearrange("p (r m x) -> p r m x", r=4, m=4)
        Fg = bflat[:, 0:1024].rearrange("p (r x) -> p r x", r=4)
        pay = bflat[:, 1024:1296]
        PTq = bflat[:, 1296:1360].rearrange("p (q e) -> p q e", e=16)
        base = bflat[:, 1360:1424].rearrange("p (r e) -> p r e", e=16)
        bme = bflat[:, 1424:1440]
        cA = bflat[:, 1440:1696].rearrange("p (r q e) -> p r q e", r=4, q=4)
        cC = bflat[:, 1696:1952].rearrange("p (r q e) -> p r q e", r=4, q=4)
        tmf = bflat[:, 1952:2464]
        bs_s = bflat[:, 2464:2976].rearrange("p (s x) -> p s x", s=2)
        bn_s = bflat[:, 2976:2992]
        lfp = bflat[:, 2992:3504].rearrange("p (s t e) -> p s t e", s=2, t=16)
        totp = bflat[:, 3504:3536].rearrange("p (s e) -> p s e", s=2)
        ptile = [bflat[:, 3536:3792].bitcast(BF16), bflat[:, 3792:4048].bitcast(BF16)]
        rden = tmpf0[:, 0:512]
        otmp = tmpf0[:, 512:1024]

        def cumsum_tiles(src3, ntile, dst_cols, tot_dst):
            b = bank()
            for tt in range(ntile):
                P.op('pe', lambda e: e.matmul(banks[b][:, tt * 16:(tt + 1) * 16], lhsT=trif, rhs=src3[:, tt, :], start=True, stop=(tt == 0)),
                     r=['cstf', 'lftm', 'lfp'], w=[('bank', b)])
                for t2 in range(tt):
                    P.op('pe', lambda e: e.matmul(banks[b][:, tt * 16:(tt + 1) * 16], lhsT=onef, rhs=src3[:, t2, :], start=False, stop=(t2 == tt - 1)),
                         r=['cstf', 'lftm', 'lfp'], w=[('bank', b)])
            for t2 in range(ntile):
                P.op('pe', lambda e: e.matmul(banks[b][:, 496:512], lhsT=onef, rhs=src3[:, t2, :], start=(t2 == 0), stop=(t2 == ntile - 1)),
                     r=['cstf', 'lftm', 'lfp'], w=[('bank', b)])
            P.op('act', lambda e: e.activation(out=dst_cols, in_=banks[b][:, 0:ntile * 16], func=AF.Identity), r=[('bank', b)], w=['pay', 'Fg', 'bs_s'])
            P.op('act', lambda e: e.activation(out=tot_dst, in_=banks[b][:, 496:512], func=AF.Identity), r=[('bank', b)], w=['pay', 'totp'])

        cumsum_tiles(lftm, 16, pay[:, 0:256], pay[:, 256:272])
        P.op('dve', lambda e: e.memset(PTq[:, 0, :], 0.0), w=['PTq'])
        b = bank()
        for qb in range(1, 4):
            for t2 in range(4 * qb):
                P.op('pe', lambda e: e.matmul(banks[b][:, qb * 16:(qb + 1) * 16], lhsT=onef, rhs=lftm[:, t2, :], start=(t2 == 0), stop=(t2 == 4 * qb - 1)),
                     r=['cstf', 'lftm'], w=[('bank', b)])
        P.op('act', lambda e: e.activation(out=PTq[:, 1:4, :], in_=banks[b][:, 16:64].rearrange("p (q e) -> p q e", e=16), func=AF.Identity),
             r=[('bank', b)], w=['PTq'])
        xgf = tmpf0[:, 0:1088].rearrange('p (r n) -> p r n', r=4)
        exchange(pay, 272, ['pay'], W=272, gdst=xgf, gkeys=['xgf'])
        P.op('dve', lambda e: e.memset(base[:, 0, :], 0.0), w=['base'])
        for r_ in range(1, 4):
            P.op('dve', lambda e: e.tensor_tensor(out=base[:, r_, :], in0=base[:, r_ - 1, :], in1=xgf[:, r_ - 1, 256:272], op=ALU.add),
                 r=['xgf', 'base'], w=['base'])
        for r_ in range(4):
            P.op('dve', lambda e: e.tensor_tensor(out=Fg[:, r_, :].rearrange("p (t e) -> p t e", e=16),
                                                  in0=xgf[:, r_, 0:256].rearrange("p (t e) -> p t e", e=16),
                                                  in1=base[:, r_, :].unsqueeze(1).to_broadcast([128, 16, 16]), op=ALU.add),
                 r=['xgf', 'base'], w=['Fg'])
        P.op('dve', lambda e: e.tensor_scalar(out=bme, in0=xgf[:, 0, 256:272], scalar1=mk[:, 4:5], scalar2=None, op0=ALU.mult), r=['xgf', 'mk'], w=['bme'])
        for r_ in range(1, 4):
            P.op('dve', lambda e: e.scalar_tensor_tensor(out=bme, in0=xgf[:, r_, 256:272], scalar=mk[:, 4 + r_:5 + r_], in1=bme, op0=ALU.mult, op1=ALU.add),
                 r=['xgf', 'mk', 'bme'], w=['bme'])
        for qb in range(4):
            P.op('dve', lambda e: e.tensor_tensor(out=PTq[:, qb, :], in0=PTq[:, qb, :], in1=bme, op=ALU.add), r=['PTq', 'bme'], w=['PTq'])
        for r_ in range(4):
            for qb in range(4):
                P.op('dve', lambda e: e.tensor_scalar(out=cA[:, r_, qb, :], in0=PTq[:, qb, :], scalar1=nbc[:, r_:r_ + 1], scalar2=None, op0=ALU.add),
                     r=['PTq', 'nbc'], w=['cA'])
                P.op('dve', lambda e: e.tensor_scalar(out=cC[:, r_, qb, :], in0=PTq[:, qb, :], scalar1=nbc[:, 4 + r_:5 + r_], scalar2=None, op0=ALU.add),
                     r=['PTq', 'nbc'], w=['cC'])
                na = min(16, 4 * qb + 4)
                bA = biasAll[:, r_, qb, 0:na * 16].rearrange("p (t e) -> p t e", e=16)
                P.op('dve', lambda e: e.tensor_tensor(out=bA, in0=cA[:, r_, qb, :].unsqueeze(1).to_broadcast([128, na, 16]),
                                                      in1=Fg[:, r_, 0:na * 16].rearrange("p (t e) -> p t e", e=16), op=ALU.subtract),
                     r=['cA', 'Fg'], w=['biasAll'])
                if na < 16:
                    bC = biasAll[:, r_, qb, na * 16:256].rearrange("p (t e) -> p t e", e=16)
                    P.op('dve', lambda e: e.tensor_tensor(out=bC, in0=cC[:, r_, qb, :].unsqueeze(1).to_broadcast([128, 16 - na, 16]),
                                                          in1=Fg[:, r_, na * 16:256].rearrange("p (t e) -> p t e", e=16), op=ALU.subtract),
                         r=['cC', 'Fg'], w=['biasAll'])
        for m_ in range(4):
            P.dma('sp', lambda e: e.dma_start(out=tmf, in_=tmk[:, m_, :]), r=['tmf'], w=['tmf'])
            for r_ in range(4):
                P.op('dve', lambda e: e.tensor_scalar(out=Mband[:, r_, m_, :], in0=tmf, scalar1=mk[:, 8 + r_:9 + r_], scalar2=mk[:, 4 + r_:5 + r_],
                                                      op0=ALU.mult, op1=ALU.add), r=['tmf', 'mk'], w=['Mband'])
        P.dma('sp', lambda e: e.dma_start(out=lfp.rearrange("p s t e -> p (s t e)"), in_=lfc[:, :, :, :].rearrange("p s t e -> p (s t e)")), w=['lfp'])
        for s_ in range(2):
            cumsum_tiles(lfp[:, s_, :, :], 16, bs_s[:, s_, :], totp[:, s_, :])
            P.op('dve', lambda e: e.tensor_tensor(out=bs_s[:, s_, :].rearrange("p (t e) -> p t e", e=16),
                                                  in0=totp[:, s_, :].unsqueeze(1).to_broadcast([128, 16, 16]),
                                                  in1=bs_s[:, s_, :].rearrange("p (t e) -> p t e", e=16), op=ALU.subtract),
                 r=['bs_s', 'totp'], w=['bs_s'])
        b = bank()
        P.op('pe', lambda e: e.matmul(banks[b][:, 0:16], lhsT=tri2, rhs=lftm[:, 16, :], start=True, stop=True), r=['cstf', 'lftm'], w=[('bank', b)])
        P.op('act', lambda e: e.activation(out=bn_s, in_=banks[b][:, 0:16], func=AF.Identity, scale=-1.0), r=[('bank', b)], w=['bn_s'])

        if FST <= 2:
            return
        ptile.append(tmpf0[:, 1024:1280].bitcast(BF16))
        dacc = tmpf0[:, 1280:1792]
        NPT = 3

        def attend(keytiles, pb, c, qlo, qn, hd):
            bo, bd = gbank('o'), gbank('d')
            nt = len(keytiles)
            sb_ = [None] * nt

            def emit_s(i):
                kT, bcol, vT, band, kp0, nk = keytiles[i]
                bs = gbank('s')
                sb_[i] = bs
                P.op('pe', lambda e: e.matmul(banks[bs][kp0:kp0 + nk, 0:qn], lhsT=kT, rhs=qT[pb:pb + 64, c, qlo:qlo + qn], start=True, stop=True),
                     r=['Kc', 'ksm'] + keys('qT', c, qlo, qlo + qn), w=[('bank', bs)])
            emit_s(0)
            if nt > 1:
                emit_s(1)
            for i, (kT, bcol, vT, band, kp0, nk) in enumerate(keytiles):
                bs = sb_[i]
                pt_ = ptile[i % NPT]
                P.op('act', lambda e: e.activation(out=pt_[kp0:kp0 + nk, 0:qn], in_=banks[bs][kp0:kp0 + nk, 0:qn], func=AF.Exp, scale=0.125, bias=bcol),
                     r=[('bank', bs), 'biasAll', 'bs_s', 'bn_s'], w=[('pt', i % NPT)])
                if band is not None:
                    P.op('dve', lambda e: e.tensor_tensor(out=pt_[kp0:kp0 + nk, 0:qn], in0=pt_[kp0:kp0 + nk, 0:qn], in1=band, op=ALU.mult),
                         r=[('pt', i % NPT), 'Mband', 't64b'], w=[('pt', i % NPT)])
                if i + 2 < nt:
                    emit_s(i + 2)
                P.op('pe', lambda e: e.matmul(banks[bo][pb:pb + 64, 0:qn], lhsT=vT, rhs=pt_[kp0:kp0 + nk, 0:qn], start=(i == 0), stop=(i == nt - 1)),
                     r=[('pt', i % NPT), 'Vc', 'vsm'], w=[('bank', bo)])
                if kp0 == 0 and nk == 128:
                    if i == 0:
                        P.op('dve', lambda e: e.tensor_copy(out=dacc[:, 0:qn], in_=pt_[:, 0:qn]), r=[('pt', i % NPT)], w=['dacc'])
                    else:
                        P.op('dve', lambda e: e.tensor_tensor(out=dacc[:, 0:qn], in0=dacc[:, 0:qn], in1=pt_[:, 0:qn], op=ALU.add),
                             r=[('pt', i % NPT), 'dacc'], w=['dacc'])
                else:
                    P.op('dve', lambda e: e.tensor_tensor(out=dacc[kp0:kp0 + nk, 0:qn], in0=dacc[kp0:kp0 + nk, 0:qn], in1=pt_[kp0:kp0 + nk, 0:qn], op=ALU.add),
                         r=[('pt', i % NPT), 'dacc'], w=['dacc'])
            P.op('pe', lambda e: e.matmul(banks[bd][pb:pb + 64, 0:qn], lhsT=onef[:, 0:64], rhs=dacc[:, 0:qn], start=True, stop=True),
                 r=['dacc', 'cstf'], w=[('bank', bd)])
            P.op('dve', lambda e: e.reciprocal(out=rden[pb:pb + 64, 0:qn], in_=banks[bd][pb:pb + 64, 0:qn]), r=[('bank', bd)], w=['rden'])
            P.op('act', lambda e: e.activation(out=otmp[pb:pb + 64, 0:qn], in_=banks[bo][pb:pb + 64, 0:qn], func=AF.Identity), r=[('bank', bo)], w=['otmp'])
            P.op('dve', lambda e: e.tensor_tensor(out=oT[pb:pb + 64, c, qlo:qlo + qn], in0=otmp[pb:pb + 64, 0:qn], in1=rden[pb:pb + 64, 0:qn], op=ALU.mult),
                 r=['rden', 'otmp'], w=keys('oT', c, qlo, qlo + qn))

        for c in range(NCH):
            for r_ in range(4):
                P.dma('sp', lambda e: e.dma_start(out=Kc[r_], in_=kv_out[c].ap()[r_ * 128:(r_ + 1) * 128, 0:2048]), r=['kv_out%d' % c], w=['Kc'])
                P.dma('sp', lambda e: e.dma_start(out=Vc[r_], in_=kv_out[c].ap()[r_ * 128:(r_ + 1) * 128, 2048:4096].rearrange("p (t n) -> p t n", n=128)),
                      r=['kv_out%d' % c], w=['Vc'])
            for hh in range(2):
                hd, pb = 2 * c + hh, 64 * hh
                for qb in range(4):
                    tiles = []
                    for r_ in range(4):
                        for kt in range(16):
                            m_ = kt - 4 * qb
                            band = Mband[:, r_, m_, :] if 0 <= m_ < 4 else None
                            tiles.append((Kc[r_][pb:pb + 64, kt * 128:(kt + 1) * 128], biasAll[:, r_, qb, kt * 16 + hd:kt * 16 + hd + 1],
                                          Vc[r_][:, kt, pb:pb + 64], band, 0, 128))
                    attend(tiles, pb, c, qb * 512, 512, hd)
            for s_ in range(2 if FST >= 4 else 0):
                P.dma('pool', lambda e: e.dma_start(out=Kc[0], in_=kcT[:, s_, c, :]), w=['Kc'])
                P.dma('pool', lambda e: e.dma_start(out=Vc[0], in_=vc[s_, :, c * 128:(c + 1) * 128].rearrange("(t p) n -> p t n", p=128)), w=['Vc'])
                for hh in range(2):
                    hd, pb = 2 * c + hh, 64 * hh
                    kp0 = 64 * s_
                    tiles = []
                    for kt in range(16):
                        tiles.append((Kc[0][pb:pb + 64, kt * 128:(kt + 1) * 128], bs_s[:, s_, kt * 16 + hd:kt * 16 + hd + 1],
                                      Vc[0][:, kt, pb:pb + 64], None, 0, 128))
                    tiles.append((ksm[pb:pb + 64, c, kp0:kp0 + 64], bn_s[kp0:kp0 + 64, hd:hd + 1], vsm[kp0:kp0 + 64, c * 128 + pb:c * 128 + pb + 64],
                                  t64b[kp0:kp0 + 64, :], kp0, 64))
                    attend(tiles, pb, c, TP + s_ * TS, TS, hd)
        P.fence()
        out_proj_residual(fox_w_o, NCH, oT, 'oT', 1)

    def ffn_layer(l):
        FV = lambda idx, fc: ffv[:, l, idx, fc:fc + 1]
        BK = [('blk16', c) for c in range(NCH)]
        gpre = tmpf0[:, 0:T]
        hal = small[:, 24:32]
        stf = fix
        P.dma('sp', lambda e: e.dma_start(out=ftmp[:, :, :].rearrange("p a c -> p (a c)"),
                                          in_=st_ffn[:, l, :, :, :].rearrange("p c s k -> p (c s k)")), w=['ftmp'])
        stv = ftmp[:, :, :].rearrange("p a c -> p (a c)").rearrange("p (c s k) -> p c s k", s=2, k=2)
        for fc in range(NFF):
            pn, o = fc // 4, fc % 4
            if o == 0:
                ncols = min(512, DFF - pn * 512)
                tg, kg = load_panel(ffn_w_gate[l], NCH, pn * 512, ncols)
                tu, ku = load_panel(ffn_w_up[l], NCH, pn * 512, ncols)
            for (lo, n) in BLKS:
                b = bank()
                for q in range(NCH):
                    P.op('pe', lambda e, b=b, o=o, q=q, tg=tg, lo=lo, n=n: e.matmul(
                        banks[b][:, 0:n], lhsT=tg[:, q, o * 128:(o + 1) * 128], rhs=h[:, q, lo:lo + n],
                        start=(q == 0), stop=(q == NCH - 1)), r=[kg] + keys('h', q, lo, lo + n), w=[('bank', b)])
                P.op('act', lambda e, b=b, lo=lo, n=n: e.activation(out=gpre[:, lo:lo + n], in_=banks[b][:, 0:n], func=AF.Identity),
                     r=[('bank', b)], w=tk(0))
            for (s, (so, ln)) in enumerate(SEQS):
                P.op('dve', lambda e, fc=fc, s=s, so=so, ln=ln: e.tensor_copy(out=gt[:, fc, s, :], in_=gpre[:, so + ln - 2:so + ln]),
                     r=tk(0), w=['gt'])
            P.op('dve', lambda e, fc=fc: e.tensor_copy(out=fix[:, fc, 0:2], in_=gpre[:, 0:2]), r=tk(0), w=['fix'])
            gc = blk16[:, :, :].rearrange('p c n -> p (c n)')[:, 0:T]
            P.op('dve', lambda e, fc=fc: e.tensor_scalar(out=gc[:, :], in0=gpre[:, :], scalar1=FV(2, fc), scalar2=FV(3, fc),
                                                        op0=ALU.mult, op1=ALU.add), r=tk(0) + ['ffv'], w=BK)
            for (s, (so, ln)) in enumerate(SEQS):
                for d_, wi in ((1, 1), (2, 0)):
                    P.op('dve', lambda e, fc=fc, so=so, ln=ln, d_=d_, wi=wi: e.scalar_tensor_tensor(
                        out=gc[:, so + d_:so + ln], in0=gpre[:, so:so + ln - d_], scalar=FV(wi, fc), in1=gc[:, so + d_:so + ln],
                        op0=ALU.mult, op1=ALU.add), r=tk(0) + ['ffv'] + BK, w=BK)
                if s > 0:
                    for (tcol, terms) in ((0, ((0, 0), (1, 1))), (1, ((0, 1),))):
                        for (wi, pk) in terms:
                            P.op('dve', lambda e, fc=fc, s=s, so=so, tcol=tcol, wi=wi, pk=pk: e.scalar_tensor_tensor(
                                out=gc[:, so + tcol:so + tcol + 1], in0=stv[:, fc, s - 1, pk:pk + 1], scalar=FV(wi, fc),
                                in1=gc[:, so + tcol:so + tcol + 1], op0=ALU.mult, op1=ALU.add),
                                r=['ftmp', 'ffv'] + BK, w=BK)
            P.op('act', lambda e: e.activation(out=gc[:, :], in_=gc[:, :], func=AF.Gelu), r=BK, w=BK)
            for (lo, n) in BLKS:
                b = bank()
                for q in range(NCH):
                    P.op('pe', lambda e, b=b, o=o, q=q, tu=tu, lo=lo, n=n: e.matmul(
                        banks[b][:, 0:n], lhsT=tu[:, q, o * 128:(o + 1) * 128], rhs=h[:, q, lo:lo + n],
                        start=(q == 0), stop=(q == NCH - 1)), r=[ku] + keys('h', q, lo, lo + n), w=[('bank', b)])
                if lo == 0:
                    P.op('act', lambda e, b=b, fc=fc: e.activation(out=fix[:, fc, 2:4], in_=banks[b][:, 0:2], func=AF.Identity),
                         r=[('bank', b)], w=['fix'])
                P.op('dve', lambda e, b=b, fc=fc, lo=lo, n=n: e.tensor_tensor(out=hid[:, fc, lo:lo + n], in0=gc[:, lo:lo + n], in1=banks[b][:, 0:n],
                                                                           op=ALU.mult),
                     r=[('bank', b)] + BK, w=keys('hid', fc, lo, lo + n))
        P.dma('sp', lambda e: e.dma_start(out=o_ffn[:, l, :, :, :], in_=gt[:, :, :, :]), r=['gt'], w=['o_ffn%d' % l])
        exchange(gt[:, :, 0, :], 2 * NFF, ['gt'], dst_view=lambda d: d.rearrange('p (c k) -> p c k', k=2))
        masked_sum(fprev[:, :, :].rearrange("p c k -> p (c k)"), ['fprev'], 2 * NFF, 0)
        W0, W1, W2, BB = ffv[:, l, 0, :], ffv[:, l, 1, :], ffv[:, l, 2, :], ffv[:, l, 3, :]
        g0, g1, t_ = ftmp[:, 0, :], ftmp[:, 1, :], ftmp[:, 2, :]
        TT = lambda out, a, b_, op, rk=(), wk=('ftmp',): P.op('dve', lambda e: e.tensor_tensor(out=out, in0=a, in1=b_, op=op),
                                                               r=['ftmp', 'fprev', 'fix', 'ffv'], w=list(wk))
        TT(g0, W0, fprev[:, :, 0], ALU.mult); TT(g0, g0, BB, ALU.add)
        TT(t_, W1, fprev[:, :, 1], ALU.mult); TT(g0, g0, t_, ALU.add)
        TT(t_, W2, fix[:, :, 0], ALU.mult); TT(g0, g0, t_, ALU.add)
        TT(g1, W0, fprev[:, :, 1], ALU.mult); TT(g1, g1, BB, ALU.add)
        TT(t_, W1, fix[:, :, 0], ALU.mult); TT(g1, g1, t_, ALU.add)
        TT(t_, W2, fix[:, :, 1], ALU.mult); TT(g1, g1, t_, ALU.add)
        P.op('act', lambda e: e.activation(out=ftmp[:, 0:2, :], in_=ftmp[:, 0:2, :], func=AF.Gelu), r=['ftmp'], w=['ftmp'])
        hkeys = [kk for fc in range(NFF) for kk in keys('hid', fc, 0, 2)]
        P.op('dve', lambda e: e.tensor_tensor(out=hid[:, :, 0], in0=g0, in1=fix[:, :, 2], op=ALU.mult), r=['ftmp', 'fix'], w=hkeys)
        P.op('dve', lambda e: e.tensor_tensor(out=hid[:, :, 1], in0=g1, in1=fix[:, :, 3], op=ALU.mult), r=['ftmp', 'fix'], w=hkeys)
        out_proj_residual(ffn_w_down[l], NFF, hid, 'hid', 3, pw=128)

    for l in range(nlayers):
        ada_layer(l)
        prenorm(0, 0)
        P.fence()
        if l in (0, 3):
            lru_layer(l, l // 3)
        elif l == 1:
            ssd_layer(l)
        elif l == 2:
            fox_layer(l)
        P.fence()
        prenorm(2, 3)
        ffn_layer(l)
        P.fence()

    for c in range(NCH):
        P.dma('sp', lambda e, c=c: e.dma_start(out=yT[:, c, :], in_=xd[:, c, :]), r=keys('xd', c, 0, T), w=['yT%d' % c])
    P.finish()
    st.close()
    return nc


def fm(a):
    sh = a.shape
    nch = sh[-1] // 128
    b = a.reshape(sh[:-1] + (nch, 128))
    return np.ascontiguousarray(np.moveaxis(b, -1, 0))


def unfm(a):
    b = np.moveaxis(a, 0, -1)
    return np.ascontiguousarray(b.reshape(b.shape[:-2] + (b.shape[-2] * 128,)))


NLAYERS = 4


def kernel(**inp):
    f = lambda n: np.asarray(inp[n], dtype=np.float32)
    nc = build(nlayers=NLAYERS)
    x_prompt, x_sample = f('x_prompt'), f('x_sample')
    lru_vec = np.stack([np.stack([f('lru_b_x')[j], f('lru_b_y')[j], f('lru_conv_w')[j][0], f('lru_conv_w')[j][1], f('lru_conv_w')[j][2],
                                   f('lru_conv_w')[j][3], f('lru_conv_b')[j], f('lru_b_a')[j], f('lru_b_i')[j], f('lru_lambda')[j],
                                   f('lru_b_o')[j]]) for j in range(2)])
    ffn_vec = np.concatenate([f('ffn_conv_w'), f('ffn_conv_b')[:, None, :]], 1)
    shared = dict(ada_bT=fm(f('ada_b')), norm_gT=fm(f('norm_g')),
                  lru_w_x=f('lru_w_x'), lru_w_y=f('lru_w_y'), lru_w_o=f('lru_w_o'), lru_w_a=f('lru_w_a'), lru_w_i=f('lru_w_i'),
                  ssd_w_in=f('ssd_w_in'), ssd_w_out=f('ssd_w_out'),
                  ssd_cvT=fm(np.concatenate([f('ssd_conv_w'), f('ssd_conv_b')[None, :]], 0)),
                  ssd_rep=np.ascontiguousarray(np.tile(np.concatenate([f('ssd_dt_bias'), f('ssd_a_log'), f('ssd_d')])[None, :], (128, 1))),
                  ssd_ngT=fm(f('ssd_norm_g')),
                  cst=np.ascontiguousarray(np.stack([np.eye(128, dtype=np.float32), np.triu(np.ones((128, 128), np.float32)),
                                                     np.ones((128, 128), np.float32),
                                                     np.kron(np.eye(2, dtype=np.float32), np.triu(np.ones((64, 64), np.float32)))], 1)),
                  fox_w_qkv=f('fox_w_qkv'), fox_w_o=f('fox_w_o'), fox_w_f=f('fox_w_f'),
                  fox_bf=np.ascontiguousarray(np.tile(f('fox_b_f')[None, :], (128, 1))),
                  tmk=np.ascontiguousarray(np.stack([(m * 128 + np.arange(128)[:, None] <= np.arange(512)[None, :]).astype(np.float32)
                                                     for m in range(4)], 1)),
                  t64=np.ascontiguousarray(((np.arange(128)[:, None] % 64) <= np.arange(64)[None, :]).astype(np.float32)),
                  lru_vT=fm(lru_vec), ffn_vT=fm(ffn_vec))
    for i in range(4):
        shared['ada_w%d' % i] = np.ascontiguousarray(f('ada_w')[i])
        shared['ffn_w_gate%d' % i] = np.ascontiguousarray(f('ffn_w_gate')[i])
        shared['ffn_w_up%d' % i] = np.ascontiguousarray(f('ffn_w_up')[i])
        shared['ffn_w_down%d' % i] = np.ascontiguousarray(f('ffn_w_down')[i])
    in_maps = []
    for c in range(NCORES):
        b, j = c // 4, c % 4
        xs = np.concatenate([x_prompt[b, j * TP:(j + 1) * TP], x_sample[2 * c], x_sample[2 * c + 1]], 0)
        xT = np.ascontiguousarray(xs.reshape(T, NCH, 128).transpose(2, 1, 0))
        cs = np.stack([f('c_prompt')[b], f('c_sample')[2 * c], f('c_sample')[2 * c + 1]], 0)
        cT = np.ascontiguousarray(cs.reshape(3, NCH, 128).transpose(2, 1, 0))
        msk = np.zeros((128, 16), np.float32)
        for r in range(4):
            msk[:, r] = 1.0 if r == j - 1 else 0.0
            msk[:, 4 + r] = 1.0 if r < j else 0.0
            msk[:, 8 + r] = 1.0 if r == j else 0.0
            msk[:, 12 + r] = 1.0 if r <= j else 0.0
        ck = f('cache_l2_k')[2 * c:2 * c + 2].reshape(2, 2048, D)
        kcT = np.ascontiguousarray(ck.transpose(0, 2, 1).reshape(2, NCH, 128, 2048).transpose(2, 0, 1, 3))
        vc_ = np.ascontiguousarray(f('cache_l2_v')[2 * c:2 * c + 2].reshape(2, 2048, D))
        lfc = np.ascontiguousarray(f('cache_l2_logf')[2 * c:2 * c + 2].reshape(2, 16, 128, 16).transpose(2, 0, 1, 3))
        lc = np.stack([f('state_l0_conv')[2 * c:2 * c + 2], f('state_l3_conv')[2 * c:2 * c + 2]])
        st_lconv = np.ascontiguousarray(lc.reshape(2, 2, 3, NCH, 128).transpose(4, 0, 3, 1, 2))
        lh = np.stack([f('state_l0_h')[2 * c:2 * c + 2], f('state_l3_h')[2 * c:2 * c + 2]])
        st_lh = np.ascontiguousarray(lh.reshape(2, 2, NCH, 128).transpose(3, 0, 2, 1))
        sf = f('state_ffn_conv')[:, 2 * c:2 * c + 2]
        st_ffn = np.ascontiguousarray(sf.reshape(4, 2, 2, NFF, 128).transpose(4, 0, 3, 1, 2))
        sc = f('state_l1_conv')[2 * c:2 * c + 2]
        st_sconv = np.ascontiguousarray(sc.reshape(2, 3, 24, 128).transpose(3, 2, 0, 1))
        ss = f('state_l1_ssm')[2 * c:2 * c + 2]
        st_ssm = np.ascontiguousarray(ss.transpose(3, 0, 1, 2).reshape(128, 2, 2048))
        m = dict(shared)
        m.update(st_sconv=st_sconv, st_ssm=st_ssm, kcT=kcT, vc=vc_, lfc=lfc)
        m.update(xT=xT, cT=cT, msk=msk, st_lconv=st_lconv, st_lh=st_lh, st_ffn=st_ffn)
        in_maps.append(m)
    res = run_bass_kernel_spmd(nc, in_maps, core_ids=list(range(NCORES)))
    R = res.results
    B, DB = 2, 16
    y_prompt = np.zeros((B, 4 * TP, D), np.float32)
    y_sample = np.zeros((DB, TS, D), np.float32)
    p_lconv = [np.zeros((B, 3, D), np.float32) for _ in range(2)]
    p_lh = [np.zeros((B, D), np.float32) for _ in range(2)]
    s_lconv = [np.zeros((DB, 3, D), np.float32) for _ in range(2)]
    s_lh = [np.zeros((DB, D), np.float32) for _ in range(2)]
    p_sconv = np.zeros((B, 3, 3072), np.float32); s_sconv = np.zeros((DB, 3, 3072), np.float32)
    p_ssm = np.zeros((B, 32, 64, 128), np.float32); s_ssm = np.zeros((DB, 32, 64, 128), np.float32)
    p_k = np.zeros((B, 4 * TP, 16, 64), np.float32); p_v = np.zeros((B, 4 * TP, 16, 64), np.float32); p_lf = np.zeros((B, 4 * TP, 16), np.float32)
    s_k = np.zeros((DB, TS, 16, 64), np.float32); s_v = np.zeros((DB, TS, 16, 64), np.float32); s_lf = np.zeros((DB, TS, 16), np.float32)
    p_ffn = np.zeros((4, B, 2, DFF), np.float32)
    s_ffn = np.zeros((4, DB, 2, DFF), np.float32)
    for c in range(NCORES):
        b, j = c // 4, c % 4
        r = R[c]
        yt = r['yT'].transpose(2, 1, 0).reshape(T, D)
        y_prompt[b, j * TP:(j + 1) * TP] = yt[0:TP]
        y_sample[2 * c] = yt[TP:TP + TS]
        y_sample[2 * c + 1] = yt[TP + TS:T]
        lc = r['o_lconv']
        lh_ = r['o_lh']
        of = r['o_ffn']
        for jj in range(2):
            cv = lc[:, jj].transpose(2, 3, 1, 0).reshape(3, 3, D)
            hv = lh_[:, jj].transpose(2, 1, 0).reshape(3, D)
            if j == 3:
                p_lconv[jj][b] = cv[0]
                p_lh[jj][b] = hv[0]
            s_lconv[jj][2 * c], s_lconv[jj][2 * c + 1] = cv[1], cv[2]
            s_lh[jj][2 * c], s_lh[jj][2 * c + 1] = hv[1], hv[2]
        kk_ = r['o_kT'].transpose(2, 1, 0).reshape(T, 16, 64)
        vv_ = r['o_v'].reshape(T, 16, 64)
        ll_ = r['o_lf']
        p_k[b, j * TP:(j + 1) * TP], p_v[b, j * TP:(j + 1) * TP], p_lf[b, j * TP:(j + 1) * TP] = kk_[0:TP], vv_[0:TP], ll_[0:TP]
        for si in range(2):
            sl = slice(TP + si * TS, TP + (si + 1) * TS)
            s_k[2 * c + si], s_v[2 * c + si], s_lf[2 * c + si] = kk_[sl], vv_[sl], ll_[sl]
        scv_ = r['o_sconv'].transpose(2, 3, 1, 0).reshape(3, 3, 3072)
        ssm_ = r['o_ssm'].reshape(128, 3, 32, 64).transpose(1, 2, 3, 0)
        if j == 3:
            p_sconv[b] = scv_[0]
            p_ssm[b] = ssm_[0]
        s_sconv[2 * c], s_sconv[2 * c + 1] = scv_[1], scv_[2]
        s_ssm[2 * c], s_ssm[2 * c + 1] = ssm_[1], ssm_[2]
        fv = of.transpose(1, 3, 4, 2, 0).reshape(4, 3, 2, DFF)
        if j == 3:
            p_ffn[:, b] = fv[:, 0]
        s_ffn[:, 2 * c], s_ffn[:, 2 * c + 1] = fv[:, 1], fv[:, 2]
    z = lambda *sh: np.zeros(sh, np.float32)
    return (y_prompt, y_sample,
            p_lconv[0], p_lh[0], p_sconv, p_ssm, p_k, p_v, p_lf,
            p_lconv[1], p_lh[1], p_ffn,
            s_lconv[0], s_lh[0], s_sconv, s_ssm, s_k, s_v, s_lf,
            s_lconv[1], s_lh[1], s_ffn)
```

```python
import numpy as np
from contextlib import ExitStack
import concourse.bass as bass
import concourse.mybir as mybir
from concourse.bass_utils import run_bass_kernel_spmd

F32 = mybir.dt.float32
BF16 = mybir.dt.bfloat16
ALU = mybir.AluOpType
AF = mybir.ActivationFunctionType

D = 1024
NCH = 8
TP = 2048
TS = 64
T = TP + 2 * TS
SEQS = [(0, TP), (TP, TS), (TP + TS, TS)]
BLKS = [(0, 512), (512, 512), (1024, 512), (1536, 512), (2048, 128)]
DFF = 2816
NFF = 22
EPS = 1e-6
NCORES = 8


class Rec:
    def __getattr__(self, name):
        def f(*a, **kw):
            self.call = (name, a, kw)
            return self
        return f


def _rec(fn):
    r = Rec()
    fn(r)
    name, a, kw = r.call
    return lambda e: getattr(e, name)(*a, **kw)


class Prog:
    def __init__(self, nc, st):
        self.nc = nc
        self.names = ['pe', 'act', 'dve', 'pool', 'sp']
        self.lists = {e: [] for e in self.names}
        self.sem = {e: st.enter_context(nc.semaphore('c_' + e)) for e in ['pe', 'act', 'dve', 'pool']}
        self.cnt = {e: 0 for e in self.sem}
        self.ND = 32
        self.dsem = [st.enter_context(nc.semaphore('d%d' % i)) for i in range(self.ND)]
        self.dcnt = [0] * self.ND
        self.dnext = {'sp': 0, 'pool': 0}
        self.dbase = {'sp': 0, 'pool': 16}
        self.csem = st.enter_context(nc.semaphore('cc'))
        self.ccnt = 0
        self.seen = {e: {} for e in self.names}
        self.res = {}

    def _deps(self, reads, writes):
        deps = {}
        for r in reads:
            s = self.res.get(r)
            if s and s[0]:
                k, v = s[0]
                deps[k] = max(deps.get(k, 0), v)
        for w in writes:
            s = self.res.get(w)
            if s:
                if s[0]:
                    k, v = s[0]
                    deps[k] = max(deps.get(k, 0), v)
                for k, v in s[1].items():
                    deps[k] = max(deps.get(k, 0), v)
        return deps

    def _need(self, eng, deps):
        for k, v in deps.items():
            if k == 'pe' and eng == 'pe':
                continue
            if self.seen[eng].get(k, 0) < v:
                self.lists[eng].append(('wait', k, v))
                self.seen[eng][k] = v

    def _mark(self, tag, reads, writes):
        k, v = tag
        for r in reads:
            s = self.res.setdefault(r, [None, {}])
            s[1][k] = max(s[1].get(k, 0), v)
        for w in writes:
            self.res[w] = [tag, {}]

    def op(self, eng, fn, r=(), w=()):
        self._need(eng, self._deps(r, w))
        self.cnt[eng] += 1
        self.lists[eng].append(('op', _rec(fn), eng))
        self._mark((eng, self.cnt[eng]), r, w)

    def dma(self, q, fn, r=(), w=()):
        i = self.dbase[q] + self.dnext[q]
        self.dnext[q] = (self.dnext[q] + 1) % 16
        deps = self._deps(r, w)
        if self.dcnt[i] > 0:
            deps[('d', i)] = max(deps.get(('d', i), 0), 16 * self.dcnt[i])
        self._need(q, deps)
        self.dcnt[i] += 1
        self.lists[q].append(('dma', _rec(fn), i))
        self._mark((('d', i), 16 * self.dcnt[i]), r, w)

    def coll(self, fn, r=(), w=()):
        self._need('pool', self._deps(r, w))
        self.ccnt += 1
        self.lists['pool'].append(('coll', _rec(fn)))
        self._mark(('cc', self.ccnt), r, w)

    def fence(self):
        allv = {e: self.cnt[e] for e in self.cnt if self.cnt[e]}
        for i in range(self.ND):
            if self.dcnt[i]:
                allv[('d', i)] = 16 * self.dcnt[i]
        if self.ccnt:
            allv['cc'] = self.ccnt
        for e in self.names:
            self._need(e, dict(allv))

    def _semof(self, k):
        if k == 'cc':
            return self.csem
        if isinstance(k, tuple):
            return self.dsem[k[1]]
        return self.sem[k]

    def finish(self):
        final = {('d', i): 16 * self.dcnt[i] for i in range(self.ND) if self.dcnt[i]}
        self._need('sp', final)
        nc = self.nc
        engs = {'pe': 'tensor', 'act': 'scalar', 'dve': 'vector', 'pool': 'gpsimd', 'sp': 'sync'}
        with nc.Block() as block:
            for name in self.names:
                lst = self.lists[name]

                def run(e, lst=lst):
                    for it in lst:
                        if it[0] == 'wait':
                            e.wait_ge(self._semof(it[1]), it[2])
                        elif it[0] == 'op':
                            it[1](e).then_inc(self.sem[it[2]], 1)
                        elif it[0] == 'dma':
                            it[1](e).then_inc(self.dsem[it[2]], 16)
                        else:
                            it[1](e).then_inc(self.csem)
                getattr(block, engs[name])(run)


def keys(name, c, lo, hi, gran=512):
    return [(name, c, b) for b in range(lo // gran, (hi - 1) // gran + 1)]


class K:
    pass


def build(nlayers=4, dbg=False):
    nc = bass.Bass("TRN2", target_bir_lowering=False)
    st = ExitStack()
    P = Prog(nc, st)
    k = K()
    k.nc, k.P, k.st = nc, P, st

    def din(name, shape):
        return nc.dram_tensor(name, list(shape), F32, kind="ExternalInput").ap()

    def dout(name, shape):
        return nc.dram_tensor(name, list(shape), F32, kind="ExternalOutput").ap()

    k.sbtot = 0

    def sb(name, shape, dt=F32):
        nb = int(np.prod(shape[1:])) * (2 if dt == BF16 else 4)
        k.sbtot += nb
        try:
            return st.enter_context(nc.sbuf_tensor(name, list(shape), dt))
        except AssertionError:
            print("SBUF OOM at", name, nb, "total requested so far", k.sbtot)
            raise

    def ps(name, shape, dt=F32):
        return st.enter_context(nc.psum_tensor(name, list(shape), dt))

    xT = din("xT", [128, NCH, T])
    cT = din("cT", [128, NCH, 3])
    msk = din("msk", [128, 16])
    ada_w = [din("ada_w%d" % i, [D, 6 * D]) for i in range(4)]
    ada_bT = din("ada_bT", [128, 4, 48])
    norm_gT = din("norm_gT", [128, 4, 4, NCH])
    lru_w_x = din("lru_w_x", [2, D, D]); lru_w_y = din("lru_w_y", [2, D, D]); lru_w_o = din("lru_w_o", [2, D, D])
    lru_w_a = din("lru_w_a", [2, 8, 128, 128]); lru_w_i = din("lru_w_i", [2, 8, 128, 128])
    lru_vT = din("lru_vT", [128, 2, 11, NCH])
    st_lconv = din("st_lconv", [128, 2, NCH, 2, 3])
    st_lh = din("st_lh", [128, 2, NCH, 2])
    ffn_w_gate = [din("ffn_w_gate%d" % i, [D, DFF]) for i in range(4)]
    ffn_w_up = [din("ffn_w_up%d" % i, [D, DFF]) for i in range(4)]
    ffn_w_down = [din("ffn_w_down%d" % i, [DFF, D]) for i in range(4)]
    ffn_vT = din("ffn_vT", [128, 4, 4, NFF])
    st_ffn = din("st_ffn", [128, 4, NFF, 2, 2])
    o_ffn = dout("o_ffn", [128, 4, NFF, 3, 2])
    ssd_w_in = din("ssd_w_in", [D, 5152]); ssd_w_out = din("ssd_w_out", [2048, D])
    ssd_cvT = din("ssd_cvT", [128, 5, 24])
    ssd_rep = din("ssd_rep", [128, 96])
    ssd_ngT = din("ssd_ngT", [128, 16])
    st_sconv = din("st_sconv", [128, 24, 2, 3])
    st_ssm = din("st_ssm", [128, 2, 2048])
    cst = din("cst", [128, 4, 128])
    o_sconv = dout("o_sconv", [128, 24, 3, 3])
    o_ssm = dout("o_ssm", [128, 3, 2048])
    fox_w_qkv = din("fox_w_qkv", [D, 3 * D]); fox_w_o = din("fox_w_o", [D, D]); fox_w_f = din("fox_w_f", [D, 16])
    fox_bf = din("fox_bf", [128, 16])
    kcT = din("kcT", [128, 2, NCH, 2048])
    vc = din("vc", [2, 2048, D])
    lfc = din("lfc", [128, 2, 16, 16])
    tmk = din("tmk", [128, 4, 512])
    t64 = din("t64", [128, 64])
    o_kT = dout("o_kT", [128, NCH, T])
    o_v = dout("o_v", [T, D])
    o_lf = dout("o_lf", [T, 16])
    kv_in = [nc.dram_tensor("kv_in%d" % c, [128, 4096], BF16) for c in range(NCH)]
    kv_out = [nc.dram_tensor("kv_out%d" % c, [4 * 128, 4096], BF16) for c in range(NCH)]
    zd = nc.dram_tensor("zd", [T, 2048], BF16).ap()
    yT = dout("yT", [128, NCH, T])
    o_lconv = dout("o_lconv", [128, 2, NCH, 3, 3])
    o_lh = dout("o_lh", [128, 2, NCH, 3])
    GROUPS = [[0, 1, 2, 3], [4, 5, 6, 7]]

    xd = nc.dram_tensor("xd", [128, NCH, T], F32).ap()
    h = sb("h", [128, NCH, T], BF16)
    big = sb("big", [128, 24 * T], BF16)
    HW = T + 9
    bufA = big[:, 0:NCH * T].rearrange("p (c t) -> p c t", t=T)
    bufB = big[:, NCH * T:NCH * T + NCH * HW].rearrange("p (c t) -> p c t", t=HW)
    hid = big[:, 0:NFF * T].rearrange("p (c t) -> p c t", t=T)
    xbc = big[:, :].rearrange("p (c t) -> p c t", t=T)
    HOFF = [3, 3 + TP + 3, 3 + TP + 3 + TS + 3]
    wpf = [sb("wp%d" % i, [128, NCH * 512], BF16) for i in range(2)]
    wsm = [sb("wsm%d" % i, [128, 2, 128], BF16) for i in range(2)]
    ones = sb("ones", [128, 128], BF16)
    siluc = sb("siluc", [128, NCH, 3], BF16)
    ctmp = sb("ctmp", [128, NCH, 3])
    ada = sb("ada", [128, 48, 3])
    adab = sb("adab", [128, 4, 48])
    ng = sb("ng", [128, 4, 4, NCH])
    gsc = sb("gsc", [128, 4, NCH, 3])
    lruv = sb("lruv", [128, 2, 11, NCH])
    cfac = sb("cfac", [128, 2, 2, NCH])
    mk = sb("mk", [128, 16])
    rstd = sb("rstd", [128, 512])
    NSQ = 4
    sq = sb("sq", [128, NSQ, 512], BF16)
    tmpf0 = sb("tmpf0", [128, T + 16])
    tmpb = sb("tmpb", [128, T], BF16)
    blk16 = sb("blk16", [128, NCH, 512])
    small = sb("small", [128, 64])
    XW = 80
    xg = sb("xg", [128, 4, XW])
    hin = sb("hin", [128, NCH, 3])
    tails = sb("tails", [128, NCH, 3, 3])
    hl = sb("hl", [128, NCH, 3])
    ffv = sb("ffv", [128, 4, 4, NFF])
    fix = sb("fix", [128, NFF, 4])
    gt = sb("gt", [128, NFF, 3, 2])
    fprev = sb("fprev", [128, NFF, 2])
    ftmp = sb("ftmp", [128, 4, NFF])
    cstf = sb("cstf", [128, 4, 128])
    ksm = sb("ksm", [128, NCH, 128], BF16)
    vsm = sb("vsm", [128, D], BF16)
    lftm = sb("lftm", [128, 17, 16])
    wf = sb("wf", [128, NCH, 16], BF16)
    ones1 = sb("ones1", [128, 64], BF16)
    fbf = sb("fbf", [128, 16])
    t64b = sb("t64b", [128, 64], BF16)
    nbc = sb("nbc", [128, 8])
    ident = sb("ident", [128, 128], BF16)
    scv = sb("scv", [128, 5, 24])
    srep = sb("srep", [128, 96])
    sng = sb("sng", [128, 16])
    wdt = sb("wdt", [128, NCH, 32], BF16)
    stail = sb("stail", [128, 24, 3, 3])
    sfirst = sb("sfirst", [128, 24, 3])
    sprev = sb("sprev", [128, 24, 3])
    sfx = sb("sfx", [128, 4, 24])
    banks = [ps("bank%d" % i, [128, 512]) for i in range(8)]
    k.bank_i = 0

    def tmpf(i):
        return h[:, 2 * (i - 1):2 * (i - 1) + 2, :].rearrange("p a t -> p (a t)").bitcast(F32)

    def tk(i):
        if i == 0:
            return keys('tmpf0', 0, 0, T)
        return keys('h', 2 * (i - 1), 0, T) + keys('h', 2 * (i - 1) + 1, 0, T)

    def bank():
        b = k.bank_i
        k.bank_i = (b + 1) % 8
        return b

    def load(dst, src, rk=(), wk=()):
        P.dma('sp', lambda e: e.dma_start(out=dst, in_=src), r=rk, w=wk)

    for c in range(NCH):
        P.dma('sp', lambda e, c=c: e.dma_start(out=xd[:, c, :], in_=xT[:, c, :]), w=keys('xd', c, 0, T))
    load(ctmp[:, :, :], cT[:, :, :], wk=['ctmp'])
    load(adab[:, :, :], ada_bT[:, :, :], wk=['adab'])
    load(ng[:, :, :, :], norm_gT[:, :, :, :], wk=['ng'])
    load(lruv[:, :, :, :], lru_vT[:, :, :, :], wk=['lruv'])
    load(ffv[:, :, :, :], ffn_vT[:, :, :, :], wk=['ffv'])
    load(mk[:, :], msk[:, :], wk=['mk'])
    load(cstf[:, :, :], cst[:, :, :], wk=['cstf'])
    load(fbf[:, :], fox_bf[:, :], wk=['fbf'])
    P.dma('sp', lambda e: e.dma_start(out=small[:, 48:50], in_=kcT[:, 0, 0, 0:2]), w=['touch'])
    P.dma('sp', lambda e: e.dma_start(out=small[:, 50:52], in_=vc[0, 0:128, 0:2]), w=['touch'])
    P.dma('sp', lambda e: e.dma_start(out=small[:, 52:54], in_=lfc[:, 0, 0, 0:2]), w=['touch'])
    P.dma('sp', lambda e: e.dma_start(out=small[:, 54:56], in_=tmk[:, 0, 0:2]), w=['touch'])
    P.dma('sp', lambda e: e.dma_start(out=small[:, 56:58], in_=fox_w_qkv[0:128, 0:2]), w=['touch'])
    P.dma('sp', lambda e: e.dma_start(out=small[:, 58:60], in_=fox_w_o[0:128, 0:2]), w=['touch'])
    P.dma('sp', lambda e: e.dma_start(out=small[:, 60:62], in_=fox_w_f[0:128, 0:2]), w=['touch'])
    P.dma('pool', lambda e: e.dma_start(out=t64b[:, :], in_=t64[:, :]), w=['t64b'])
    P.op('dve', lambda e: e.memset(ones1[:, :], 1.0), w=['ones1'])
    P.op('dve', lambda e: e.tensor_scalar(out=nbc[:, 0:4], in0=mk[:, 12:16], scalar1=-1.0, scalar2=30000.0, op0=ALU.add, op1=ALU.mult), r=['mk'], w=['nbc'])
    P.op('dve', lambda e: e.tensor_scalar(out=nbc[:, 4:8], in0=mk[:, 4:8], scalar1=-1.0, scalar2=30000.0, op0=ALU.add, op1=ALU.mult), r=['mk'], w=['nbc'])
    load(scv[:, :, :], ssd_cvT[:, :, :], wk=['scv'])
    load(srep[:, :], ssd_rep[:, :], wk=['srep'])
    load(sng[:, :], ssd_ngT[:, :], wk=['sng'])
    P.dma('pool', lambda e: e.dma_start(out=ident[:, :], in_=cst[:, 0, :]), w=['ident'])
    P.op('act', lambda e: e.activation(out=srep[:, 32:64], in_=srep[:, 32:64], func=AF.Exp), r=['srep'], w=['srep'])
    P.op('dve', lambda e: e.tensor_scalar(out=srep[:, 32:64], in0=srep[:, 32:64], scalar1=-1.0, scalar2=None, op0=ALU.mult), r=['srep'], w=['srep'])
    P.op('dve', lambda e: e.memset(ones[:, :], 1.0 / D), w=['ones'])
    P.op('dve', lambda e: e.memset(small[:, 0:1], EPS), w=['small'])
    P.op('dve', lambda e: e.memset(small[:, 1:2], 1.0), w=['small'])
    P.op('act', lambda e: e.activation(out=ada[:, 0:NCH, :], in_=ctmp[:, :, :], func=AF.Sigmoid), r=['ctmp'], w=['ada'])
    P.op('dve', lambda e: e.tensor_tensor(out=siluc[:, :, :], in0=ada[:, 0:NCH, :], in1=ctmp[:, :, :], op=ALU.mult),
         r=['ada', 'ctmp'], w=['siluc'])
    for j in range(2):
        zz = cfac[:, j, 0, :]
        t1 = cfac[:, j, 1, :]
        P.op('act', lambda e, j=j, zz=zz: e.activation(out=zz, in_=lruv[:, j, 9, :], func=AF.Exp, scale=-1.0), r=['lruv'], w=['cfac'])
        P.op('dve', lambda e, zz=zz, t1=t1: e.tensor_scalar(out=t1, in0=zz, scalar1=-1.0 / 3.0, scalar2=0.5, op0=ALU.mult, op1=ALU.add),
             r=['cfac'], w=['cfac'])
        P.op('dve', lambda e, zz=zz, t1=t1: e.tensor_tensor(out=t1, in0=t1, in1=zz, op=ALU.mult), r=['cfac'], w=['cfac'])
        P.op('dve', lambda e, zz=zz, t1=t1: e.tensor_scalar(out=t1, in0=t1, scalar1=-1.0, scalar2=1.0, op0=ALU.mult, op1=ALU.add),
             r=['cfac'], w=['cfac'])
        P.op('dve', lambda e, zz=zz, t1=t1: e.tensor_tensor(out=t1, in0=t1, in1=zz, op=ALU.mult), r=['cfac'], w=['cfac'])
        P.op('dve', lambda e, zz=zz, t1=t1: e.tensor_scalar(out=zz, in0=t1, scalar1=-8.0, scalar2=None, op0=ALU.mult), r=['cfac'], w=['cfac'])
        P.op('dve', lambda e, zz=zz, t1=t1: e.tensor_scalar(out=t1, in0=t1, scalar1=-16.0, scalar2=None, op0=ALU.mult), r=['cfac'], w=['cfac'])

    k.wi = 0

    def load_panel(wap, kc, c0, ncols):
        i = k.wi
        k.wi = (i + 1) % 2
        t = wpf[i][:, 0:kc * ncols].rearrange("p (q n) -> p q n", n=ncols)
        src = wap.rearrange("(kc p) n -> p kc n", p=128)
        key = ('wp', i)
        for q in range(kc):
            P.dma('pool', lambda e, q=q: e.dma_start(out=t[:, q, :], in_=src[:, q, c0:c0 + ncols]), w=[key])
        return t, key

    def ada_layer(l):
        for pn in range(12):
            t, key = load_panel(ada_w[l], NCH, pn * 512, 512)
            b = bank()
            for o in range(4):
                for q in range(NCH):
                    P.op('pe', lambda e, b=b, o=o, q=q, t=t: e.matmul(banks[b][:, o * 4:o * 4 + 3], lhsT=t[:, q, o * 128:(o + 1) * 128],
                                                                         rhs=siluc[:, q, :], start=(q == 0), stop=(q == NCH - 1)),
                         r=[key, 'siluc'], w=[('bank', b)])
            for o in range(4):
                oc = pn * 4 + o
                P.op('act', lambda e, b=b, o=o, oc=oc: e.activation(out=ada[:, oc, :], in_=banks[b][:, o * 4:o * 4 + 3], func=AF.Identity,
                                                                     bias=adab[:, l, oc:oc + 1]),
                     r=[('bank', b), 'adab'], w=['ada'])
        for kind, (gi, ai, plus1) in enumerate([(0, 1, True), (1, 2, False), (2, 4, True), (3, 5, False)]):
            for s in range(3):
                if plus1:
                    P.op('dve', lambda e, kind=kind, gi=gi, ai=ai, s=s: e.scalar_tensor_tensor(
                        out=gsc[:, kind, :, s], in0=ada[:, ai * 8:(ai + 1) * 8, s], scalar=1.0, in1=ng[:, l, gi, :],
                        op0=ALU.add, op1=ALU.mult), r=['ada', 'ng'], w=['gsc'])
                else:
                    P.op('dve', lambda e, kind=kind, gi=gi, ai=ai, s=s: e.tensor_tensor(
                        out=gsc[:, kind, :, s], in0=ada[:, ai * 8:(ai + 1) * 8, s], in1=ng[:, l, gi, :], op=ALU.mult),
                        r=['ada', 'ng'], w=['gsc'])

    def rstd_block(src_fn, nch, src_keys_fn, lo, n, dim):
        b = bank()
        for c in range(nch):
            if c % 2 == 0:
                P.op('act', lambda e, c=c: e.activation(out=sq[:, c % NSQ, 0:n], in_=src_fn(c), func=AF.Square),
                     r=src_keys_fn(c), w=[('sq', c % NSQ)])
            else:
                P.op('dve', lambda e, c=c: e.tensor_tensor(out=sq[:, c % NSQ, 0:n], in0=src_fn(c), in1=src_fn(c), op=ALU.mult),
                     r=src_keys_fn(c), w=[('sq', c % NSQ)])
            P.op('pe', lambda e, c=c, b=b: e.matmul(banks[b][:, 0:n], lhsT=ones[:, :], rhs=sq[:, c % NSQ, 0:n],
                                                     start=(c == 0), stop=(c == nch - 1)),
                 r=['ones', ('sq', c % NSQ)], w=[('bank', b)])
        P.op('act', lambda e, b=b: e.activation(out=rstd[:, 0:n], in_=banks[b][:, 0:n], func=AF.Sqrt,
                                                 scale=float(D) / dim, bias=small[:, 0:1]),
             r=[('bank', b), 'small'], w=['rstd'])
        P.op('dve', lambda e: e.reciprocal(out=rstd[:, 0:n], in_=rstd[:, 0:n]),
             r=['rstd'], w=['rstd'])

    def seq_pieces(lo, n):
        out = []
        for s, (o, ln) in enumerate(SEQS):
            a, b_ = max(lo, o), min(lo + n, o + ln)
            if a < b_:
                out.append((s, a, b_))
        return out

    def prenorm(kind, shift_idx):
        for (lo, n) in BLKS:
            P.dma('sp', lambda e, lo=lo, n=n: e.dma_start(out=blk16[:, :, 0:n], in_=xd[:, :, lo:lo + n]),
                  r=[kk for c in range(NCH) for kk in keys('xd', c, lo, lo + n)], w=[('blk16', c) for c in range(NCH)])
            rstd_block(lambda c: blk16[:, c, 0:n], NCH, lambda c: [('blk16', c)], lo, n, D)
            for c in range(NCH):
                for (s, a, b_) in seq_pieces(lo, n):
                    P.op('dve', lambda e, c=c, s=s, a=a, b_=b_, lo=lo: e.scalar_tensor_tensor(
                        out=tmpf0[:, a:b_], in0=blk16[:, c, a - lo:b_ - lo], scalar=gsc[:, kind, c, s:s + 1], in1=rstd[:, a - lo:b_ - lo],
                        op0=ALU.mult, op1=ALU.mult),
                        r=[('blk16', c), 'gsc'] + ['rstd'], w=keys('tmpf0', 0, a, b_))
                    P.op('act', lambda e, c=c, s=s, a=a, b_=b_: e.activation(
                        out=h[:, c, a:b_], in_=tmpf0[:, a:b_], func=AF.Identity, bias=ada[:, shift_idx * 8 + c, s:s + 1]),
                        r=keys('tmpf0', 0, a, b_) + ['ada'], w=keys('h', c, a, b_))

    def project(wap, kc, nout, src, srcname, evac, pw=512):
        for c0 in range(0, nout, pw):
            ncols = min(pw, nout - c0)
            t, key = load_panel(wap, kc, c0, ncols)
            for o in range(ncols // 128):
                oc = c0 // 128 + o
                for (lo, n) in BLKS:
                    b = bank()
                    for q in range(kc):
                        P.op('pe', lambda e, b=b, o=o, q=q, t=t, lo=lo, n=n: e.matmul(
                            banks[b][:, 0:n], lhsT=t[:, q, o * 128:(o + 1) * 128], rhs=src[:, q, lo:lo + n],
                            start=(q == 0), stop=(q == kc - 1)),
                            r=[key] + keys(srcname, q, lo, lo + n), w=[('bank', b)])
                    evac(oc, lo, n, b)

    def out_proj_residual(wap, kc, src, srcname, gkind, bias_fn=None, pw=512):
        def ev(oc, lo, n, b):
            if bias_fn is not None:
                P.op('act', lambda e: e.activation(out=h[:, oc, lo:lo + n], in_=banks[b][:, 0:n], func=AF.Identity, bias=bias_fn(oc)),
                     r=[('bank', b), 'lruv'], w=keys('h', oc, lo, lo + n))
            else:
                P.op('act', lambda e: e.activation(out=h[:, oc, lo:lo + n], in_=banks[b][:, 0:n], func=AF.Identity),
                     r=[('bank', b)], w=keys('h', oc, lo, lo + n))
        project(wap, kc, D, src, srcname, ev, pw=pw)
        for (lo, n) in BLKS:
            rstd_block(lambda c: h[:, c, lo:lo + n], NCH, lambda c: keys('h', c, lo, lo + n), lo, n, D)
            for c in range(NCH):
                for (s, a, b_) in seq_pieces(lo, n):
                    P.op('dve', lambda e, c=c, s=s, a=a, b_=b_, lo=lo: e.scalar_tensor_tensor(
                        out=blk16[:, c, a - lo:b_ - lo], in0=h[:, c, a:b_], scalar=gsc[:, gkind, c, s:s + 1],
                        in1=rstd[:, a - lo:b_ - lo], op0=ALU.mult, op1=ALU.mult),
                        r=keys('h', c, a, b_) + ['gsc'] + ['rstd'], w=[('blk16', c)])
            P.dma('pool', lambda e, lo=lo, n=n: e.dma_start(out=xd[:, :, lo:lo + n], in_=blk16[:, :, 0:n], accum_op=ALU.add),
                  r=[('blk16', c) for c in range(NCH)],
                  w=[kk for c in range(NCH) for kk in keys('xd', c, lo, lo + n)])

    k.xi = 0

    def exchange(src_ap, ncols, src_keys, dst_view=None, W=None, gdst=None, gkeys=None):
        i = k.xi
        k.xi += 1
        W = W or XW
        gdst = xg if gdst is None else gdst
        gkeys = ['xg'] if gkeys is None else gkeys
        xin = nc.dram_tensor("xb_in%d" % i, [128, W], F32)
        xout = nc.dram_tensor("xb_out%d" % i, [4 * 128, W], F32)
        dst = xin.ap()[:, 0:ncols]
        if dst_view is not None:
            dst = dst_view(dst)
        P.dma('pool', lambda e: e.dma_start(out=dst, in_=src_ap), r=src_keys, w=['xb_in%d' % i])
        P.coll(lambda e: e.collective_compute("AllGather", ALU.bypass, replica_groups=GROUPS,
                                              ins=[xin.ap().opt()], outs=[xout.ap().opt()]),
               r=['xb_in%d' % i], w=['xb_out%d' % i])
        P.dma('pool', lambda e: e.dma_start(out=gdst[:, :, 0:W], in_=xout.ap().rearrange("(r p) n -> p r n", p=128)),
              r=['xb_out%d' % i], w=gkeys)

    def masked_sum(dst, dkeys, ncols, mcol0):
        P.op('dve', lambda e: e.tensor_scalar(out=dst, in0=xg[:, 0, 0:ncols], scalar1=mk[:, mcol0:mcol0 + 1], scalar2=None, op0=ALU.mult),
             r=['xg', 'mk'], w=dkeys)
        for r_ in range(1, 4):
            P.op('dve', lambda e, r_=r_: e.scalar_tensor_tensor(out=dst, in0=xg[:, r_, 0:ncols], scalar=mk[:, mcol0 + r_:mcol0 + r_ + 1],
                                                               in1=dst, op0=ALU.mult, op1=ALU.add),
                 r=['xg', 'mk'] + dkeys, w=dkeys)

    def lru_layer(l, j):
        V = lambda idx, c: lruv[:, j, idx, c:c + 1]

        def evac_x(oc, lo, n, b):
            for (s, a, b_) in seq_pieces(lo, n):
                off = HOFF[s] - SEQS[s][0]
                P.op('act', lambda e, a=a, b_=b_, off=off: e.activation(out=bufB[:, oc, a + off:b_ + off], in_=banks[b][:, a - lo:b_ - lo],
                                                                         func=AF.Identity, bias=V(0, oc)),
                     r=[('bank', b), 'lruv'], w=[('bufB', oc)])
                if b_ == SEQS[s][0] + SEQS[s][1]:
                    P.op('act', lambda e, b_=b_, s=s: e.activation(out=tails[:, oc, s, :], in_=banks[b][:, b_ - 3 - lo:b_ - lo],
                                                                  func=AF.Identity, bias=V(0, oc)),
                         r=[('bank', b), 'lruv'], w=['tails'])
        project(lru_w_x[j], NCH, D, h, 'h', evac_x)

        def evac_y(oc, lo, n, b):
            P.op('act', lambda e: e.activation(out=bufA[:, oc, lo:lo + n], in_=banks[b][:, 0:n], func=AF.Gelu, bias=V(1, oc)),
                 r=[('bank', b), 'lruv'], w=keys('bufA', oc, lo, lo + n))
        project(lru_w_y[j], NCH, D, h, 'h', evac_y)
        P.dma('sp', lambda e: e.dma_start(out=o_lconv[:, j, :, :, :], in_=tails[:, :, :, :]), r=['tails'], w=['o_lconv%d' % j])
        P.dma('sp', lambda e: e.dma_start(out=tmpf0[:, 0:NCH * 6], in_=st_lconv[:, j, :, :, :].rearrange("p c s k -> p (c s k)")),
              w=tk(0))
        for c in range(NCH):
            for s in (1, 2):
                P.op('dve', lambda e, c=c, s=s: e.tensor_copy(out=bufB[:, c, HOFF[s] - 3:HOFF[s]],
                                                               in_=tmpf0[:, c * 6 + (s - 1) * 3:c * 6 + (s - 1) * 3 + 3]),
                     r=tk(0), w=[('bufB', c)])
        exchange(tails[:, :, 0, :], 24, ['tails'], dst_view=lambda d: d.rearrange('p (c k) -> p c k', k=3))
        masked_sum(small[:, 32:56], ['pt'], 24, 0)
        for c in range(NCH):
            P.op('dve', lambda e, c=c: e.tensor_copy(out=bufB[:, c, 0:3], in_=small[:, 32 + c * 3:32 + c * 3 + 3]),
                 r=['pt'], w=[('bufB', c)])
        P.dma('sp', lambda e: e.dma_start(out=hin[:, :, 1:3], in_=st_lh[:, j, :, :]), w=['hin'])
        P.op('dve', lambda e: e.memset(hin[:, :, 0:1], 0.0), w=['hin'])
        u, rr, ig, aa, sc_ = tmpf0[:, 0:T], tmpf(1), tmpf(2), tmpf(3), tmpf(4)
        zeros = sq[:, :, :].rearrange('p c n -> p (c n)')
        ZK = [('sq', i_) for i_ in range(NSQ)]
        P.op('dve', lambda e: e.memset(zeros[:, :], 0.0), w=ZK)
        summ = small
        for c in range(NCH):
            wt = wsm[c % 2]
            wkey = ('wsm', c % 2)
            P.dma('pool', lambda e, c=c, wt=wt: e.dma_start(out=wt[:, 0, :], in_=lru_w_a[j, c]), w=[wkey])
            P.dma('pool', lambda e, c=c, wt=wt: e.dma_start(out=wt[:, 1, :], in_=lru_w_i[j, c]), w=[wkey])
            for (s, (o, ln)) in enumerate(SEQS):
                ho = HOFF[s]
                P.op('dve', lambda e, c=c, o=o, ln=ln, ho=ho: e.tensor_scalar(
                    out=u[:, o:o + ln], in0=bufB[:, c, ho - 3:ho - 3 + ln], scalar1=V(2, c), scalar2=V(6, c), op0=ALU.mult, op1=ALU.add),
                    r=[('bufB', c), 'lruv'], w=tk(0))
                for kk in (1, 2, 3):
                    P.op('dve', lambda e, c=c, o=o, ln=ln, ho=ho, kk=kk: e.scalar_tensor_tensor(
                        out=u[:, o:o + ln], in0=bufB[:, c, ho - 3 + kk:ho - 3 + kk + ln], scalar=V(2 + kk, c), in1=u[:, o:o + ln],
                        op0=ALU.mult, op1=ALU.add),
                        r=[('bufB', c), 'lruv'] + tk(0), w=tk(0))
            P.op('act', lambda e: e.activation(out=tmpb[:, :], in_=u[:, :], func=AF.Identity), r=tk(0), w=['tmpb'])
            for (lo, n) in BLKS:
                for gi, (dst, di, bidx) in enumerate([(rr, 1, 7), (ig, 2, 8)]):
                    b = bank()
                    P.op('pe', lambda e, b=b, gi=gi, wt=wt, lo=lo, n=n: e.matmul(banks[b][:, 0:n], lhsT=wt[:, gi, :], rhs=tmpb[:, lo:lo + n],
                                                                                 start=True, stop=True),
                         r=[wkey, 'tmpb'], w=[('bank', b)])
                    P.op('act', lambda e, b=b, dst=dst, bidx=bidx, c=c, lo=lo, n=n: e.activation(
                        out=dst[:, lo:lo + n], in_=banks[b][:, 0:n], func=AF.Sigmoid, bias=V(bidx, c)),
                        r=[('bank', b), 'lruv'], w=tk(di))
            P.op('act', lambda e, c=c: e.activation(out=aa[:, :], in_=rr[:, :], func=AF.Exp, scale=cfac[:, j, 1, c:c + 1]),
                 r=tk(1) + ['cfac'], w=tk(3))
            P.op('act', lambda e: e.activation(out=aa[:, :], in_=aa[:, :], func=AF.Sqrt, scale=-1.0, bias=small[:, 1:2]),
                 r=tk(3) + ['small'], w=tk(3))
            P.op('dve', lambda e: e.tensor_tensor(out=ig[:, :], in0=ig[:, :], in1=u[:, :], op=ALU.mult), r=tk(2) + tk(0), w=tk(2))
            P.op('dve', lambda e: e.tensor_tensor(out=ig[:, :], in0=ig[:, :], in1=aa[:, :], op=ALU.mult), r=tk(2) + tk(3), w=tk(2))
            P.op('act', lambda e, c=c: e.activation(out=aa[:, :], in_=rr[:, :], func=AF.Exp, scale=cfac[:, j, 0, c:c + 1]),
                 r=tk(1) + ['cfac'], w=tk(3))
            P.op('dve', lambda e, c=c: e.tensor_tensor_scan(out=sc_[:, 0:TP], data0=aa[:, 0:TP], data1=zeros[:, :], initial=1.0,
                                                            op0=ALU.mult, op1=ALU.add), r=tk(3) + ZK, w=tk(4))
            P.op('dve', lambda e, c=c: e.tensor_copy(out=summ[:, 8 + c:9 + c], in_=sc_[:, TP - 1:TP]), r=tk(4), w=['summ'])
            P.op('dve', lambda e, c=c: e.tensor_tensor(out=bufB[:, c, 0:TP], in0=bufA[:, c, 0:TP], in1=sc_[:, 0:TP], op=ALU.mult),
                 r=keys('bufA', c, 0, TP) + tk(4), w=[('bufB', c)])
            for (s, (o, ln)) in enumerate(SEQS):
                P.op('dve', lambda e, c=c, s=s, o=o, ln=ln: e.tensor_tensor_scan(
                    out=sc_[:, o:o + ln], data0=aa[:, o:o + ln], data1=ig[:, o:o + ln], initial=hin[:, c, s:s + 1],
                    op0=ALU.mult, op1=ALU.add), r=tk(3) + tk(2) + ['hin'], w=tk(4))
            for (s, (o, ln)) in enumerate(SEQS):
                P.op('dve', lambda e, c=c, s=s, o=o, ln=ln: e.tensor_copy(out=hl[:, c, s:s + 1], in_=sc_[:, o + ln - 1:o + ln]),
                     r=tk(4), w=['hl'])
            P.op('dve', lambda e, c=c: e.tensor_tensor(out=bufA[:, c, :], in0=bufA[:, c, :], in1=sc_[:, :], op=ALU.mult),
                 r=keys('bufA', c, 0, T) + tk(4), w=keys('bufA', c, 0, T))
        P.op('dve', lambda e: e.tensor_copy(out=summ[:, 16:24], in_=hl[:, :, 0]), r=['hl'], w=['summ'])
        exchange(summ[:, 8:24], 16, ['summ'])
        hp = small[:, 32:48]
        P.op('dve', lambda e: e.memset(hp[:, 0:8], 0.0), w=['pt'])
        for r_ in range(4):
            P.op('dve', lambda e, r_=r_: e.tensor_tensor(out=hp[:, 8:16], in0=hp[:, 0:8], in1=xg[:, r_, 0:8], op=ALU.mult),
                 r=['xg', 'pt'], w=['pt'])
            P.op('dve', lambda e, r_=r_: e.tensor_tensor(out=hp[:, 8:16], in0=hp[:, 8:16], in1=xg[:, r_, 8:16], op=ALU.add),
                 r=['xg', 'pt'], w=['pt'])
            P.op('dve', lambda e, r_=r_: e.tensor_tensor(out=hp[:, 8:16], in0=hp[:, 8:16], in1=hp[:, 0:8], op=ALU.subtract),
                 r=['pt'], w=['pt'])
            P.op('dve', lambda e, r_=r_: e.scalar_tensor_tensor(out=hp[:, 0:8], in0=hp[:, 8:16], scalar=mk[:, 4 + r_:5 + r_], in1=hp[:, 0:8],
                                                               op0=ALU.mult, op1=ALU.add), r=['pt', 'mk'], w=['pt'])
        P.op('dve', lambda e: e.tensor_tensor(out=hp[:, 8:16], in0=hp[:, 0:8], in1=summ[:, 8:16], op=ALU.mult), r=['pt', 'summ'], w=['pt'])
        P.op('dve', lambda e: e.tensor_tensor(out=hl[:, :, 0], in0=hl[:, :, 0], in1=hp[:, 8:16], op=ALU.add), r=['pt', 'hl'], w=['hl'])
        P.dma('sp', lambda e: e.dma_start(out=o_lh[:, j, :, :], in_=hl[:, :, :]), r=['hl'], w=['o_lh%d' % j])
        for c in range(NCH):
            P.op('dve', lambda e, c=c: e.scalar_tensor_tensor(out=bufA[:, c, 0:TP], in0=bufB[:, c, 0:TP], scalar=hp[:, c:c + 1],
                                                             in1=bufA[:, c, 0:TP], op0=ALU.mult, op1=ALU.add),
                 r=[('bufB', c), 'pt'] + keys('bufA', c, 0, TP), w=keys('bufA', c, 0, TP))
        out_proj_residual(lru_w_o[j], NCH, bufA, 'bufA', 1, bias_fn=lambda oc: V(10, oc))


    def exchange_big(src_ap, ncols, src_keys, name):
        xin = nc.dram_tensor("xB_in_" + name, [128, ncols], F32)
        xout = nc.dram_tensor("xB_out_" + name, [4 * 128, ncols], F32)
        P.dma('pool', lambda e: e.dma_start(out=xin.ap()[:, :], in_=src_ap), r=src_keys, w=['xB_in_' + name])
        import os
        if not os.environ.get('SSD_NOCOLL'):
            P.coll(lambda e: e.collective_compute("AllGather", ALU.bypass, replica_groups=GROUPS,
                                                  ins=[xin.ap().opt()], outs=[xout.ap().opt()]),
                   r=['xB_in_' + name], w=['xB_out_' + name])
        return xout.ap(), 'xB_out_' + name

    def ssd_layer(l):
        NCK = T // 64
        hflat = h[:, :, :].rearrange("p c t -> p (c t)")
        bflat = blk16[:, :, :].rearrange("p c n -> p (c n)")
        BKALL = [('blk16', c) for c in range(NCH)]
        zst = bflat[:, 0:256].bitcast(BF16)
        zsg = bflat[:, 256:768]
        for cg in range(4):
            t, key = load_panel(ssd_w_in, NCH, cg * 512, 512)
            for tt in range(T // 128):
                b = bank()
                for q in range(NCH):
                    P.op('pe', lambda e: e.matmul(banks[b][:, 0:512], lhsT=h[:, q, tt * 128:(tt + 1) * 128], rhs=t[:, q, :],
                                                  start=(q == 0), stop=(q == NCH - 1)),
                         r=[key] + keys('h', q, tt * 128, tt * 128 + 128), w=[('bank', b)])
                P.op('act', lambda e: e.activation(out=zsg, in_=banks[b][:, 0:512], func=AF.Sigmoid), r=[('bank', b)], w=['zsg'])
                P.op('dve', lambda e: e.tensor_tensor(out=zst, in0=zsg, in1=banks[b][:, 0:512], op=ALU.mult), r=[('bank', b), 'zsg'], w=['zst'])
                P.dma('sp', lambda e: e.dma_start(out=zd[tt * 128:(tt + 1) * 128, cg * 512:(cg + 1) * 512], in_=zst), r=['zst'], w=['zd'])
        P.dma('pool', lambda e: e.dma_start(out=wdt[:, :, :], in_=ssd_w_in.rearrange("(kc p) n -> p kc n", p=128)[:, :, 5120:5152]), w=['wdt'])
        dttm = tmpb[:, 0:T].bitcast(F32)
        dtv = dttm.rearrange("p (c e) -> p c e", e=32)
        for c0 in range(0, NCK, 16):
            ncq = min(16, NCK - c0)
            b = bank()
            for ci in range(c0, c0 + ncq):
                for q in range(NCH):
                    P.op('pe', lambda e: e.matmul(banks[b][0:64, (ci - c0) * 32:(ci - c0 + 1) * 32], lhsT=h[:, q, ci * 64:(ci + 1) * 64],
                                                  rhs=wdt[:, q, :], start=(q == 0), stop=(q == NCH - 1)),
                         r=['wdt'] + keys('h', q, ci * 64, ci * 64 + 64), w=[('bank', b)])
            P.op('dve', lambda e: e.tensor_tensor(out=dtv[0:64, c0:c0 + ncq, :], in0=banks[b][0:64, 0:ncq * 32].rearrange("p (c e) -> p c e", e=32),
                                                  in1=srep[0:64, 0:32].unsqueeze(1).to_broadcast([64, ncq, 32]), op=ALU.add),
                 r=[('bank', b), 'srep'], w=['dttm'])
        sp1 = bflat[0:64, 0:1088]
        sp2 = bflat[0:64, 1088:2176]
        P.op('act', lambda e: e.activation(out=sp1, in_=dttm[0:64, :], func=AF.Abs), r=['dttm', 'zsg', 'zst'], w=['sp1'])
        P.op('act', lambda e: e.activation(out=sp1, in_=sp1, func=AF.Exp, scale=-1.0), r=['sp1'], w=['sp1'])
        P.op('act', lambda e: e.activation(out=sp1, in_=sp1, func=AF.Ln, bias=small[0:64, 1:2]), r=['sp1', 'small'], w=['sp1'])
        P.op('dve', lambda e: e.tensor_scalar_max(out=sp2, in0=dttm[0:64, :], scalar1=0.0), r=['dttm'], w=['sp2'])
        P.op('dve', lambda e: e.tensor_tensor(out=dttm[0:64, :], in0=sp1, in1=sp2, op=ALU.add), r=['sp1', 'sp2'], w=['dttm'])
        P.fence()
        pre = tmpf0[:, 0:HW]
        acc = bflat[:, 0:T]
        sig = bflat[:, T:2 * T] if 2 * T <= 4096 else None
        sig = rstd
        SV = lambda idx, oc: scv[:, idx, oc:oc + 1]
        sst = sfx
        shal = bflat[:, 2 * 1088:2 * 1088 + 144]
        P.dma('sp', lambda e: e.dma_start(out=shal, in_=st_sconv[:, :, :, :].rearrange("p c s k -> p (c s k)")), w=['shal'])

        def evac_xbc(oc, lo, n, b):
            for (s_, a, b_) in seq_pieces(lo, n):
                off = HOFF[s_] - SEQS[s_][0]
                P.op('act', lambda e: e.activation(out=pre[:, a + off:b_ + off], in_=banks[b][:, a - lo:b_ - lo], func=AF.Identity),
                     r=[('bank', b)], w=['pre'])
            if lo + n == T:
                for (s_, (o, ln)) in enumerate(SEQS):
                    ho = HOFF[s_]
                    P.op('dve', lambda e: e.tensor_copy(out=stail[:, oc, s_, :], in_=pre[:, ho + ln - 3:ho + ln]), r=['pre'], w=['stail'])
                    if s_ == 0:
                        P.op('dve', lambda e: e.tensor_copy(out=sfirst[:, oc, :], in_=pre[:, ho:ho + 3]), r=['pre'], w=['sfirst'])
                        P.op('dve', lambda e: e.memset(pre[:, 0:3], 0.0), r=['pre'], w=['pre'])
                    else:
                        P.op('dve', lambda e: e.tensor_copy(out=pre[:, ho - 3:ho], in_=shal[:, oc * 6 + (s_ - 1) * 3:oc * 6 + (s_ - 1) * 3 + 3]),
                             r=['pre', 'shal'], w=['pre'])
                for (s_, (o, ln)) in enumerate(SEQS):
                    ho = HOFF[s_]
                    P.op('dve', lambda e: e.tensor_scalar(out=acc[:, o:o + ln], in0=pre[:, ho - 3:ho - 3 + ln], scalar1=SV(0, oc), scalar2=SV(4, oc),
                                                          op0=ALU.mult, op1=ALU.add), r=['pre', 'scv'], w=['acc'])
                    for kk in (1, 2, 3):
                        P.op('dve', lambda e: e.scalar_tensor_tensor(out=acc[:, o:o + ln], in0=pre[:, ho - 3 + kk:ho - 3 + kk + ln], scalar=SV(kk, oc),
                                                                     in1=acc[:, o:o + ln], op0=ALU.mult, op1=ALU.add),
                             r=['pre', 'scv', 'acc'], w=['acc'])
                sgt = hflat[:, 0:2 * T].bitcast(F32)
                P.op('act', lambda e: e.activation(out=pre[:, 0:T], in_=acc, func=AF.Sigmoid), r=['acc', 'pre'], w=['pre'])
                P.op('dve', lambda e: e.tensor_tensor(out=xbc[:, oc, :], in0=pre[:, 0:T], in1=acc, op=ALU.mult), r=['pre', 'acc'],
                     w=keys('xbc', oc, 0, T))
        project(ssd_w_in[:, 2048:5120], NCH, 3072, h, 'h', evac_xbc)
        P.dma('sp', lambda e: e.dma_start(out=o_sconv[:, :, :, :], in_=stail[:, :, :, :]), r=['stail'], w=['o_sconv'])
        exchange(stail[:, :, 0, :], 72, ['stail'], dst_view=lambda d: d.rearrange('p (c k) -> p c k', k=3))
        masked_sum(sprev[:, :, :].rearrange("p c k -> p (c k)"), ['sprev'], 72, 0)
        Wk = lambda kk: scv[:, kk, :]
        U = lambda i: sfx[:, i, :]
        tt_ = sfx[:, 3, :]
        def TT2(out, a, b_, op):
            P.op('dve', lambda e: e.tensor_tensor(out=out, in0=a, in1=b_, op=op), r=['sfx', 'sprev', 'sfirst', 'scv'], w=['sfx'])
        terms = [[(0, ('p', 0)), (1, ('p', 1)), (2, ('p', 2)), (3, ('x', 0))],
                 [(0, ('p', 1)), (1, ('p', 2)), (2, ('x', 0)), (3, ('x', 1))],
                 [(0, ('p', 2)), (1, ('x', 0)), (2, ('x', 1)), (3, ('x', 2))]]
        for ti in range(3):
            for n_, (kk, (src, idx)) in enumerate(terms[ti]):
                v = sprev[:, :, idx] if src == 'p' else sfirst[:, :, idx]
                if n_ == 0:
                    TT2(U(ti), Wk(kk), v, ALU.mult)
                    TT2(U(ti), U(ti), scv[:, 4, :], ALU.add)
                else:
                    TT2(tt_, Wk(kk), v, ALU.mult)
                    TT2(U(ti), U(ti), tt_, ALU.add)
            P.op('act', lambda e: e.activation(out=tt_, in_=U(ti), func=AF.Sigmoid), r=['sfx'], w=['sfx'])
            P.op('dve', lambda e: e.tensor_tensor(out=xbc[:, :, ti], in0=U(ti), in1=tt_, op=ALU.mult), r=['sfx'],
                 w=[kk_ for c in range(24) for kk_ in keys('xbc', c, 0, 3)])
        P.fence()
        ST = hflat[:, 0:4096].bitcast(F32)
        ytm = hflat[:, 4096:8192].bitcast(F32)
        xtm = hflat[:, 8192:10240]
        xdt = hflat[:, 10240:12288]
        xdd = hflat[:, 12288:14336]
        Sb = hflat[:, 14336:16384]
        zch = bflat[:, 0:1024].bitcast(BF16)
        Lt = bflat[:, 1024:1536]
        Mt = bflat[:, 1536:1792].bitcast(BF16)
        Btm = bflat[:, 1792:2048].bitcast(BF16)
        mG = bflat[:, 2048:2112]
        dgx = bflat[:, 2112:2624]
        sm = bflat[:, 2624:2944]
        dA, cum, dend, ecum = sm[:, 0:32], sm[:, 32:64], sm[:, 64:96], sm[:, 96:128]
        tot, cd, totall, rs = sm[:, 128:160], sm[:, 160:192], sm[:, 192:224], sm[:, 224:226]
        wn = xdt
        tpb = [banks[i][:, :].bitcast(BF16) for i in range(8)]
        Sx = tmpf0[:, 0:2080]
        idf, trif, onef = cstf[:, 0, :], cstf[:, 1, :], cstf[:, 2, :]
        br3 = lambda ap_, n0, n1, n2: ap_.unsqueeze(2).to_broadcast([n0, n1, n2])

        def chunk(ci, full):
            t0 = ci * 64
            g8 = lambda ap_, g: ap_[0:64, g * 8:(g + 1) * 8]
            P.op('dve', lambda e: e.tensor_tensor(out=dA[0:64, :], in0=dtv[0:64, ci, :], in1=srep[0:64, 32:64], op=ALU.mult),
                 r=['dttm', 'srep'], w=['dA'])
            b = bank()
            P.op('pe', lambda e: e.matmul(banks[b][0:64, 0:32], lhsT=trif[0:64, 0:64], rhs=dA[0:64, :], start=True, stop=True),
                 r=['cstf', 'dA'], w=[('bank', b)])
            P.op('pe', lambda e: e.matmul(banks[b][:, 32:64], lhsT=onef[0:64, 0:128], rhs=dA[0:64, :], start=True, stop=True),
                 r=['cstf', 'dA'], w=[('bank', b)])
            P.op('act', lambda e: e.activation(out=cum[0:64, :], in_=banks[b][0:64, 0:32], func=AF.Identity), r=[('bank', b)], w=['cum'])
            P.op('act', lambda e: e.activation(out=tot[:, :], in_=banks[b][:, 32:64], func=AF.Identity), r=[('bank', b)], w=['tot'])
            P.op('dve', lambda e: e.tensor_tensor(out=dend[0:64, :], in0=tot[0:64, :], in1=cum[0:64, :], op=ALU.subtract), r=['tot', 'cum'], w=['dend'])
            P.op('act', lambda e: e.activation(out=dend[0:64, :], in_=dend[0:64, :], func=AF.Exp), r=['dend'], w=['dend'])
            P.op('act', lambda e: e.activation(out=cd[:, :], in_=tot[:, :], func=AF.Exp), r=['tot'], w=['cd'])
            if full:
                P.op('act', lambda e: e.activation(out=ecum[0:64, :], in_=cum[0:64, :], func=AF.Exp), r=['cum'], w=['ecum'])
            else:
                P.op('dve', lambda e: e.tensor_tensor(out=totall[:, :], in0=totall[:, :], in1=tot[:, :], op=ALU.add), r=['tot', 'totall'], w=['totall'])
            for half in range(2):
                b = bank()
                for i in range(8):
                    xc = half * 8 + i
                    P.op('pe', lambda e: e.transpose(tpb[b][0:64, i * 128:(i + 1) * 128], xbc[:, xc, t0:t0 + 64], ident[:, :]),
                         r=keys('xbc', xc, t0, t0 + 64) + ['ident'], w=[('bank', b)])
                P.op('act', lambda e: e.activation(out=xtm[0:64, half * 1024:(half + 1) * 1024], in_=tpb[b][0:64, 0:1024], func=AF.Identity),
                     r=[('bank', b)], w=['xtm'])
            b = bank()
            for g in range(4):
                P.op('pe', lambda e: e.transpose(tpb[b][0:64, g * 128:(g + 1) * 128], xbc[:, 16 + g, t0:t0 + 64], ident[:, :]),
                     r=keys('xbc', 16 + g, t0, t0 + 64) + ['ident'], w=[('bank', b)])
            P.op('act', lambda e: e.activation(out=Btm[0:64, :], in_=tpb[b][0:64, 0:512], func=AF.Identity), r=[('bank', b)], w=['Btm'])
            x3 = lambda ap_: ap_[0:64, :].rearrange("p (e q) -> p e q", q=64)
            P.op('dve', lambda e: e.tensor_tensor(out=x3(xdt), in0=x3(xtm), in1=br3(dtv[0:64, ci, :], 64, 32, 64), op=ALU.mult),
                 r=['xtm', 'dttm'], w=['xdt'])
            P.op('dve', lambda e: e.tensor_tensor(out=x3(xdd), in0=x3(xdt), in1=br3(dend[0:64, :], 64, 32, 64), op=ALU.mult),
                 r=['xdt', 'dend'], w=['xdd'])
            for g in range(4):
                gs = slice(g * 512, (g + 1) * 512)
                if full:
                    bg = bank()
                    P.op('pe', lambda e: e.matmul(banks[bg][0:64, 0:64], lhsT=xbc[:, 16 + g, t0:t0 + 64], rhs=xbc[:, 20 + g, t0:t0 + 64],
                                                  start=True, stop=True),
                         r=keys('xbc', 16 + g, t0, t0 + 64) + keys('xbc', 20 + g, t0, t0 + 64), w=[('bank', bg)])
                    P.op('dve', lambda e: e.tensor_tensor(out=mG[0:64, :], in0=banks[bg][0:64, 0:64], in1=trif[0:64, 0:64], op=ALU.mult),
                         r=[('bank', bg), 'cstf'], w=['mG'])
                    d3 = dgx[0:64, :].rearrange("p (e q) -> p e q", q=64)
                    P.op('dve', lambda e: e.tensor_tensor(out=d3, in0=idf[0:64, 0:64].unsqueeze(1).to_broadcast([64, 8, 64]),
                                                          in1=br3(g8(cum, g), 64, 8, 64), op=ALU.mult), r=['cstf', 'cum'], w=['dgx'])
                    bc = bank()
                    P.op('pe', lambda e: e.matmul(banks[bc][0:64, 0:512], lhsT=onef[0:64, 0:64], rhs=dgx[0:64, :], start=True, stop=True),
                         r=['cstf', 'dgx'], w=[('bank', bc)])
                    L3 = Lt[0:64, :].rearrange("p (e q) -> p e q", q=64)
                    P.op('dve', lambda e: e.tensor_tensor(out=L3, in0=banks[bc][0:64, 0:512].rearrange("p (e q) -> p e q", q=64),
                                                          in1=br3(g8(cum, g), 64, 8, 64), op=ALU.subtract), r=[('bank', bc), 'cum'], w=['Lt'])
                    P.op('dve', lambda e: e.tensor_scalar_min(out=Lt[0:64, :], in0=Lt[0:64, :], scalar1=0.0), r=['Lt'], w=['Lt'])
                    P.op('act', lambda e: e.activation(out=Lt[0:64, :], in_=Lt[0:64, :], func=AF.Exp), r=['Lt'], w=['Lt'])
                    P.op('dve', lambda e: e.tensor_tensor(out=Mt[0:64, :].rearrange("p (e q) -> p e q", q=64), in0=L3,
                                                          in1=mG[0:64, :].unsqueeze(1).to_broadcast([64, 8, 64]), op=ALU.mult),
                         r=['Lt', 'mG'], w=['Mt'])
                    byd = bank()
                    for e_ in range(8):
                        P.op('pe', lambda e: e.matmul(banks[byd][0:64, e_ * 64:(e_ + 1) * 64], lhsT=Mt[0:64, e_ * 64:(e_ + 1) * 64],
                                                      rhs=xdt[0:64, g * 512 + e_ * 64:g * 512 + (e_ + 1) * 64], start=True, stop=True),
                             r=['Mt', 'xdt'], w=[('bank', byd)])
                    byo = bank()
                    P.op('pe', lambda e: e.matmul(banks[byo][0:64, 0:512], lhsT=xbc[:, 20 + g, t0:t0 + 64], rhs=Sb[:, gs], start=True, stop=True),
                         r=keys('xbc', 20 + g, t0, t0 + 64) + ['Sb'], w=[('bank', byo)])
                    y3 = ytm[0:64, gs].rearrange("p (e q) -> p e q", q=64)
                    P.op('dve', lambda e: e.tensor_tensor(out=y3, in0=banks[byo][0:64, 0:512].rearrange("p (e q) -> p e q", q=64),
                                                          in1=br3(g8(ecum, g), 64, 8, 64), op=ALU.mult), r=[('bank', byo), 'ecum'], w=['ytm'])
                    P.op('dve', lambda e: e.tensor_tensor(out=ytm[0:64, gs], in0=ytm[0:64, gs], in1=banks[byd][0:64, 0:512], op=ALU.add),
                         r=[('bank', byd), 'ytm'], w=['ytm'])
                bs = bank()
                P.op('pe', lambda e: e.matmul(banks[bs][:, 0:512], lhsT=Btm[0:64, g * 128:(g + 1) * 128], rhs=xdd[0:64, gs], start=True, stop=True),
                     r=['Btm', 'xdd'], w=[('bank', bs)])
                S3 = ST[:, gs].rearrange("p (e q) -> p e q", q=64)
                P.op('dve', lambda e: e.tensor_tensor(out=S3, in0=S3, in1=br3(cd[:, g * 8:(g + 1) * 8], 128, 8, 64), op=ALU.mult),
                     r=['ST', 'cd', 'Sb'], w=['ST'])
                P.op('dve', lambda e: e.tensor_tensor(out=ST[:, gs], in0=ST[:, gs], in1=banks[bs][:, 0:512], op=ALU.add),
                     r=['ST', ('bank', bs)], w=['ST'])
            P.op('act', lambda e: e.activation(out=Sb[:, :], in_=ST[:, :], func=AF.Identity), r=['ST', 'Sb'], w=['Sb'])
            if full:
                P.dma('sp', lambda e: e.dma_start(out=zch[0:64, :], in_=zd[t0:t0 + 64, :]), r=['zd', 'zch'], w=['zch'])
                P.op('dve', lambda e: e.tensor_tensor(out=x3(xdd), in0=x3(xtm), in1=br3(srep[0:64, 64:96], 64, 32, 64), op=ALU.mult),
                     r=['xtm', 'srep', 'xdd'], w=['xdd'])
                P.op('dve', lambda e: e.tensor_tensor(out=ytm[0:64, :], in0=ytm[0:64, :], in1=xdd[0:64, :], op=ALU.add), r=['ytm', 'xdd'], w=['ytm'])
                P.op('dve', lambda e: e.tensor_tensor(out=ytm[0:64, :], in0=ytm[0:64, :], in1=zch[0:64, :], op=ALU.mult), r=['ytm', 'zch'], w=['ytm'])
                P.op('act', lambda e: e.activation(out=xdd[0:64, :], in_=ytm[0:64, :], func=AF.Square, accum_out=rs[0:64, 0:1]),
                     r=['ytm', 'xdd'], w=['xdd', 'rs'])
                P.op('act', lambda e: e.activation(out=rs[0:64, 0:1], in_=rs[0:64, 0:1], func=AF.Sqrt, scale=1.0 / 2048.0, bias=small[0:64, 0:1]),
                     r=['rs', 'small'], w=['rs'])
                P.op('dve', lambda e: e.reciprocal(out=rs[0:64, 0:1], in_=rs[0:64, 0:1]), r=['rs'], w=['rs'])
                P.op('dve', lambda e: e.tensor_scalar(out=wn[0:64, :], in0=ytm[0:64, :], scalar1=rs[0:64, 0:1], scalar2=None, op0=ALU.mult),
                     r=['ytm', 'rs', 'xdt'], w=['xdt'])
                for half in range(2):
                    b = bank()
                    for i in range(8):
                        xc = half * 8 + i
                        P.op('pe', lambda e: e.transpose(tpb[b][:, i * 64:(i + 1) * 64], wn[0:64, xc * 128:(xc + 1) * 128], ident[0:64, 0:64]),
                             r=['xdt', 'ident'], w=[('bank', b)])
                    for i in range(8):
                        xc = half * 8 + i
                        P.op('act', lambda e: e.activation(out=xbc[:, xc, t0:t0 + 64], in_=tpb[b][:, i * 64:(i + 1) * 64], func=AF.Identity,
                                                           scale=sng[:, xc:xc + 1]),
                             r=[('bank', b), 'sng'], w=keys('xbc', xc, t0, t0 + 64))

        P.op('dve', lambda e: e.memset(ST[:, :], 0.0), w=['ST'])
        P.op('dve', lambda e: e.memset(totall[:, :], 0.0), w=['totall'])
        P.op('act', lambda e: e.activation(out=Sb[:, :], in_=ST[:, :], func=AF.Identity), r=['ST'], w=['Sb'])
        for ci in range(TP // 64):
            chunk(ci, False)
        P.op('dve', lambda e: e.tensor_copy(out=Sx[:, 0:2048], in_=ST[:, :]), r=['ST'], w=['Sx'])
        gath, gkey = exchange_big(Sx[:, 0:2048], 2048, ['Sx'], 'ssd')
        exchange(totall[:, :], 32, ['totall'])
        Sacc = ytm
        P.op('dve', lambda e: e.memset(Sacc[:, :], 0.0), w=['ytm'])
        for r_ in range(4):
            P.dma('sp', lambda e: e.dma_start(out=Sx[:, 0:2048], in_=gath[r_ * 128:(r_ + 1) * 128, :]), r=[gkey, 'Sx'], w=['Sx'])
            P.op('act', lambda e: e.activation(out=cd[:, :], in_=xg[:, r_, 0:32], func=AF.Exp), r=['xg'], w=['cd'])
            S3 = ST[:, :].rearrange("p (e q) -> p e q", q=64)
            A3 = Sacc[:, :].rearrange("p (e q) -> p e q", q=64)
            P.op('dve', lambda e: e.tensor_tensor(out=S3, in0=A3, in1=br3(cd[:, :], 128, 32, 64), op=ALU.mult), r=['ytm', 'cd', 'ST'], w=['ST'])
            P.op('dve', lambda e: e.tensor_tensor(out=ST[:, :], in0=ST[:, :], in1=Sx[:, 0:2048], op=ALU.add), r=['ST', 'Sx'], w=['ST'])
            P.op('dve', lambda e: e.tensor_tensor(out=ST[:, :], in0=ST[:, :], in1=Sacc[:, :], op=ALU.subtract), r=['ST', 'ytm'], w=['ST'])
            P.op('dve', lambda e: e.scalar_tensor_tensor(out=Sacc[:, :], in0=ST[:, :], scalar=mk[:, 4 + r_:5 + r_], in1=Sacc[:, :],
                                                         op0=ALU.mult, op1=ALU.add), r=['ST', 'mk', 'ytm'], w=['ytm'])
        P.op('dve', lambda e: e.tensor_copy(out=ST[:, :], in_=Sacc[:, :]), r=['ytm'], w=['ST'])
        P.op('act', lambda e: e.activation(out=Sb[:, :], in_=ST[:, :], func=AF.Identity), r=['ST'], w=['Sb'])
        for ci in range(TP // 64):
            chunk(ci, True)
        P.dma('sp', lambda e: e.dma_start(out=o_ssm[:, 0, :], in_=ST[:, :]), r=['ST'], w=['o_ssm0'])
        for s_ in (1, 2):
            P.dma('sp', lambda e: e.dma_start(out=ST[:, :], in_=st_ssm[:, s_ - 1, :]), r=['ST'], w=['ST'])
            P.op('act', lambda e: e.activation(out=Sb[:, :], in_=ST[:, :], func=AF.Identity), r=['ST'], w=['Sb'])
            chunk(TP // 64 + s_ - 1, True)
            P.dma('sp', lambda e: e.dma_start(out=o_ssm[:, s_, :], in_=ST[:, :]), r=['ST'], w=['o_ssm%d' % s_])
        P.fence()
        out_proj_residual(ssd_w_out, 16, xbc, 'xbc', 1, pw=256)


    k.bg = {'s': 0, 'o': 0, 'd': 0}

    def gbank(grp):
        base, n = {'s': (0, 4), 'o': (4, 2), 'd': (6, 2)}[grp]
        i = k.bg[grp]
        k.bg[grp] = (i + 1) % n
        return base + i

    def fox_layer(l):
        hflat = h[:, :, :].rearrange("p c t -> p (c t)")
        bflat = blk16[:, :, :].rearrange("p c n -> p (c n)")
        qT = big[:, 0:NCH * T].rearrange("p (c t) -> p c t", t=T)
        oT = big[:, NCH * T:2 * NCH * T].rearrange("p (c t) -> p c t", t=T)
        kvb = big[:, 2 * NCH * T:3 * NCH * T]
        Kc = [kvb[:, r_ * 2048:(r_ + 1) * 2048] for r_ in range(4)]
        Vc = [kvb[:, 8192 + r_ * 2048:8192 + (r_ + 1) * 2048].rearrange("p (t n) -> p t n", n=128) for r_ in range(4)]
        kst = bflat[:, 0:512]
        kb = bflat[:, 512:768].bitcast(BF16)
        vst = bflat[:, 768:1280]
        vb = bflat[:, 1280:1536].bitcast(BF16)
        lfa = bflat[:, 1536:1808]
        lfb = bflat[:, 1808:2080]
        import os
        FST = float(os.environ.get('FOX_STAGE', '9'))
        if FST <= 0.05:
            return
        def evac_q(oc, lo, n, b):
            P.op('act', lambda e: e.activation(out=qT[:, oc, lo:lo + n], in_=banks[b][:, 0:n], func=AF.Identity), r=[('bank', b)],
                 w=keys('qT', oc, lo, lo + n))
        project(fox_w_qkv[:, 0:D], NCH, D, h, 'h', evac_q)
        if FST <= 0.08:
            return

        def evac_k(oc, lo, n, b):
            P.op('act', lambda e: e.activation(out=kst[:, 0:n], in_=banks[b][:, 0:n], func=AF.Identity), r=[('bank', b)], w=['kst'])
            P.dma('sp', lambda e: e.dma_start(out=o_kT[:, oc, lo:lo + n], in_=kst[:, 0:n]), r=['kst'], w=['o_kT'])
            if FST <= 0.09:
                return
            if lo < TP:
                P.op('act', lambda e: e.activation(out=kb[:, 0:n], in_=banks[b][:, 0:n], func=AF.Identity), r=[('bank', b)], w=['kb'])
                if FST > 0.095:
                    P.dma('sp', lambda e: e.dma_start(out=kv_in[oc].ap()[:, lo:lo + n], in_=kb[:, 0:n]), r=['kb'], w=['kv_in%d' % oc])
            else:
                P.op('act', lambda e: e.activation(out=ksm[:, oc, :], in_=banks[b][:, 0:n], func=AF.Identity), r=[('bank', b)], w=['ksm'])
        project(fox_w_qkv[:, D:2 * D], NCH, D, h, 'h', evac_k)
        if FST <= 0.4:
            return
        for cg in range(2):
            t, key = load_panel(fox_w_qkv[:, 2 * D:3 * D], NCH, cg * 512, 512)
            for tt in range(17):
                b = bank()
                for q in range(NCH):
                    P.op('pe', lambda e: e.matmul(banks[b][:, 0:512], lhsT=h[:, q, tt * 128:(tt + 1) * 128], rhs=t[:, q, :],
                                                  start=(q == 0), stop=(q == NCH - 1)),
                         r=[key] + keys('h', q, tt * 128, tt * 128 + 128), w=[('bank', b)])
                P.op('act', lambda e: e.activation(out=vst, in_=banks[b][:, 0:512], func=AF.Identity), r=[('bank', b)], w=['vst'])
                P.dma('sp', lambda e: e.dma_start(out=o_v[tt * 128:(tt + 1) * 128, cg * 512:(cg + 1) * 512], in_=vst), r=['vst'], w=['o_v'])
                if tt < 16:
                    P.op('act', lambda e: e.activation(out=vb, in_=banks[b][:, 0:512], func=AF.Identity), r=[('bank', b)], w=['vb'])
                    for cl in range(4):
                        c_ = cg * 4 + cl
                        P.dma('sp', lambda e: e.dma_start(out=kv_in[c_].ap()[:, 2048 + tt * 128:2048 + (tt + 1) * 128],
                                                          in_=vb[:, cl * 128:(cl + 1) * 128]), r=['vb'], w=['kv_in%d' % c_])
                else:
                    P.op('act', lambda e: e.activation(out=vsm[:, cg * 512:(cg + 1) * 512], in_=banks[b][:, 0:512], func=AF.Identity), r=[('bank', b)], w=['vsm'])
        if FST <= 0.6:
            return
        P.dma('pool', lambda e: e.dma_start(out=wf[:, :, :], in_=fox_w_f.rearrange("(kc p) n -> p kc n", p=128)), w=['wf'])
        b = bank()
        for tt in range(17):
            for q in range(NCH):
                P.op('pe', lambda e: e.matmul(banks[b][:, tt * 16:(tt + 1) * 16], lhsT=h[:, q, tt * 128:(tt + 1) * 128], rhs=wf[:, q, :],
                                              start=(q == 0), stop=(q == NCH - 1)),
                     r=['wf'] + keys('h', q, tt * 128, tt * 128 + 128), w=[('bank', b)])
        lf2 = lftm[:, :, :].rearrange("p t e -> p (t e)")
        P.op('dve', lambda e: e.tensor_tensor(out=lftm[:, :, :], in0=banks[b][:, 0:272].rearrange("p (t e) -> p t e", e=16),
                                              in1=fbf[:, :].unsqueeze(1).to_broadcast([128, 17, 16]), op=ALU.add), r=[('bank', b), 'fbf'], w=['lftm'])
        P.op('act', lambda e: e.activation(out=lfa, in_=lf2, func=AF.Abs), r=['lftm'], w=['lfa'])
        P.op('act', lambda e: e.activation(out=lfa, in_=lfa, func=AF.Exp, scale=-1.0), r=['lfa'], w=['lfa'])
        P.op('act', lambda e: e.activation(out=lfa, in_=lfa, func=AF.Ln, bias=small[:, 1:2]), r=['lfa', 'small'], w=['lfa'])
        P.op('dve', lambda e: e.tensor_scalar_min(out=lfb, in0=lf2, scalar1=0.0), r=['lftm'], w=['lfb'])
        P.op('dve', lambda e: e.tensor_tensor(out=lf2, in0=lfb, in1=lfa, op=ALU.subtract), r=['lfa', 'lfb'], w=['lftm'])
        P.dma('sp', lambda e: e.dma_start(out=o_lf.rearrange("(t p) e -> p t e", p=128), in_=lftm[:, :, :]), r=['lftm'], w=['o_lf'])
        if FST <= 0.8:
            return
        for c in range(NCH):
            P.coll(lambda e: e.collective_compute("AllGather", ALU.bypass, replica_groups=GROUPS,
                                                  ins=[kv_in[c].ap().opt()], outs=[kv_out[c].ap().opt()]),
                   r=['kv_in%d' % c], w=['kv_out%d' % c])
        P.fence()
        import os
        if FST <= 1:
            return
        idf, trif, onef, tri2 = cstf[:, 0, :], cstf[:, 1, :], cstf[:, 2, :], cstf[:, 3, :]
        HF = hflat[:, :].bitcast(F32)
        biasAll = HF[:, 0:4096].rearrange("p (r q x) -> p r q x", r=4, q=4)
        Mband = hflat[:, 8192:16384].rearrange("p (r m x) -> p r m x", r=4, m=4)
        Fg = bflat[:, 0:1024].rearrange("p (r x) -> p r x", r=4)
        pay = bflat[:, 1024:1296]
        PTq = bflat[:, 1296:1360].rearrange("p (q e) -> p q e", e=16)
        base = bflat[:, 1360:1424].rearrange("p (r e) -> p r e", e=16)
        bme = bflat[:, 1424:1440]
        cA = bflat[:, 1440:1696].rearrange("p (r q e) -> p r q e", r=4, q=4)
        cC = bflat[:, 1696:1952].rearrange("p (r q e) -> p r q e", r=4, q=4)
        tmf = bflat[:, 1952:2464]
        bs_s = bflat[:, 2464:2976].rearrange("p (s x) -> p s x", s=2)
        bn_s = bflat[:, 2976:2992]
        lfp = bflat[:, 2992:3504].rearrange("p (s t e) -> p s t e", s=2, t=16)
        totp = bflat[:, 3504:3536].rearrange("p (s e) -> p s e", s=2)
        ptile = [bflat[:, 3536:3792].bitcast(BF16), bflat[:, 3792:4048].bitcast(BF16)]
        rden = tmpf0[:, 0:512]
        otmp = tmpf0[:, 512:1024]

        def cumsum_tiles(src3, ntile, dst_cols, tot_dst):
            b = bank()
            for tt in range(ntile):
                P.op('pe', lambda e: e.matmul(banks[b][:, tt * 16:(tt + 1) * 16], lhsT=trif, rhs=src3[:, tt, :], start=True, stop=(tt == 0)),
                     r=['cstf', 'lftm', 'lfp'], w=[('bank', b)])
                for t2 in range(tt):
                    P.op('pe', lambda e: e.matmul(banks[b][:, tt * 16:(tt + 1) * 16], lhsT=onef, rhs=src3[:, t2, :], start=False, stop=(t2 == tt - 1)),
                         r=['cstf', 'lftm', 'lfp'], w=[('bank', b)])
            for t2 in range(ntile):
                P.op('pe', lambda e: e.matmul(banks[b][:, 496:512], lhsT=onef, rhs=src3[:, t2, :], start=(t2 == 0), stop=(t2 == ntile - 1)),
                     r=['cstf', 'lftm', 'lfp'], w=[('bank', b)])
            P.op('act', lambda e: e.activation(out=dst_cols, in_=banks[b][:, 0:ntile * 16], func=AF.Identity), r=[('bank', b)], w=['pay', 'Fg', 'bs_s'])
            P.op('act', lambda e: e.activation(out=tot_dst, in_=banks[b][:, 496:512], func=AF.Identity), r=[('bank', b)], w=['pay', 'totp'])

        cumsum_tiles(lftm, 16, pay[:, 0:256], pay[:, 256:272])
        P.op('dve', lambda e: e.memset(PTq[:, 0, :], 0.0), w=['PTq'])
        b = bank()
        for qb in range(1, 4):
            for t2 in range(4 * qb):
                P.op('pe', lambda e: e.matmul(banks[b][:, qb * 16:(qb + 1) * 16], lhsT=onef, rhs=lftm[:, t2, :], start=(t2 == 0), stop=(t2 == 4 * qb - 1)),
                     r=['cstf', 'lftm'], w=[('bank', b)])
        P.op('act', lambda e: e.activation(out=PTq[:, 1:4, :], in_=banks[b][:, 16:64].rearrange("p (q e) -> p q e", e=16), func=AF.Identity),
             r=[('bank', b)], w=['PTq'])
        xgf = tmpf0[:, 0:1088].rearrange('p (r n) -> p r n', r=4)
        exchange(pay, 272, ['pay'], W=272, gdst=xgf, gkeys=['xgf'])
        P.op('dve', lambda e: e.memset(base[:, 0, :], 0.0), w=['base'])
        for r_ in range(1, 4):
            P.op('dve', lambda e: e.tensor_tensor(out=base[:, r_, :], in0=base[:, r_ - 1, :], in1=xgf[:, r_ - 1, 256:272], op=ALU.add),
                 r=['xgf', 'base'], w=['base'])
        for r_ in range(4):
            P.op('dve', lambda e: e.tensor_tensor(out=Fg[:, r_, :].rearrange("p (t e) -> p t e", e=16),
                                                  in0=xgf[:, r_, 0:256].rearrange("p (t e) -> p t e", e=16),
                                                  in1=base[:, r_, :].unsqueeze(1).to_broadcast([128, 16, 16]), op=ALU.add),
                 r=['xgf', 'base'], w=['Fg'])
        P.op('dve', lambda e: e.tensor_scalar(out=bme, in0=xgf[:, 0, 256:272], scalar1=mk[:, 4:5], scalar2=None, op0=ALU.mult), r=['xgf', 'mk'], w=['bme'])
        for r_ in range(1, 4):
            P.op('dve', lambda e: e.scalar_tensor_tensor(out=bme, in0=xgf[:, r_, 256:272], scalar=mk[:, 4 + r_:5 + r_], in1=bme, op0=ALU.mult, op1=ALU.add),
                 r=['xgf', 'mk', 'bme'], w=['bme'])
        for qb in range(4):
            P.op('dve', lambda e: e.tensor_tensor(out=PTq[:, qb, :], in0=PTq[:, qb, :], in1=bme, op=ALU.add), r=['PTq', 'bme'], w=['PTq'])
        for r_ in range(4):
            for qb in range(4):
                P.op('dve', lambda e: e.tensor_scalar(out=cA[:, r_, qb, :], in0=PTq[:, qb, :], scalar1=nbc[:, r_:r_ + 1], scalar2=None, op0=ALU.add),
                     r=['PTq', 'nbc'], w=['cA'])
                P.op('dve', lambda e: e.tensor_scalar(out=cC[:, r_, qb, :], in0=PTq[:, qb, :], scalar1=nbc[:, 4 + r_:5 + r_], scalar2=None, op0=ALU.add),
                     r=['PTq', 'nbc'], w=['cC'])
                na = min(16, 4 * qb + 4)
                bA = biasAll[:, r_, qb, 0:na * 16].rearrange("p (t e) -> p t e", e=16)
                P.op('dve', lambda e: e.tensor_tensor(out=bA, in0=cA[:, r_, qb, :].unsqueeze(1).to_broadcast([128, na, 16]),
                                                      in1=Fg[:, r_, 0:na * 16].rearrange("p (t e) -> p t e", e=16), op=ALU.subtract),
                     r=['cA', 'Fg'], w=['biasAll'])
                if na < 16:
                    bC = biasAll[:, r_, qb, na * 16:256].rearrange("p (t e) -> p t e", e=16)
                    P.op('dve', lambda e: e.tensor_tensor(out=bC, in0=cC[:, r_, qb, :].unsqueeze(1).to_broadcast([128, 16 - na, 16]),
                                                          in1=Fg[:, r_, na * 16:256].rearrange("p (t e) -> p t e", e=16), op=ALU.subtract),
                         r=['cC', 'Fg'], w=['biasAll'])
        for m_ in range(4):
            P.dma('sp', lambda e: e.dma_start(out=tmf, in_=tmk[:, m_, :]), r=['tmf'], w=['tmf'])
            for r_ in range(4):
                P.op('dve', lambda e: e.tensor_scalar(out=Mband[:, r_, m_, :], in0=tmf, scalar1=mk[:, 8 + r_:9 + r_], scalar2=mk[:, 4 + r_:5 + r_],
                                                      op0=ALU.mult, op1=ALU.add), r=['tmf', 'mk'], w=['Mband'])
        P.dma('sp', lambda e: e.dma_start(out=lfp.rearrange("p s t e -> p (s t e)"), in_=lfc[:, :, :, :].rearrange("p s t e -> p (s t e)")), w=['lfp'])
        for s_ in range(2):
            cumsum_tiles(lfp[:, s_, :, :], 16, bs_s[:, s_, :], totp[:, s_, :])
            P.op('dve', lambda e: e.tensor_tensor(out=bs_s[:, s_, :].rearrange("p (t e) -> p t e", e=16),
                                                  in0=totp[:, s_, :].unsqueeze(1).to_broadcast([128, 16, 16]),
                                                  in1=bs_s[:, s_, :].rearrange("p (t e) -> p t e", e=16), op=ALU.subtract),
                 r=['bs_s', 'totp'], w=['bs_s'])
        b = bank()
        P.op('pe', lambda e: e.matmul(banks[b][:, 0:16], lhsT=tri2, rhs=lftm[:, 16, :], start=True, stop=True), r=['cstf', 'lftm'], w=[('bank', b)])
        P.op('act', lambda e: e.activation(out=bn_s, in_=banks[b][:, 0:16], func=AF.Identity, scale=-1.0), r=[('bank', b)], w=['bn_s'])

        if FST <= 2:
            return
        ptile.append(tmpf0[:, 1024:1280].bitcast(BF16))
        NPT = 3
        pts = [ptile, [sq[:, i_, :] for i_ in range(3)]]
        daccs = [tmpf0[:, 1280:1792], rstd[:, 0:512]]
        SQK = [('sq', i_) for i_ in range(NSQ)]

        def attend2(keytiles, c, qlo, qn):
            bo = gbank('o')
            bd = gbank('d')
            nt = len(keytiles)
            sbk = [None] * nt

            def emit_s(i):
                Ks, bias2, Vs, band, kp0, nk = keytiles[i]
                bb = (gbank('s'), gbank('s'))
                sbk[i] = bb
                for hh in range(2):
                    pb = 64 * hh
                    P.op('pe', lambda e: e.matmul(banks[bb[hh]][kp0:kp0 + nk, 0:qn], lhsT=Ks[pb:pb + 64, :], rhs=qT[pb:pb + 64, c, qlo:qlo + qn],
                                                  start=True, stop=True),
                         r=['Kc', 'ksm'] + keys('qT', c, qlo, qlo + qn), w=[('bank', bb[hh])])
            emit_s(0)
            for i, (Ks, bias2, Vs, band, kp0, nk) in enumerate(keytiles):
                bb = sbk[i]
                for hh in range(2):
                    pt_ = pts[hh][i % NPT]
                    pk = [('pt', hh, i % NPT)] + (SQK if hh == 1 else [])
                    P.op('act', lambda e: e.activation(out=pt_[kp0:kp0 + nk, 0:qn], in_=banks[bb[hh]][kp0:kp0 + nk, 0:qn], func=AF.Exp, scale=0.125,
                                                       bias=bias2[hh]),
                         r=[('bank', bb[hh]), 'biasAll', 'bs_s', 'bn_s'], w=pk)
                    if band is not None:
                        P.op('dve', lambda e: e.tensor_tensor(out=pt_[kp0:kp0 + nk, 0:qn], in0=pt_[kp0:kp0 + nk, 0:qn], in1=band, op=ALU.mult),
                             r=pk + ['Mband', 't64b'], w=pk)
                if i + 1 < nt:
                    emit_s(i + 1)
                for hh in range(2):
                    pb = 64 * hh
                    pt_ = pts[hh][i % NPT]
                    pk = [('pt', hh, i % NPT)] + (SQK if hh == 1 else [])
                    P.op('pe', lambda e: e.matmul(banks[bo][pb:pb + 64, 0:qn], lhsT=Vs[:, pb:pb + 64], rhs=pt_[kp0:kp0 + nk, 0:qn],
                                                  start=(i == 0), stop=(i == nt - 1)),
                         r=pk + ['Vc', 'vsm'], w=[('bank', bo)])
                    da = daccs[hh]
                    dk = ['dacc%d' % hh] + (['rstd'] if hh == 1 else [])
                    if kp0 == 0 and nk == 128 and i == 0:
                        P.op('dve', lambda e: e.tensor_copy(out=da[:, 0:qn], in_=pt_[:, 0:qn]), r=pk, w=dk)
                    else:
                        P.op('dve', lambda e: e.tensor_tensor(out=da[kp0:kp0 + nk, 0:qn], in0=da[kp0:kp0 + nk, 0:qn], in1=pt_[kp0:kp0 + nk, 0:qn], op=ALU.add),
                             r=pk + dk, w=dk)
            for hh in range(2):
                pb = 64 * hh
                dk = ['dacc%d' % hh] + (['rstd'] if hh == 1 else [])
                P.op('pe', lambda e: e.matmul(banks[bd][pb:pb + 64, 0:qn], lhsT=onef[:, 0:64], rhs=daccs[hh][:, 0:qn], start=True, stop=True),
                     r=dk + ['cstf'], w=[('bank', bd)])
            P.op('dve', lambda e: e.reciprocal(out=rden[:, 0:qn], in_=banks[bd][:, 0:qn]), r=[('bank', bd)], w=['rden'])
            P.op('act', lambda e: e.activation(out=otmp[:, 0:qn], in_=banks[bo][:, 0:qn], func=AF.Identity), r=[('bank', bo)], w=['otmp'])
            P.op('dve', lambda e: e.tensor_tensor(out=oT[:, c, qlo:qlo + qn], in0=otmp[:, 0:qn], in1=rden[:, 0:qn], op=ALU.mult),
                 r=['rden', 'otmp'], w=keys('oT', c, qlo, qlo + qn))

        for c in range(NCH):
            for r_ in range(4):
                P.dma('sp', lambda e: e.dma_start(out=Kc[r_], in_=kv_out[c].ap()[r_ * 128:(r_ + 1) * 128, 0:2048]), r=['kv_out%d' % c], w=['Kc'])
                P.dma('sp', lambda e: e.dma_start(out=Vc[r_], in_=kv_out[c].ap()[r_ * 128:(r_ + 1) * 128, 2048:4096].rearrange("p (t n) -> p t n", n=128)),
                      r=['kv_out%d' % c], w=['Vc'])
            for qb in range(4):
                tiles = []
                for r_ in range(4):
                    for kt in range(16):
                        m_ = kt - 4 * qb
                        band = Mband[:, r_, m_, :] if 0 <= m_ < 4 else None
                        bcols = tuple(biasAll[:, r_, qb, kt * 16 + 2 * c + hh:kt * 16 + 2 * c + hh + 1] for hh in range(2))
                        tiles.append((Kc[r_][:, kt * 128:(kt + 1) * 128], bcols, Vc[r_][:, kt, :], band, 0, 128))
                attend2(tiles, c, qb * 512, 512)
            for s_ in range(2):
                P.dma('pool', lambda e: e.dma_start(out=Kc[0], in_=kcT[:, s_, c, :]), w=['Kc'])
                P.dma('pool', lambda e: e.dma_start(out=Vc[0], in_=vc[s_, :, c * 128:(c + 1) * 128].rearrange("(t p) n -> p t n", p=128)), w=['Vc'])
                kp0 = 64 * s_
                tiles = []
                for kt in range(16):
                    bcols = tuple(bs_s[:, s_, kt * 16 + 2 * c + hh:kt * 16 + 2 * c + hh + 1] for hh in range(2))
                    tiles.append((Kc[0][:, kt * 128:(kt + 1) * 128], bcols, Vc[0][:, kt, :], None, 0, 128))
                bcols = tuple(bn_s[kp0:kp0 + 64, 2 * c + hh:2 * c + hh + 1] for hh in range(2))
                tiles.append((ksm[:, c, kp0:kp0 + 64], bcols, vsm[kp0:kp0 + 64, c * 128:(c + 1) * 128], t64b[kp0:kp0 + 64, :], kp0, 64))
                attend2(tiles, c, TP + s_ * TS, TS)
        P.fence()
        out_proj_residual(fox_w_o, NCH, oT, 'oT', 1)

    def ffn_layer(l):
        FV = lambda idx, fc: ffv[:, l, idx, fc:fc + 1]
        BK = [('blk16', c) for c in range(NCH)]
        gpre = tmpf0[:, 0:T]
        hal = small[:, 24:32]
        stf = fix
        P.dma('sp', lambda e: e.dma_start(out=ftmp[:, :, :].rearrange("p a c -> p (a c)"),
                                          in_=st_ffn[:, l, :, :, :].rearrange("p c s k -> p (c s k)")), w=['ftmp'])
        stv = ftmp[:, :, :].rearrange("p a c -> p (a c)").rearrange("p (c s k) -> p c s k", s=2, k=2)
        for fc in range(NFF):
            pn, o = fc // 4, fc % 4
            if o == 0:
                ncols = min(512, DFF - pn * 512)
                tg, kg = load_panel(ffn_w_gate[l], NCH, pn * 512, ncols)
                tu, ku = load_panel(ffn_w_up[l], NCH, pn * 512, ncols)
            for (lo, n) in BLKS:
                b = bank()
                for q in range(NCH):
                    P.op('pe', lambda e, b=b, o=o, q=q, tg=tg, lo=lo, n=n: e.matmul(
                        banks[b][:, 0:n], lhsT=tg[:, q, o * 128:(o + 1) * 128], rhs=h[:, q, lo:lo + n],
                        start=(q == 0), stop=(q == NCH - 1)), r=[kg] + keys('h', q, lo, lo + n), w=[('bank', b)])
                P.op('act', lambda e, b=b, lo=lo, n=n: e.activation(out=gpre[:, lo:lo + n], in_=banks[b][:, 0:n], func=AF.Identity),
                     r=[('bank', b)], w=tk(0))
            for (s, (so, ln)) in enumerate(SEQS):
                P.op('dve', lambda e, fc=fc, s=s, so=so, ln=ln: e.tensor_copy(out=gt[:, fc, s, :], in_=gpre[:, so + ln - 2:so + ln]),
                     r=tk(0), w=['gt'])
            P.op('dve', lambda e, fc=fc: e.tensor_copy(out=fix[:, fc, 0:2], in_=gpre[:, 0:2]), r=tk(0), w=['fix'])
            gc = blk16[:, :, :].rearrange('p c n -> p (c n)')[:, 0:T]
            P.op('dve', lambda e, fc=fc: e.tensor_scalar(out=gc[:, :], in0=gpre[:, :], scalar1=FV(2, fc), scalar2=FV(3, fc),
                                                        op0=ALU.mult, op1=ALU.add), r=tk(0) + ['ffv'], w=BK)
            for (s, (so, ln)) in enumerate(SEQS):
                for d_, wi in ((1, 1), (2, 0)):
                    P.op('dve', lambda e, fc=fc, so=so, ln=ln, d_=d_, wi=wi: e.scalar_tensor_tensor(
                        out=gc[:, so + d_:so + ln], in0=gpre[:, so:so + ln - d_], scalar=FV(wi, fc), in1=gc[:, so + d_:so + ln],
                        op0=ALU.mult, op1=ALU.add), r=tk(0) + ['ffv'] + BK, w=BK)
                if s > 0:
                    for (tcol, terms) in ((0, ((0, 0), (1, 1))), (1, ((0, 1),))):
                        for (wi, pk) in terms:
                            P.op('dve', lambda e, fc=fc, s=s, so=so, tcol=tcol, wi=wi, pk=pk: e.scalar_tensor_tensor(
                                out=gc[:, so + tcol:so + tcol + 1], in0=stv[:, fc, s - 1, pk:pk + 1], scalar=FV(wi, fc),
                                in1=gc[:, so + tcol:so + tcol + 1], op0=ALU.mult, op1=ALU.add),
                                r=['ftmp', 'ffv'] + BK, w=BK)
            P.op('act', lambda e: e.activation(out=gc[:, :], in_=gc[:, :], func=AF.Gelu), r=BK, w=BK)
            for (lo, n) in BLKS:
                b = bank()
                for q in range(NCH):
                    P.op('pe', lambda e, b=b, o=o, q=q, tu=tu, lo=lo, n=n: e.matmul(
                        banks[b][:, 0:n], lhsT=tu[:, q, o * 128:(o + 1) * 128], rhs=h[:, q, lo:lo + n],
                        start=(q == 0), stop=(q == NCH - 1)), r=[ku] + keys('h', q, lo, lo + n), w=[('bank', b)])
                if lo == 0:
                    P.op('act', lambda e, b=b, fc=fc: e.activation(out=fix[:, fc, 2:4], in_=banks[b][:, 0:2], func=AF.Identity),
                         r=[('bank', b)], w=['fix'])
                P.op('dve', lambda e, b=b, fc=fc, lo=lo, n=n: e.tensor_tensor(out=hid[:, fc, lo:lo + n], in0=gc[:, lo:lo + n], in1=banks[b][:, 0:n],
                                                                           op=ALU.mult),
                     r=[('bank', b)] + BK, w=keys('hid', fc, lo, lo + n))
        P.dma('sp', lambda e: e.dma_start(out=o_ffn[:, l, :, :, :], in_=gt[:, :, :, :]), r=['gt'], w=['o_ffn%d' % l])
        exchange(gt[:, :, 0, :], 2 * NFF, ['gt'], dst_view=lambda d: d.rearrange('p (c k) -> p c k', k=2))
        masked_sum(fprev[:, :, :].rearrange("p c k -> p (c k)"), ['fprev'], 2 * NFF, 0)
        W0, W1, W2, BB = ffv[:, l, 0, :], ffv[:, l, 1, :], ffv[:, l, 2, :], ffv[:, l, 3, :]
        g0, g1, t_ = ftmp[:, 0, :], ftmp[:, 1, :], ftmp[:, 2, :]
        TT = lambda out, a, b_, op, rk=(), wk=('ftmp',): P.op('dve', lambda e: e.tensor_tensor(out=out, in0=a, in1=b_, op=op),
                                                               r=['ftmp', 'fprev', 'fix', 'ffv'], w=list(wk))
        TT(g0, W0, fprev[:, :, 0], ALU.mult); TT(g0, g0, BB, ALU.add)
        TT(t_, W1, fprev[:, :, 1], ALU.mult); TT(g0, g0, t_, ALU.add)
        TT(t_, W2, fix[:, :, 0], ALU.mult); TT(g0, g0, t_, ALU.add)
        TT(g1, W0, fprev[:, :, 1], ALU.mult); TT(g1, g1, BB, ALU.add)
        TT(t_, W1, fix[:, :, 0], ALU.mult); TT(g1, g1, t_, ALU.add)
        TT(t_, W2, fix[:, :, 1], ALU.mult); TT(g1, g1, t_, ALU.add)
        P.op('act', lambda e: e.activation(out=ftmp[:, 0:2, :], in_=ftmp[:, 0:2, :], func=AF.Gelu), r=['ftmp'], w=['ftmp'])
        hkeys = [kk for fc in range(NFF) for kk in keys('hid', fc, 0, 2)]
        P.op('dve', lambda e: e.tensor_tensor(out=hid[:, :, 0], in0=g0, in1=fix[:, :, 2], op=ALU.mult), r=['ftmp', 'fix'], w=hkeys)
        P.op('dve', lambda e: e.tensor_tensor(out=hid[:, :, 1], in0=g1, in1=fix[:, :, 3], op=ALU.mult), r=['ftmp', 'fix'], w=hkeys)
        out_proj_residual(ffn_w_down[l], NFF, hid, 'hid', 3, pw=128)

    for l in range(nlayers):
        ada_layer(l)
        prenorm(0, 0)
        P.fence()
        if l in (0, 3):
            lru_layer(l, l // 3)
        elif l == 1:
            ssd_layer(l)
        elif l == 2:
            fox_layer(l)
        P.fence()
        prenorm(2, 3)
        ffn_layer(l)
        P.fence()

    for c in range(NCH):
        P.dma('sp', lambda e, c=c: e.dma_start(out=yT[:, c, :], in_=xd[:, c, :]), r=keys('xd', c, 0, T), w=['yT%d' % c])
    P.finish()
    st.close()
    return nc


def fm(a):
    sh = a.shape
    nch = sh[-1] // 128
    b = a.reshape(sh[:-1] + (nch, 128))
    return np.ascontiguousarray(np.moveaxis(b, -1, 0))


def unfm(a):
    b = np.moveaxis(a, 0, -1)
    return np.ascontiguousarray(b.reshape(b.shape[:-2] + (b.shape[-2] * 128,)))


NLAYERS = 4


def kernel(**inp):
    f = lambda n: np.asarray(inp[n], dtype=np.float32)
    nc = build(nlayers=NLAYERS)
    x_prompt, x_sample = f('x_prompt'), f('x_sample')
    lru_vec = np.stack([np.stack([f('lru_b_x')[j], f('lru_b_y')[j], f('lru_conv_w')[j][0], f('lru_conv_w')[j][1], f('lru_conv_w')[j][2],
                                   f('lru_conv_w')[j][3], f('lru_conv_b')[j], f('lru_b_a')[j], f('lru_b_i')[j], f('lru_lambda')[j],
                                   f('lru_b_o')[j]]) for j in range(2)])
    ffn_vec = np.concatenate([f('ffn_conv_w'), f('ffn_conv_b')[:, None, :]], 1)
    shared = dict(ada_bT=fm(f('ada_b')), norm_gT=fm(f('norm_g')),
                  lru_w_x=f('lru_w_x'), lru_w_y=f('lru_w_y'), lru_w_o=f('lru_w_o'), lru_w_a=f('lru_w_a'), lru_w_i=f('lru_w_i'),
                  ssd_w_in=f('ssd_w_in'), ssd_w_out=f('ssd_w_out'),
                  ssd_cvT=fm(np.concatenate([f('ssd_conv_w'), f('ssd_conv_b')[None, :]], 0)),
                  ssd_rep=np.ascontiguousarray(np.tile(np.concatenate([f('ssd_dt_bias'), f('ssd_a_log'), f('ssd_d')])[None, :], (128, 1))),
                  ssd_ngT=fm(f('ssd_norm_g')),
                  cst=np.ascontiguousarray(np.stack([np.eye(128, dtype=np.float32), np.triu(np.ones((128, 128), np.float32)),
                                                     np.ones((128, 128), np.float32),
                                                     np.kron(np.eye(2, dtype=np.float32), np.triu(np.ones((64, 64), np.float32)))], 1)),
                  fox_w_qkv=f('fox_w_qkv'), fox_w_o=f('fox_w_o'), fox_w_f=f('fox_w_f'),
                  fox_bf=np.ascontiguousarray(np.tile(f('fox_b_f')[None, :], (128, 1))),
                  tmk=np.ascontiguousarray(np.stack([(m * 128 + np.arange(128)[:, None] <= np.arange(512)[None, :]).astype(np.float32)
                                                     for m in range(4)], 1)),
                  t64=np.ascontiguousarray(((np.arange(128)[:, None] % 64) <= np.arange(64)[None, :]).astype(np.float32)),
                  lru_vT=fm(lru_vec), ffn_vT=fm(ffn_vec))
    for i in range(4):
        shared['ada_w%d' % i] = np.ascontiguousarray(f('ada_w')[i])
        shared['ffn_w_gate%d' % i] = np.ascontiguousarray(f('ffn_w_gate')[i])
        shared['ffn_w_up%d' % i] = np.ascontiguousarray(f('ffn_w_up')[i])
        shared['ffn_w_down%d' % i] = np.ascontiguousarray(f('ffn_w_down')[i])
    in_maps = []
    for c in range(NCORES):
        b, j = c // 4, c % 4
        xs = np.concatenate([x_prompt[b, j * TP:(j + 1) * TP], x_sample[2 * c], x_sample[2 * c + 1]], 0)
        xT = np.ascontiguousarray(xs.reshape(T, NCH, 128).transpose(2, 1, 0))
        cs = np.stack([f('c_prompt')[b], f('c_sample')[2 * c], f('c_sample')[2 * c + 1]], 0)
        cT = np.ascontiguousarray(cs.reshape(3, NCH, 128).transpose(2, 1, 0))
        msk = np.zeros((128, 16), np.float32)
        for r in range(4):
            msk[:, r] = 1.0 if r == j - 1 else 0.0
            msk[:, 4 + r] = 1.0 if r < j else 0.0
            msk[:, 8 + r] = 1.0 if r == j else 0.0
            msk[:, 12 + r] = 1.0 if r <= j else 0.0
        ck = f('cache_l2_k')[2 * c:2 * c + 2].reshape(2, 2048, D)
        kcT = np.ascontiguousarray(ck.transpose(0, 2, 1).reshape(2, NCH, 128, 2048).transpose(2, 0, 1, 3))
        vc_ = np.ascontiguousarray(f('cache_l2_v')[2 * c:2 * c + 2].reshape(2, 2048, D))
        lfc = np.ascontiguousarray(f('cache_l2_logf')[2 * c:2 * c + 2].reshape(2, 16, 128, 16).transpose(2, 0, 1, 3))
        lc = np.stack([f('state_l0_conv')[2 * c:2 * c + 2], f('state_l3_conv')[2 * c:2 * c + 2]])
        st_lconv = np.ascontiguousarray(lc.reshape(2, 2, 3, NCH, 128).transpose(4, 0, 3, 1, 2))
        lh = np.stack([f('state_l0_h')[2 * c:2 * c + 2], f('state_l3_h')[2 * c:2 * c + 2]])
        st_lh = np.ascontiguousarray(lh.reshape(2, 2, NCH, 128).transpose(3, 0, 2, 1))
        sf = f('state_ffn_conv')[:, 2 * c:2 * c + 2]
        st_ffn = np.ascontiguousarray(sf.reshape(4, 2, 2, NFF, 128).transpose(4, 0, 3, 1, 2))
        sc = f('state_l1_conv')[2 * c:2 * c + 2]
        st_sconv = np.ascontiguousarray(sc.reshape(2, 3, 24, 128).transpose(3, 2, 0, 1))
        ss = f('state_l1_ssm')[2 * c:2 * c + 2]
        st_ssm = np.ascontiguousarray(ss.transpose(3, 0, 1, 2).reshape(128, 2, 2048))
        m = dict(shared)
        m.update(st_sconv=st_sconv, st_ssm=st_ssm, kcT=kcT, vc=vc_, lfc=lfc)
        m.update(xT=xT, cT=cT, msk=msk, st_lconv=st_lconv, st_lh=st_lh, st_ffn=st_ffn)
        in_maps.append(m)
    res = run_bass_kernel_spmd(nc, in_maps, core_ids=list(range(NCORES)))
    R = res.results
    B, DB = 2, 16
    y_prompt = np.zeros((B, 4 * TP, D), np.float32)
    y_sample = np.zeros((DB, TS, D), np.float32)
    p_lconv = [np.zeros((B, 3, D), np.float32) for _ in range(2)]
    p_lh = [np.zeros((B, D), np.float32) for _ in range(2)]
    s_lconv = [np.zeros((DB, 3, D), np.float32) for _ in range(2)]
    s_lh = [np.zeros((DB, D), np.float32) for _ in range(2)]
    p_sconv = np.zeros((B, 3, 3072), np.float32); s_sconv = np.zeros((DB, 3, 3072), np.float32)
    p_ssm = np.zeros((B, 32, 64, 128), np.float32); s_ssm = np.zeros((DB, 32, 64, 128), np.float32)
    p_k = np.zeros((B, 4 * TP, 16, 64), np.float32); p_v = np.zeros((B, 4 * TP, 16, 64), np.float32); p_lf = np.zeros((B, 4 * TP, 16), np.float32)
    s_k = np.zeros((DB, TS, 16, 64), np.float32); s_v = np.zeros((DB, TS, 16, 64), np.float32); s_lf = np.zeros((DB, TS, 16), np.float32)
    p_ffn = np.zeros((4, B, 2, DFF), np.float32)
    s_ffn = np.zeros((4, DB, 2, DFF), np.float32)
    for c in range(NCORES):
        b, j = c // 4, c % 4
        r = R[c]
        yt = r['yT'].transpose(2, 1, 0).reshape(T, D)
        y_prompt[b, j * TP:(j + 1) * TP] = yt[0:TP]
        y_sample[2 * c] = yt[TP:TP + TS]
        y_sample[2 * c + 1] = yt[TP + TS:T]
        lc = r['o_lconv']
        lh_ = r['o_lh']
        of = r['o_ffn']
        for jj in range(2):
            cv = lc[:, jj].transpose(2, 3, 1, 0).reshape(3, 3, D)
            hv = lh_[:, jj].transpose(2, 1, 0).reshape(3, D)
            if j == 3:
                p_lconv[jj][b] = cv[0]
                p_lh[jj][b] = hv[0]
            s_lconv[jj][2 * c], s_lconv[jj][2 * c + 1] = cv[1], cv[2]
            s_lh[jj][2 * c], s_lh[jj][2 * c + 1] = hv[1], hv[2]
        kk_ = r['o_kT'].transpose(2, 1, 0).reshape(T, 16, 64)
        vv_ = r['o_v'].reshape(T, 16, 64)
        ll_ = r['o_lf']
        p_k[b, j * TP:(j + 1) * TP], p_v[b, j * TP:(j + 1) * TP], p_lf[b, j * TP:(j + 1) * TP] = kk_[0:TP], vv_[0:TP], ll_[0:TP]
        for si in range(2):
            sl = slice(TP + si * TS, TP + (si + 1) * TS)
            s_k[2 * c + si], s_v[2 * c + si], s_lf[2 * c + si] = kk_[sl], vv_[sl], ll_[sl]
        scv_ = r['o_sconv'].transpose(2, 3, 1, 0).reshape(3, 3, 3072)
        ssm_ = r['o_ssm'].reshape(128, 3, 32, 64).transpose(1, 2, 3, 0)
        if j == 3:
            p_sconv[b] = scv_[0]
            p_ssm[b] = ssm_[0]
        s_sconv[2 * c], s_sconv[2 * c + 1] = scv_[1], scv_[2]
        s_ssm[2 * c], s_ssm[2 * c + 1] = ssm_[1], ssm_[2]
        fv = of.transpose(1, 3, 4, 2, 0).reshape(4, 3, 2, DFF)
        if j == 3:
            p_ffn[:, b] = fv[:, 0]
        s_ffn[:, 2 * c], s_ffn[:, 2 * c + 1] = fv[:, 1], fv[:, 2]
    z = lambda *sh: np.zeros(sh, np.float32)
    return (y_prompt, y_sample,
            p_lconv[0], p_lh[0], p_sconv, p_ssm, p_k, p_v, p_lf,
            p_lconv[1], p_lh[1], p_ffn,
            s_lconv[0], s_lh[0], s_sconv, s_ssm, s_k, s_v, s_lf,
            s_lconv[1], s_lh[1], s_ffn)
```

```python
import numpy as np
from contextlib import ExitStack
import concourse.bass as bass
import concourse.mybir as mybir
from concourse.bass_utils import run_bass_kernel_spmd

F32 = mybir.dt.float32
BF16 = mybir.dt.bfloat16
ALU = mybir.AluOpType
AF = mybir.ActivationFunctionType

D = 1024
NCH = 8
TP = 2048
TS = 64
T = TP + 2 * TS
SEQS = [(0, TP), (TP, TS), (TP + TS, TS)]
BLKS = [(0, 512), (512, 512), (1024, 512), (1536, 512), (2048, 128)]
DFF = 2816
NFF = 22
EPS = 1e-6
NCORES = 8


class Rec:
    def __getattr__(self, name):
        def f(*a, **kw):
            self.call = (name, a, kw)
            return self
        return f


def _rec(fn):
    r = Rec()
    fn(r)
    name, a, kw = r.call
    return lambda e: getattr(e, name)(*a, **kw)


class Prog:
    def __init__(self, nc, st):
        self.nc = nc
        self.names = ['pe', 'act', 'dve', 'pool', 'sp']
        self.lists = {e: [] for e in self.names}
        self.sem = {e: st.enter_context(nc.semaphore('c_' + e)) for e in ['pe', 'act', 'dve', 'pool']}
        self.cnt = {e: 0 for e in self.sem}
        self.ND = 32
        self.dsem = [st.enter_context(nc.semaphore('d%d' % i)) for i in range(self.ND)]
        self.dcnt = [0] * self.ND
        self.dnext = {'sp': 0, 'pool': 0}
        self.dbase = {'sp': 0, 'pool': 16}
        self.csem = st.enter_context(nc.semaphore('cc'))
        self.ccnt = 0
        self.seen = {e: {} for e in self.names}
        self.res = {}

    def _deps(self, reads, writes):
        deps = {}
        for r in reads:
            s = self.res.get(r)
            if s and s[0]:
                k, v = s[0]
                deps[k] = max(deps.get(k, 0), v)
        for w in writes:
            s = self.res.get(w)
            if s:
                if s[0]:
                    k, v = s[0]
                    deps[k] = max(deps.get(k, 0), v)
                for k, v in s[1].items():
                    deps[k] = max(deps.get(k, 0), v)
        return deps

    def _need(self, eng, deps):
        for k, v in deps.items():
            if k == 'pe' and eng == 'pe':
                continue
            if self.seen[eng].get(k, 0) < v:
                self.lists[eng].append(('wait', k, v))
                self.seen[eng][k] = v

    def _mark(self, tag, reads, writes):
        k, v = tag
        for r in reads:
            s = self.res.setdefault(r, [None, {}])
            s[1][k] = max(s[1].get(k, 0), v)
        for w in writes:
            self.res[w] = [tag, {}]

    def op(self, eng, fn, r=(), w=()):
        self._need(eng, self._deps(r, w))
        self.cnt[eng] += 1
        self.lists[eng].append(('op', _rec(fn), eng))
        self._mark((eng, self.cnt[eng]), r, w)

    def dma(self, q, fn, r=(), w=()):
        i = self.dbase[q] + self.dnext[q]
        self.dnext[q] = (self.dnext[q] + 1) % 16
        deps = self._deps(r, w)
        if self.dcnt[i] > 0:
            deps[('d', i)] = max(deps.get(('d', i), 0), 16 * self.dcnt[i])
        self._need(q, deps)
        self.dcnt[i] += 1
        self.lists[q].append(('dma', _rec(fn), i))
        self._mark((('d', i), 16 * self.dcnt[i]), r, w)

    def coll(self, fn, r=(), w=()):
        self._need('pool', self._deps(r, w))
        self.ccnt += 1
        self.lists['pool'].append(('coll', _rec(fn)))
        self._mark(('cc', self.ccnt), r, w)

    def fence(self):
        allv = {e: self.cnt[e] for e in self.cnt if self.cnt[e]}
        for i in range(self.ND):
            if self.dcnt[i]:
                allv[('d', i)] = 16 * self.dcnt[i]
        if self.ccnt:
            allv['cc'] = self.ccnt
        for e in self.names:
            self._need(e, dict(allv))

    def _semof(self, k):
        if k == 'cc':
            return self.csem
        if isinstance(k, tuple):
            return self.dsem[k[1]]
        return self.sem[k]

    def finish(self):
        final = {('d', i): 16 * self.dcnt[i] for i in range(self.ND) if self.dcnt[i]}
        self._need('sp', final)
        nc = self.nc
        engs = {'pe': 'tensor', 'act': 'scalar', 'dve': 'vector', 'pool': 'gpsimd', 'sp': 'sync'}
        with nc.Block() as block:
            for name in self.names:
                lst = self.lists[name]

                def run(e, lst=lst):
                    for it in lst:
                        if it[0] == 'wait':
                            e.wait_ge(self._semof(it[1]), it[2])
                        elif it[0] == 'op':
                            it[1](e).then_inc(self.sem[it[2]], 1)
                        elif it[0] == 'dma':
                            it[1](e).then_inc(self.dsem[it[2]], 16)
                        else:
                            it[1](e).then_inc(self.csem)
                getattr(block, engs[name])(run)


def keys(name, c, lo, hi, gran=512):
    return [(name, c, b) for b in range(lo // gran, (hi - 1) // gran + 1)]


class K:
    pass


def build(nlayers=4, dbg=False):
    nc = bass.Bass("TRN2", target_bir_lowering=False)
    st = ExitStack()
    P = Prog(nc, st)
    k = K()
    k.nc, k.P, k.st = nc, P, st

    def din(name, shape):
        return nc.dram_tensor(name, list(shape), F32, kind="ExternalInput").ap()

    def dout(name, shape):
        return nc.dram_tensor(name, list(shape), F32, kind="ExternalOutput").ap()

    k.sbtot = 0

    def sb(name, shape, dt=F32):
        nb = int(np.prod(shape[1:])) * (2 if dt == BF16 else 4)
        k.sbtot += nb
        try:
            return st.enter_context(nc.sbuf_tensor(name, list(shape), dt))
        except AssertionError:
            print("SBUF OOM at", name, nb, "total requested so far", k.sbtot)
            raise

    def ps(name, shape, dt=F32):
        return st.enter_context(nc.psum_tensor(name, list(shape), dt))

    xT = din("xT", [128, NCH, T])
    msk = din("msk", [128, 16])
    ada_ws = din("ada_ws", [4, D, 768])
    ada_bs = din("ada_bs", [128, 4, 6])
    cT_all = din("cT_all", [128, NCH, 24])
    asel = din("asel", [128, 3, 24])
    norm_gT = din("norm_gT", [128, 4, 4, NCH])
    lru_w_x = din("lru_w_x", [2, D, D]); lru_w_y = din("lru_w_y", [2, D, D]); lru_w_o = din("lru_w_o", [2, D, D])
    lru_w_a = din("lru_w_a", [2, 8, 128, 128]); lru_w_i = din("lru_w_i", [2, 8, 128, 128])
    lru_vT = din("lru_vT", [128, 2, 11, NCH])
    st_lconv = din("st_lconv", [128, 2, NCH, 2, 3])
    st_lh = din("st_lh", [128, 2, NCH, 2])
    ffn_w_gate = [din("ffn_w_gate%d" % i, [D, DFF]) for i in range(4)]
    ffn_w_up = [din("ffn_w_up%d" % i, [D, DFF]) for i in range(4)]
    ffn_w_down = [din("ffn_w_down%d" % i, [DFF, D]) for i in range(4)]
    ffn_vT = din("ffn_vT", [128, 4, 4, NFF])
    st_ffn = din("st_ffn", [128, 4, NFF, 2, 2])
    o_ffn = dout("o_ffn", [128, 4, NFF, 3, 2])
    ssd_w_in = din("ssd_w_in", [D, 5152]); ssd_w_out = din("ssd_w_out", [2048, D])
    ssd_cvT = din("ssd_cvT", [128, 5, 24])
    ssd_rep = din("ssd_rep", [128, 96])
    ssd_ngT = din("ssd_ngT", [128, 16])
    st_sconv = din("st_sconv", [128, 24, 2, 3])
    st_ssm = din("st_ssm", [128, 2, 2048])
    cst = din("cst", [128, 4, 128])
    o_sconv = dout("o_sconv", [128, 24, 3, 3])
    o_ssm = dout("o_ssm", [128, 3, 2048])
    fox_w_qkv = din("fox_w_qkv", [D, 3 * D]); fox_w_o = din("fox_w_o", [D, D]); fox_w_f = din("fox_w_f", [D, 16])
    fox_bf = din("fox_bf", [128, 16])
    kcT = din("kcT", [128, 2, NCH, 2048])
    vc = din("vc", [2, 2048, D])
    lfc = din("lfc", [128, 2, 16, 16])
    tmk = din("tmk", [128, 4, 512])
    t64 = din("t64", [128, 64])
    o_kT = dout("o_kT", [128, NCH, T])
    o_v = dout("o_v", [T, D])
    o_lf = dout("o_lf", [T, 16])
    kv_in = [nc.dram_tensor("kv_in%d" % c, [128, 4096], BF16) for c in range(NCH)]
    kv_out = [nc.dram_tensor("kv_out%d" % c, [4 * 128, 4096], BF16) for c in range(NCH)]
    zd = nc.dram_tensor("zd", [T, 2048], BF16).ap()
    yT = dout("yT", [128, NCH, T])
    o_lconv = dout("o_lconv", [128, 2, NCH, 3, 3])
    o_lh = dout("o_lh", [128, 2, NCH, 3])
    GROUPS = [[0, 1, 2, 3], [4, 5, 6, 7]]

    xd = nc.dram_tensor("xd", [128, NCH, T], F32).ap()
    h = sb("h", [128, NCH, T], BF16)
    big = sb("big", [128, 24 * T], BF16)
    HW = T + 9
    bufA = big[:, 0:NCH * T].rearrange("p (c t) -> p c t", t=T)
    bufB = big[:, NCH * T:NCH * T + NCH * HW].rearrange("p (c t) -> p c t", t=HW)
    hid = big[:, 0:NFF * T].rearrange("p (c t) -> p c t", t=T)
    xbc = big[:, :].rearrange("p (c t) -> p c t", t=T)
    HOFF = [3, 3 + TP + 3, 3 + TP + 3 + TS + 3]
    wpf = [sb("wp%d" % i, [128, NCH * 512], BF16) for i in range(2)]
    wsm = [sb("wsm%d" % i, [128, 2, 128], BF16) for i in range(2)]
    ones = sb("ones", [128, 128], BF16)
    ada_all = sb("ada_all", [128, 4, 48, 3])
    asl = sb("asl", [128, 3, 24])
    abs_ = sb("abs_", [128, 4, 6])
    ng = sb("ng", [128, 4, 4, NCH])
    gsc = sb("gsc", [128, 4, NCH, 3])
    lruv = sb("lruv", [128, 2, 11, NCH])
    cfac = sb("cfac", [128, 2, 2, NCH])
    mk = sb("mk", [128, 16])
    rstd = sb("rstd", [128, 512])
    NSQ = 4
    sq = sb("sq", [128, NSQ, 512], BF16)
    tmpf0 = sb("tmpf0", [128, T + 16])
    ctmp = tmpf0[:, 0:192].rearrange("p (c n) -> p c n", n=24)
    siluc = tmpf0[:, 192:288].bitcast(BF16).rearrange("p (c n) -> p c n", n=24)
    tmpb = sb("tmpb", [128, T], BF16)
    blk16 = sb("blk16", [128, NCH, 512])
    small = sb("small", [128, 64])
    XW = 80
    xg = sb("xg", [128, 4, XW])
    hin = sb("hin", [128, NCH, 3])
    tails = sb("tails", [128, NCH, 3, 3])
    hl = sb("hl", [128, NCH, 3])
    ffv = sb("ffv", [128, 4, 4, NFF])
    fix = sb("fix", [128, NFF, 4])
    gt = sb("gt", [128, NFF, 3, 2])
    fprev = sb("fprev", [128, NFF, 2])
    ftmp = sb("ftmp", [128, 4, NFF])
    cstf = sb("cstf", [128, 4, 128])
    ksm = sb("ksm", [128, NCH, 128], BF16)
    vsm = sb("vsm", [128, D], BF16)
    lftm = sb("lftm", [128, 17, 16])
    wf = sb("wf", [128, NCH, 16], BF16)
    ones1 = sb("ones1", [128, 64], BF16)
    fbf = sb("fbf", [128, 16])
    t64b = sb("t64b", [128, 64], BF16)
    nbc = sb("nbc", [128, 8])
    ident = sb("ident", [128, 128], BF16)
    scv = sb("scv", [128, 5, 24])
    srep = sb("srep", [128, 96])
    sng = sb("sng", [128, 16])
    stail = sb("stail", [128, 24, 3, 3])
    sfirst = sb("sfirst", [128, 24, 3])
    sprev = sb("sprev", [128, 24, 3])
    sfx = sb("sfx", [128, 4, 24])
    banks = [ps("bank%d" % i, [128, 512]) for i in range(8)]
    k.bank_i = 0

    def tmpf(i):
        return h[:, 2 * (i - 1):2 * (i - 1) + 2, :].rearrange("p a t -> p (a t)").bitcast(F32)

    def tk(i):
        if i == 0:
            return keys('tmpf0', 0, 0, T)
        return keys('h', 2 * (i - 1), 0, T) + keys('h', 2 * (i - 1) + 1, 0, T)

    def bank():
        b = k.bank_i
        k.bank_i = (b + 1) % 8
        return b

    def load(dst, src, rk=(), wk=()):
        P.dma('sp', lambda e: e.dma_start(out=dst, in_=src), r=rk, w=wk)

    for c in range(NCH):
        P.dma('sp', lambda e, c=c: e.dma_start(out=xd[:, c, :], in_=xT[:, c, :]), w=keys('xd', c, 0, T))
    load(ctmp[:, :, :], cT_all[:, :, :], wk=['ctmp'])
    load(asl[:, :, :], asel[:, :, :], wk=['asl'])
    load(abs_[:, :, :], ada_bs[:, :, :], wk=['abs_'])
    load(ng[:, :, :, :], norm_gT[:, :, :, :], wk=['ng'])
    load(lruv[:, :, :, :], lru_vT[:, :, :, :], wk=['lruv'])
    load(ffv[:, :, :, :], ffn_vT[:, :, :, :], wk=['ffv'])
    load(mk[:, :], msk[:, :], wk=['mk'])
    load(cstf[:, :, :], cst[:, :, :], wk=['cstf'])
    load(fbf[:, :], fox_bf[:, :], wk=['fbf'])
    P.dma('sp', lambda e: e.dma_start(out=small[:, 48:50], in_=kcT[:, 0, 0, 0:2]), w=['touch'])
    P.dma('sp', lambda e: e.dma_start(out=small[:, 50:52], in_=vc[0, 0:128, 0:2]), w=['touch'])
    P.dma('sp', lambda e: e.dma_start(out=small[:, 52:54], in_=lfc[:, 0, 0, 0:2]), w=['touch'])
    P.dma('sp', lambda e: e.dma_start(out=small[:, 54:56], in_=tmk[:, 0, 0:2]), w=['touch'])
    P.dma('sp', lambda e: e.dma_start(out=small[:, 56:58], in_=fox_w_qkv[0:128, 0:2]), w=['touch'])
    P.dma('sp', lambda e: e.dma_start(out=small[:, 58:60], in_=fox_w_o[0:128, 0:2]), w=['touch'])
    P.dma('sp', lambda e: e.dma_start(out=small[:, 60:62], in_=fox_w_f[0:128, 0:2]), w=['touch'])
    P.dma('pool', lambda e: e.dma_start(out=t64b[:, :], in_=t64[:, :]), w=['t64b'])
    P.op('dve', lambda e: e.memset(ones1[:, :], 1.0), w=['ones1'])
    P.op('dve', lambda e: e.tensor_scalar(out=nbc[:, 0:4], in0=mk[:, 12:16], scalar1=-1.0, scalar2=30000.0, op0=ALU.add, op1=ALU.mult), r=['mk'], w=['nbc'])
    P.op('dve', lambda e: e.tensor_scalar(out=nbc[:, 4:8], in0=mk[:, 4:8], scalar1=-1.0, scalar2=30000.0, op0=ALU.add, op1=ALU.mult), r=['mk'], w=['nbc'])
    load(scv[:, :, :], ssd_cvT[:, :, :], wk=['scv'])
    load(srep[:, :], ssd_rep[:, :], wk=['srep'])
    load(sng[:, :], ssd_ngT[:, :], wk=['sng'])
    P.dma('pool', lambda e: e.dma_start(out=ident[:, :], in_=cst[:, 0, :]), w=['ident'])
    P.op('act', lambda e: e.activation(out=srep[:, 32:64], in_=srep[:, 32:64], func=AF.Exp), r=['srep'], w=['srep'])
    P.op('dve', lambda e: e.tensor_scalar(out=srep[:, 32:64], in0=srep[:, 32:64], scalar1=-1.0, scalar2=None, op0=ALU.mult), r=['srep'], w=['srep'])
    P.op('dve', lambda e: e.memset(ones[:, :], 1.0 / D), w=['ones'])
    P.op('dve', lambda e: e.memset(small[:, 0:1], EPS), w=['small'])
    P.op('dve', lambda e: e.memset(small[:, 1:2], 1.0), w=['small'])
    sgt = blk16[:, :, :].rearrange("p c n -> p (c n)")[:, 0:NCH * 24].rearrange("p (c n) -> p c n", n=24)
    P.op('act', lambda e: e.activation(out=sgt, in_=ctmp[:, :, :], func=AF.Sigmoid), r=['ctmp'], w=[('blk16', c_) for c_ in range(NCH)])
    P.op('dve', lambda e: e.tensor_tensor(out=siluc[:, :, :], in0=sgt, in1=ctmp[:, :, :], op=ALU.mult),
         r=[('blk16', c_) for c_ in range(NCH)] + ['ctmp'], w=['siluc'])
    for j in range(2):
        zz = cfac[:, j, 0, :]
        t1 = cfac[:, j, 1, :]
        P.op('act', lambda e, j=j, zz=zz: e.activation(out=zz, in_=lruv[:, j, 9, :], func=AF.Exp, scale=-1.0), r=['lruv'], w=['cfac'])
        P.op('dve', lambda e, zz=zz, t1=t1: e.tensor_scalar(out=t1, in0=zz, scalar1=-1.0 / 3.0, scalar2=0.5, op0=ALU.mult, op1=ALU.add),
             r=['cfac'], w=['cfac'])
        P.op('dve', lambda e, zz=zz, t1=t1: e.tensor_tensor(out=t1, in0=t1, in1=zz, op=ALU.mult), r=['cfac'], w=['cfac'])
        P.op('dve', lambda e, zz=zz, t1=t1: e.tensor_scalar(out=t1, in0=t1, scalar1=-1.0, scalar2=1.0, op0=ALU.mult, op1=ALU.add),
             r=['cfac'], w=['cfac'])
        P.op('dve', lambda e, zz=zz, t1=t1: e.tensor_tensor(out=t1, in0=t1, in1=zz, op=ALU.mult), r=['cfac'], w=['cfac'])
        P.op('dve', lambda e, zz=zz, t1=t1: e.tensor_scalar(out=zz, in0=t1, scalar1=-8.0, scalar2=None, op0=ALU.mult), r=['cfac'], w=['cfac'])
        P.op('dve', lambda e, zz=zz, t1=t1: e.tensor_scalar(out=t1, in0=t1, scalar1=-16.0, scalar2=None, op0=ALU.mult), r=['cfac'], w=['cfac'])

    k.wi = 0

    def load_panel(wap, kc, c0, ncols):
        i = k.wi
        k.wi = (i + 1) % 2
        t = wpf[i][:, 0:kc * ncols].rearrange("p (q n) -> p q n", n=ncols)
        src = wap.rearrange("(kc p) n -> p kc n", p=128)
        key = ('wp', i)
        P.dma('pool', lambda e: e.dma_start(out=t[:, :, :], in_=src[:, :, c0:c0 + ncols]), w=[key])
        return t, key

    def ada_precompute():
        apart = blk16[:, :, :].rearrange("p c n -> p (c n)")[:, 1024:1024 + 576].rearrange("p (l o n) -> p l o n", l=4, o=6)
        for l in range(4):
            t, key = load_panel(ada_ws[l], NCH, 0, 768) if False else (None, None)
            for (c0, ncols) in ((0, 512), (512, 256)):
                t, key = load_panel(ada_ws[l], NCH, c0, ncols)
                b = bank()
                for o in range(ncols // 128):
                    for q in range(NCH):
                        P.op('pe', lambda e: e.matmul(banks[b][:, o * 32:o * 32 + 24], lhsT=t[:, q, o * 128:(o + 1) * 128], rhs=siluc[:, q, :],
                                                      start=(q == 0), stop=(q == NCH - 1)), r=[key, 'siluc'], w=[('bank', b)])
                for o in range(ncols // 128):
                    oc = c0 // 128 + o
                    P.op('act', lambda e: e.activation(out=apart[:, l, oc, :], in_=banks[b][:, o * 32:o * 32 + 24], func=AF.Identity,
                                                       bias=abs_[:, l, oc:oc + 1]), r=[('bank', b), 'abs_'], w=['apart'])
        xin = nc.dram_tensor("ada_in", [128, 576], F32)
        xout = nc.dram_tensor("ada_out", [8 * 128, 576], F32)
        P.dma('pool', lambda e: e.dma_start(out=xin.ap()[:, :], in_=apart.rearrange("p l o n -> p (l o n)")), r=['apart'], w=['ada_in'])
        P.coll(lambda e: e.collective_compute("AllGather", ALU.bypass, replica_groups=[list(range(8))],
                                              ins=[xin.ap().opt()], outs=[xout.ap().opt()]), r=['ada_in'], w=['ada_out'])
        gat = blk16[:, :, :].rearrange("p c n -> p (c n)")[:, 0:576].rearrange("p (l o n) -> p l o n", l=4, o=6)
        gtm = blk16[:, :, :].rearrange("p c n -> p (c n)")[:, 2048:2048 + 576].rearrange("p (l o n) -> p l o n", l=4, o=6)
        BKA = [('blk16', c_) for c_ in range(NCH)]
        for r_ in range(8):
            P.dma('sp', lambda e: e.dma_start(out=gat.rearrange("p l o n -> p (l o n)"), in_=xout.ap()[r_ * 128:(r_ + 1) * 128, :]),
                  r=['ada_out', 'apart'], w=BKA)
            for s_ in range(3):
                P.op('dve', lambda e: e.tensor_tensor(out=gtm.rearrange("p l o n -> p (l o) n"), in0=gat.rearrange("p l o n -> p (l o) n"),
                                                      in1=asl[:, s_, :].unsqueeze(1).to_broadcast([128, 24, 24]), op=ALU.mult), r=BKA + ['asl'], w=BKA)
                P.op('dve', lambda e: e.tensor_reduce(out=ada_all[:, :, r_ * 6:(r_ + 1) * 6, s_], in_=gtm, axis=mybir.AxisListType.X, op=ALU.add),
                     r=BKA, w=['ada_all'])

    def ada_layer(l):
        k.ada = ada_all[:, l, :, :]
        for kind, (gi, ai, plus1) in enumerate([(0, 1, True), (1, 2, False), (2, 4, True), (3, 5, False)]):
            for s in range(3):
                if plus1:
                    P.op('dve', lambda e, kind=kind, gi=gi, ai=ai, s=s: e.scalar_tensor_tensor(
                        out=gsc[:, kind, :, s], in0=k.ada[:, ai * 8:(ai + 1) * 8, s], scalar=1.0, in1=ng[:, l, gi, :],
                        op0=ALU.add, op1=ALU.mult), r=['ada', 'ng'], w=['gsc'])
                else:
                    P.op('dve', lambda e, kind=kind, gi=gi, ai=ai, s=s: e.tensor_tensor(
                        out=gsc[:, kind, :, s], in0=k.ada[:, ai * 8:(ai + 1) * 8, s], in1=ng[:, l, gi, :], op=ALU.mult),
                        r=['ada', 'ng'], w=['gsc'])

    def rstd_block(src_fn, nch, src_keys_fn, lo, n, dim):
        b = bank()
        for c in range(nch):
            if c % 2 == 0:
                P.op('act', lambda e, c=c: e.activation(out=sq[:, c % NSQ, 0:n], in_=src_fn(c), func=AF.Square),
                     r=src_keys_fn(c), w=[('sq', c % NSQ)])
            else:
                P.op('dve', lambda e, c=c: e.tensor_tensor(out=sq[:, c % NSQ, 0:n], in0=src_fn(c), in1=src_fn(c), op=ALU.mult),
                     r=src_keys_fn(c), w=[('sq', c % NSQ)])
            P.op('pe', lambda e, c=c, b=b: e.matmul(banks[b][:, 0:n], lhsT=ones[:, :], rhs=sq[:, c % NSQ, 0:n],
                                                     start=(c == 0), stop=(c == nch - 1)),
                 r=['ones', ('sq', c % NSQ)], w=[('bank', b)])
        P.op('act', lambda e, b=b: e.activation(out=rstd[:, 0:n], in_=banks[b][:, 0:n], func=AF.Sqrt,
                                                 scale=float(D) / dim, bias=small[:, 0:1]),
             r=[('bank', b), 'small'], w=['rstd'])
        P.op('dve', lambda e: e.reciprocal(out=rstd[:, 0:n], in_=rstd[:, 0:n]),
             r=['rstd'], w=['rstd'])

    def seq_pieces(lo, n):
        out = []
        for s, (o, ln) in enumerate(SEQS):
            a, b_ = max(lo, o), min(lo + n, o + ln)
            if a < b_:
                out.append((s, a, b_))
        return out

    def prenorm(kind, shift_idx):
        for (lo, n) in BLKS:
            P.dma('sp', lambda e, lo=lo, n=n: e.dma_start(out=blk16[:, :, 0:n], in_=xd[:, :, lo:lo + n]),
                  r=[kk for c in range(NCH) for kk in keys('xd', c, lo, lo + n)], w=[('blk16', c) for c in range(NCH)])
            rstd_block(lambda c: blk16[:, c, 0:n], NCH, lambda c: [('blk16', c)], lo, n, D)
            for c in range(NCH):
                for (s, a, b_) in seq_pieces(lo, n):
                    P.op('dve', lambda e, c=c, s=s, a=a, b_=b_, lo=lo: e.scalar_tensor_tensor(
                        out=tmpf0[:, a:b_], in0=blk16[:, c, a - lo:b_ - lo], scalar=gsc[:, kind, c, s:s + 1], in1=rstd[:, a - lo:b_ - lo],
                        op0=ALU.mult, op1=ALU.mult),
                        r=[('blk16', c), 'gsc'] + ['rstd'], w=keys('tmpf0', 0, a, b_))
                    P.op('act', lambda e, c=c, s=s, a=a, b_=b_: e.activation(
                        out=h[:, c, a:b_], in_=tmpf0[:, a:b_], func=AF.Identity, bias=k.ada[:, shift_idx * 8 + c, s:s + 1]),
                        r=keys('tmpf0', 0, a, b_) + ['ada'], w=keys('h', c, a, b_))

    def project(wap, kc, nout, src, srcname, evac, pw=512):
        for c0 in range(0, nout, pw):
            ncols = min(pw, nout - c0)
            t, key = load_panel(wap, kc, c0, ncols)
            for o in range(ncols // 128):
                oc = c0 // 128 + o
                for (lo, n) in BLKS:
                    b = bank()
                    for q in range(kc):
                        P.op('pe', lambda e, b=b, o=o, q=q, t=t, lo=lo, n=n: e.matmul(
                            banks[b][:, 0:n], lhsT=t[:, q, o * 128:(o + 1) * 128], rhs=src[:, q, lo:lo + n],
                            start=(q == 0), stop=(q == kc - 1)),
                            r=[key] + keys(srcname, q, lo, lo + n), w=[('bank', b)])
                    evac(oc, lo, n, b)

    def out_proj_residual(wap, kc, src, srcname, gkind, bias_fn=None, pw=512):
        def ev(oc, lo, n, b):
            if bias_fn is not None:
                P.op('act', lambda e: e.activation(out=h[:, oc, lo:lo + n], in_=banks[b][:, 0:n], func=AF.Identity, bias=bias_fn(oc)),
                     r=[('bank', b), 'lruv'], w=keys('h', oc, lo, lo + n))
            else:
                P.op('act', lambda e: e.activation(out=h[:, oc, lo:lo + n], in_=banks[b][:, 0:n], func=AF.Identity),
                     r=[('bank', b)], w=keys('h', oc, lo, lo + n))
        project(wap, kc, D, src, srcname, ev, pw=pw)
        for (lo, n) in BLKS:
            rstd_block(lambda c: h[:, c, lo:lo + n], NCH, lambda c: keys('h', c, lo, lo + n), lo, n, D)
            for c in range(NCH):
                for (s, a, b_) in seq_pieces(lo, n):
                    P.op('dve', lambda e, c=c, s=s, a=a, b_=b_, lo=lo: e.scalar_tensor_tensor(
                        out=blk16[:, c, a - lo:b_ - lo], in0=h[:, c, a:b_], scalar=gsc[:, gkind, c, s:s + 1],
                        in1=rstd[:, a - lo:b_ - lo], op0=ALU.mult, op1=ALU.mult),
                        r=keys('h', c, a, b_) + ['gsc'] + ['rstd'], w=[('blk16', c)])
            P.dma('pool', lambda e, lo=lo, n=n: e.dma_start(out=xd[:, :, lo:lo + n], in_=blk16[:, :, 0:n], accum_op=ALU.add),
                  r=[('blk16', c) for c in range(NCH)],
                  w=[kk for c in range(NCH) for kk in keys('xd', c, lo, lo + n)])

    k.xi = 0

    def exchange(src_ap, ncols, src_keys, dst_view=None, W=None, gdst=None, gkeys=None):
        i = k.xi
        k.xi += 1
        W = W or XW
        gdst = xg if gdst is None else gdst
        gkeys = ['xg'] if gkeys is None else gkeys
        xin = nc.dram_tensor("xb_in%d" % i, [128, W], F32)
        xout = nc.dram_tensor("xb_out%d" % i, [4 * 128, W], F32)
        dst = xin.ap()[:, 0:ncols]
        if dst_view is not None:
            dst = dst_view(dst)
        P.dma('pool', lambda e: e.dma_start(out=dst, in_=src_ap), r=src_keys, w=['xb_in%d' % i])
        P.coll(lambda e: e.collective_compute("AllGather", ALU.bypass, replica_groups=GROUPS,
                                              ins=[xin.ap().opt()], outs=[xout.ap().opt()]),
               r=['xb_in%d' % i], w=['xb_out%d' % i])
        P.dma('pool', lambda e: e.dma_start(out=gdst[:, :, 0:W], in_=xout.ap().rearrange("(r p) n -> p r n", p=128)),
              r=['xb_out%d' % i], w=gkeys)

    def masked_sum(dst, dkeys, ncols, mcol0):
        P.op('dve', lambda e: e.tensor_scalar(out=dst, in0=xg[:, 0, 0:ncols], scalar1=mk[:, mcol0:mcol0 + 1], scalar2=None, op0=ALU.mult),
             r=['xg', 'mk'], w=dkeys)
        for r_ in range(1, 4):
            P.op('dve', lambda e, r_=r_: e.scalar_tensor_tensor(out=dst, in0=xg[:, r_, 0:ncols], scalar=mk[:, mcol0 + r_:mcol0 + r_ + 1],
                                                               in1=dst, op0=ALU.mult, op1=ALU.add),
                 r=['xg', 'mk'] + dkeys, w=dkeys)

    def lru_layer(l, j):
        V = lambda idx, c: lruv[:, j, idx, c:c + 1]

        def evac_x(oc, lo, n, b):
            for (s, a, b_) in seq_pieces(lo, n):
                off = HOFF[s] - SEQS[s][0]
                P.op('act', lambda e, a=a, b_=b_, off=off: e.activation(out=bufB[:, oc, a + off:b_ + off], in_=banks[b][:, a - lo:b_ - lo],
                                                                         func=AF.Identity, bias=V(0, oc)),
                     r=[('bank', b), 'lruv'], w=[('bufB', oc)])
                if b_ == SEQS[s][0] + SEQS[s][1]:
                    P.op('act', lambda e, b_=b_, s=s: e.activation(out=tails[:, oc, s, :], in_=banks[b][:, b_ - 3 - lo:b_ - lo],
                                                                  func=AF.Identity, bias=V(0, oc)),
                         r=[('bank', b), 'lruv'], w=['tails'])
        project(lru_w_x[j], NCH, D, h, 'h', evac_x)

        def evac_y(oc, lo, n, b):
            P.op('act', lambda e: e.activation(out=bufA[:, oc, lo:lo + n], in_=banks[b][:, 0:n], func=AF.Gelu, bias=V(1, oc)),
                 r=[('bank', b), 'lruv'], w=keys('bufA', oc, lo, lo + n))
        project(lru_w_y[j], NCH, D, h, 'h', evac_y)
        P.dma('sp', lambda e: e.dma_start(out=o_lconv[:, j, :, :, :], in_=tails[:, :, :, :]), r=['tails'], w=['o_lconv%d' % j])
        P.dma('sp', lambda e: e.dma_start(out=tmpf0[:, 0:NCH * 6], in_=st_lconv[:, j, :, :, :].rearrange("p c s k -> p (c s k)")),
              w=tk(0))
        for c in range(NCH):
            for s in (1, 2):
                P.op('dve', lambda e, c=c, s=s: e.tensor_copy(out=bufB[:, c, HOFF[s] - 3:HOFF[s]],
                                                               in_=tmpf0[:, c * 6 + (s - 1) * 3:c * 6 + (s - 1) * 3 + 3]),
                     r=tk(0), w=[('bufB', c)])
        exchange(tails[:, :, 0, :], 24, ['tails'], dst_view=lambda d: d.rearrange('p (c k) -> p c k', k=3))
        masked_sum(small[:, 32:56], ['pt'], 24, 0)
        for c in range(NCH):
            P.op('dve', lambda e, c=c: e.tensor_copy(out=bufB[:, c, 0:3], in_=small[:, 32 + c * 3:32 + c * 3 + 3]),
                 r=['pt'], w=[('bufB', c)])
        P.dma('sp', lambda e: e.dma_start(out=hin[:, :, 1:3], in_=st_lh[:, j, :, :]), w=['hin'])
        P.op('dve', lambda e: e.memset(hin[:, :, 0:1], 0.0), w=['hin'])
        u, rr, ig, aa, sc_ = tmpf0[:, 0:T], tmpf(1), tmpf(2), tmpf(3), tmpf(4)
        zeros = sq[:, :, :].rearrange('p c n -> p (c n)')
        ZK = [('sq', i_) for i_ in range(NSQ)]
        P.op('dve', lambda e: e.memset(zeros[:, :], 0.0), w=ZK)
        summ = small
        for c in range(NCH):
            wt = wsm[c % 2]
            wkey = ('wsm', c % 2)
            P.dma('pool', lambda e, c=c, wt=wt: e.dma_start(out=wt[:, 0, :], in_=lru_w_a[j, c]), w=[wkey])
            P.dma('pool', lambda e, c=c, wt=wt: e.dma_start(out=wt[:, 1, :], in_=lru_w_i[j, c]), w=[wkey])
            for (s, (o, ln)) in enumerate(SEQS):
                ho = HOFF[s]
                P.op('dve', lambda e, c=c, o=o, ln=ln, ho=ho: e.tensor_scalar(
                    out=u[:, o:o + ln], in0=bufB[:, c, ho - 3:ho - 3 + ln], scalar1=V(2, c), scalar2=V(6, c), op0=ALU.mult, op1=ALU.add),
                    r=[('bufB', c), 'lruv'], w=tk(0))
                for kk in (1, 2, 3):
                    P.op('dve', lambda e, c=c, o=o, ln=ln, ho=ho, kk=kk: e.scalar_tensor_tensor(
                        out=u[:, o:o + ln], in0=bufB[:, c, ho - 3 + kk:ho - 3 + kk + ln], scalar=V(2 + kk, c), in1=u[:, o:o + ln],
                        op0=ALU.mult, op1=ALU.add),
                        r=[('bufB', c), 'lruv'] + tk(0), w=tk(0))
            P.op('act', lambda e: e.activation(out=tmpb[:, :], in_=u[:, :], func=AF.Identity), r=tk(0), w=['tmpb'])
            for (lo, n) in BLKS:
                for gi, (dst, di, bidx) in enumerate([(rr, 1, 7), (ig, 2, 8)]):
                    b = bank()
                    P.op('pe', lambda e, b=b, gi=gi, wt=wt, lo=lo, n=n: e.matmul(banks[b][:, 0:n], lhsT=wt[:, gi, :], rhs=tmpb[:, lo:lo + n],
                                                                                 start=True, stop=True),
                         r=[wkey, 'tmpb'], w=[('bank', b)])
                    P.op('act', lambda e, b=b, dst=dst, bidx=bidx, c=c, lo=lo, n=n: e.activation(
                        out=dst[:, lo:lo + n], in_=banks[b][:, 0:n], func=AF.Sigmoid, bias=V(bidx, c)),
                        r=[('bank', b), 'lruv'], w=tk(di))
            P.op('act', lambda e, c=c: e.activation(out=aa[:, :], in_=rr[:, :], func=AF.Exp, scale=cfac[:, j, 1, c:c + 1]),
                 r=tk(1) + ['cfac'], w=tk(3))
            P.op('act', lambda e: e.activation(out=aa[:, :], in_=aa[:, :], func=AF.Sqrt, scale=-1.0, bias=small[:, 1:2]),
                 r=tk(3) + ['small'], w=tk(3))
            P.op('dve', lambda e: e.tensor_tensor(out=ig[:, :], in0=ig[:, :], in1=u[:, :], op=ALU.mult), r=tk(2) + tk(0), w=tk(2))
            P.op('dve', lambda e: e.tensor_tensor(out=ig[:, :], in0=ig[:, :], in1=aa[:, :], op=ALU.mult), r=tk(2) + tk(3), w=tk(2))
            P.op('act', lambda e, c=c: e.activation(out=aa[:, :], in_=rr[:, :], func=AF.Exp, scale=cfac[:, j, 0, c:c + 1]),
                 r=tk(1) + ['cfac'], w=tk(3))
            P.op('dve', lambda e, c=c: e.tensor_tensor_scan(out=sc_[:, 0:TP], data0=aa[:, 0:TP], data1=zeros[:, :], initial=1.0,
                                                            op0=ALU.mult, op1=ALU.add), r=tk(3) + ZK, w=tk(4))
            P.op('dve', lambda e, c=c: e.tensor_copy(out=summ[:, 8 + c:9 + c], in_=sc_[:, TP - 1:TP]), r=tk(4), w=['summ'])
            P.op('dve', lambda e, c=c: e.tensor_tensor(out=bufB[:, c, 0:TP], in0=bufA[:, c, 0:TP], in1=sc_[:, 0:TP], op=ALU.mult),
                 r=keys('bufA', c, 0, TP) + tk(4), w=[('bufB', c)])
            for (s, (o, ln)) in enumerate(SEQS):
                P.op('dve', lambda e, c=c, s=s, o=o, ln=ln: e.tensor_tensor_scan(
                    out=sc_[:, o:o + ln], data0=aa[:, o:o + ln], data1=ig[:, o:o + ln], initial=hin[:, c, s:s + 1],
                    op0=ALU.mult, op1=ALU.add), r=tk(3) + tk(2) + ['hin'], w=tk(4))
            for (s, (o, ln)) in enumerate(SEQS):
                P.op('dve', lambda e, c=c, s=s, o=o, ln=ln: e.tensor_copy(out=hl[:, c, s:s + 1], in_=sc_[:, o + ln - 1:o + ln]),
                     r=tk(4), w=['hl'])
            P.op('dve', lambda e, c=c: e.tensor_tensor(out=bufA[:, c, :], in0=bufA[:, c, :], in1=sc_[:, :], op=ALU.mult),
                 r=keys('bufA', c, 0, T) + tk(4), w=keys('bufA', c, 0, T))
        P.op('dve', lambda e: e.tensor_copy(out=summ[:, 16:24], in_=hl[:, :, 0]), r=['hl'], w=['summ'])
        exchange(summ[:, 8:24], 16, ['summ'])
        hp = small[:, 32:48]
        P.op('dve', lambda e: e.memset(hp[:, 0:8], 0.0), w=['pt'])
        for r_ in range(4):
            P.op('dve', lambda e, r_=r_: e.tensor_tensor(out=hp[:, 8:16], in0=hp[:, 0:8], in1=xg[:, r_, 0:8], op=ALU.mult),
                 r=['xg', 'pt'], w=['pt'])
            P.op('dve', lambda e, r_=r_: e.tensor_tensor(out=hp[:, 8:16], in0=hp[:, 8:16], in1=xg[:, r_, 8:16], op=ALU.add),
                 r=['xg', 'pt'], w=['pt'])
            P.op('dve', lambda e, r_=r_: e.tensor_tensor(out=hp[:, 8:16], in0=hp[:, 8:16], in1=hp[:, 0:8], op=ALU.subtract),
                 r=['pt'], w=['pt'])
            P.op('dve', lambda e, r_=r_: e.scalar_tensor_tensor(out=hp[:, 0:8], in0=hp[:, 8:16], scalar=mk[:, 4 + r_:5 + r_], in1=hp[:, 0:8],
                                                               op0=ALU.mult, op1=ALU.add), r=['pt', 'mk'], w=['pt'])
        P.op('dve', lambda e: e.tensor_tensor(out=hp[:, 8:16], in0=hp[:, 0:8], in1=summ[:, 8:16], op=ALU.mult), r=['pt', 'summ'], w=['pt'])
        P.op('dve', lambda e: e.tensor_tensor(out=hl[:, :, 0], in0=hl[:, :, 0], in1=hp[:, 8:16], op=ALU.add), r=['pt', 'hl'], w=['hl'])
        P.dma('sp', lambda e: e.dma_start(out=o_lh[:, j, :, :], in_=hl[:, :, :]), r=['hl'], w=['o_lh%d' % j])
        for c in range(NCH):
            P.op('dve', lambda e, c=c: e.scalar_tensor_tensor(out=bufA[:, c, 0:TP], in0=bufB[:, c, 0:TP], scalar=hp[:, c:c + 1],
                                                             in1=bufA[:, c, 0:TP], op0=ALU.mult, op1=ALU.add),
                 r=[('bufB', c), 'pt'] + keys('bufA', c, 0, TP), w=keys('bufA', c, 0, TP))
        out_proj_residual(lru_w_o[j], NCH, bufA, 'bufA', 1, bias_fn=lambda oc: V(10, oc))


    def exchange_big(src_ap, ncols, src_keys, name):
        xin = nc.dram_tensor("xB_in_" + name, [128, ncols], F32)
        xout = nc.dram_tensor("xB_out_" + name, [4 * 128, ncols], F32)
        P.dma('pool', lambda e: e.dma_start(out=xin.ap()[:, :], in_=src_ap), r=src_keys, w=['xB_in_' + name])
        import os
        if not os.environ.get('SSD_NOCOLL'):
            P.coll(lambda e: e.collective_compute("AllGather", ALU.bypass, replica_groups=GROUPS,
                                                  ins=[xin.ap().opt()], outs=[xout.ap().opt()]),
                   r=['xB_in_' + name], w=['xB_out_' + name])
        return xout.ap(), 'xB_out_' + name

    def ssd_layer(l):
        NCK = T // 64
        wdt = wsm[0][:, :, :].rearrange('p a n -> p (a n)').rearrange('p (c n) -> p c n', n=32)
        hflat = h[:, :, :].rearrange("p c t -> p (c t)")
        bflat = blk16[:, :, :].rearrange("p c n -> p (c n)")
        BKALL = [('blk16', c) for c in range(NCH)]
        zst = bflat[:, 0:256].bitcast(BF16)
        zsg = bflat[:, 256:768]
        for cg in range(4):
            t, key = load_panel(ssd_w_in, NCH, cg * 512, 512)
            for tt in range(T // 128):
                b = bank()
                for q in range(NCH):
                    P.op('pe', lambda e: e.matmul(banks[b][:, 0:512], lhsT=h[:, q, tt * 128:(tt + 1) * 128], rhs=t[:, q, :],
                                                  start=(q == 0), stop=(q == NCH - 1)),
                         r=[key] + keys('h', q, tt * 128, tt * 128 + 128), w=[('bank', b)])
                P.op('act', lambda e: e.activation(out=zsg, in_=banks[b][:, 0:512], func=AF.Sigmoid), r=[('bank', b)], w=['zsg'])
                P.op('dve', lambda e: e.tensor_tensor(out=zst, in0=zsg, in1=banks[b][:, 0:512], op=ALU.mult), r=[('bank', b), 'zsg'], w=['zst'])
                P.dma('sp', lambda e: e.dma_start(out=zd[tt * 128:(tt + 1) * 128, cg * 512:(cg + 1) * 512], in_=zst), r=['zst'], w=['zd'])
        P.dma('pool', lambda e: e.dma_start(out=wdt[:, :, :], in_=ssd_w_in.rearrange("(kc p) n -> p kc n", p=128)[:, :, 5120:5152]), w=['wdt'])
        dttm = tmpb[:, 0:T].bitcast(F32)
        dtv = dttm.rearrange("p (c e) -> p c e", e=32)
        for c0 in range(0, NCK, 16):
            ncq = min(16, NCK - c0)
            b = bank()
            for ci in range(c0, c0 + ncq):
                for q in range(NCH):
                    P.op('pe', lambda e: e.matmul(banks[b][0:64, (ci - c0) * 32:(ci - c0 + 1) * 32], lhsT=h[:, q, ci * 64:(ci + 1) * 64],
                                                  rhs=wdt[:, q, :], start=(q == 0), stop=(q == NCH - 1)),
                         r=['wdt'] + keys('h', q, ci * 64, ci * 64 + 64), w=[('bank', b)])
            P.op('dve', lambda e: e.tensor_tensor(out=dtv[0:64, c0:c0 + ncq, :], in0=banks[b][0:64, 0:ncq * 32].rearrange("p (c e) -> p c e", e=32),
                                                  in1=srep[0:64, 0:32].unsqueeze(1).to_broadcast([64, ncq, 32]), op=ALU.add),
                 r=[('bank', b), 'srep'], w=['dttm'])
        sp1 = bflat[0:64, 0:1088]
        sp2 = bflat[0:64, 1088:2176]
        P.op('act', lambda e: e.activation(out=sp1, in_=dttm[0:64, :], func=AF.Abs), r=['dttm', 'zsg', 'zst'], w=['sp1'])
        P.op('act', lambda e: e.activation(out=sp1, in_=sp1, func=AF.Exp, scale=-1.0), r=['sp1'], w=['sp1'])
        P.op('act', lambda e: e.activation(out=sp1, in_=sp1, func=AF.Ln, bias=small[0:64, 1:2]), r=['sp1', 'small'], w=['sp1'])
        P.op('dve', lambda e: e.tensor_scalar_max(out=sp2, in0=dttm[0:64, :], scalar1=0.0), r=['dttm'], w=['sp2'])
        P.op('dve', lambda e: e.tensor_tensor(out=dttm[0:64, :], in0=sp1, in1=sp2, op=ALU.add), r=['sp1', 'sp2'], w=['dttm'])
        P.fence()
        pre = tmpf0[:, 0:HW]
        acc = bflat[:, 0:T]
        sig = bflat[:, T:2 * T] if 2 * T <= 4096 else None
        sig = rstd
        SV = lambda idx, oc: scv[:, idx, oc:oc + 1]
        sst = sfx
        shal = bflat[:, 2 * 1088:2 * 1088 + 144]
        P.dma('sp', lambda e: e.dma_start(out=shal, in_=st_sconv[:, :, :, :].rearrange("p c s k -> p (c s k)")), w=['shal'])

        def evac_xbc(oc, lo, n, b):
            for (s_, a, b_) in seq_pieces(lo, n):
                off = HOFF[s_] - SEQS[s_][0]
                P.op('act', lambda e: e.activation(out=pre[:, a + off:b_ + off], in_=banks[b][:, a - lo:b_ - lo], func=AF.Identity),
                     r=[('bank', b)], w=['pre'])
            if lo + n == T:
                for (s_, (o, ln)) in enumerate(SEQS):
                    ho = HOFF[s_]
                    P.op('dve', lambda e: e.tensor_copy(out=stail[:, oc, s_, :], in_=pre[:, ho + ln - 3:ho + ln]), r=['pre'], w=['stail'])
                    if s_ == 0:
                        P.op('dve', lambda e: e.tensor_copy(out=sfirst[:, oc, :], in_=pre[:, ho:ho + 3]), r=['pre'], w=['sfirst'])
                        P.op('dve', lambda e: e.memset(pre[:, 0:3], 0.0), r=['pre'], w=['pre'])
                    else:
                        P.op('dve', lambda e: e.tensor_copy(out=pre[:, ho - 3:ho], in_=shal[:, oc * 6 + (s_ - 1) * 3:oc * 6 + (s_ - 1) * 3 + 3]),
                             r=['pre', 'shal'], w=['pre'])
                for (s_, (o, ln)) in enumerate(SEQS):
                    ho = HOFF[s_]
                    P.op('dve', lambda e: e.tensor_scalar(out=acc[:, o:o + ln], in0=pre[:, ho - 3:ho - 3 + ln], scalar1=SV(0, oc), scalar2=SV(4, oc),
                                                          op0=ALU.mult, op1=ALU.add), r=['pre', 'scv'], w=['acc'])
                    for kk in (1, 2, 3):
                        P.op('dve', lambda e: e.scalar_tensor_tensor(out=acc[:, o:o + ln], in0=pre[:, ho - 3 + kk:ho - 3 + kk + ln], scalar=SV(kk, oc),
                                                                     in1=acc[:, o:o + ln], op0=ALU.mult, op1=ALU.add),
                             r=['pre', 'scv', 'acc'], w=['acc'])
                sgt = hflat[:, 0:2 * T].bitcast(F32)
                P.op('act', lambda e: e.activation(out=pre[:, 0:T], in_=acc, func=AF.Sigmoid), r=['acc', 'pre'], w=['pre'])
                P.op('dve', lambda e: e.tensor_tensor(out=xbc[:, oc, :], in0=pre[:, 0:T], in1=acc, op=ALU.mult), r=['pre', 'acc'],
                     w=keys('xbc', oc, 0, T))
        project(ssd_w_in[:, 2048:5120], NCH, 3072, h, 'h', evac_xbc)
        P.dma('sp', lambda e: e.dma_start(out=o_sconv[:, :, :, :], in_=stail[:, :, :, :]), r=['stail'], w=['o_sconv'])
        exchange(stail[:, :, 0, :], 72, ['stail'], dst_view=lambda d: d.rearrange('p (c k) -> p c k', k=3))
        masked_sum(sprev[:, :, :].rearrange("p c k -> p (c k)"), ['sprev'], 72, 0)
        Wk = lambda kk: scv[:, kk, :]
        U = lambda i: sfx[:, i, :]
        tt_ = sfx[:, 3, :]
        def TT2(out, a, b_, op):
            P.op('dve', lambda e: e.tensor_tensor(out=out, in0=a, in1=b_, op=op), r=['sfx', 'sprev', 'sfirst', 'scv'], w=['sfx'])
        terms = [[(0, ('p', 0)), (1, ('p', 1)), (2, ('p', 2)), (3, ('x', 0))],
                 [(0, ('p', 1)), (1, ('p', 2)), (2, ('x', 0)), (3, ('x', 1))],
                 [(0, ('p', 2)), (1, ('x', 0)), (2, ('x', 1)), (3, ('x', 2))]]
        for ti in range(3):
            for n_, (kk, (src, idx)) in enumerate(terms[ti]):
                v = sprev[:, :, idx] if src == 'p' else sfirst[:, :, idx]
                if n_ == 0:
                    TT2(U(ti), Wk(kk), v, ALU.mult)
                    TT2(U(ti), U(ti), scv[:, 4, :], ALU.add)
                else:
                    TT2(tt_, Wk(kk), v, ALU.mult)
                    TT2(U(ti), U(ti), tt_, ALU.add)
            P.op('act', lambda e: e.activation(out=tt_, in_=U(ti), func=AF.Sigmoid), r=['sfx'], w=['sfx'])
            P.op('dve', lambda e: e.tensor_tensor(out=xbc[:, :, ti], in0=U(ti), in1=tt_, op=ALU.mult), r=['sfx'],
                 w=[kk_ for c in range(24) for kk_ in keys('xbc', c, 0, 3)])
        P.fence()
        ST = hflat[:, 0:4096].bitcast(F32)
        ytm = hflat[:, 4096:8192].bitcast(F32)
        xtm = hflat[:, 8192:10240]
        xdt = hflat[:, 10240:12288]
        xdd = hflat[:, 12288:14336]
        Sb = hflat[:, 14336:16384]
        zch = bflat[:, 0:1024].bitcast(BF16)
        Lt = bflat[:, 1024:1536]
        Mt = bflat[:, 1536:1792].bitcast(BF16)
        Btm = bflat[:, 1792:2048].bitcast(BF16)
        mG = bflat[:, 2048:2112]
        dgx = bflat[:, 2112:2624]
        sm = bflat[:, 2624:2944]
        dA, cum, dend, ecum = sm[:, 0:32], sm[:, 32:64], sm[:, 64:96], sm[:, 96:128]
        tot, cd, totall, rs = sm[:, 128:160], sm[:, 160:192], sm[:, 192:224], sm[:, 224:226]
        wn = xdt
        tpb = [banks[i][:, :].bitcast(BF16) for i in range(8)]
        Sx = tmpf0[:, 0:2080]
        idf, trif, onef = cstf[:, 0, :], cstf[:, 1, :], cstf[:, 2, :]
        br3 = lambda ap_, n0, n1, n2: ap_.unsqueeze(2).to_broadcast([n0, n1, n2])

        def chunk(ci, full):
            t0 = ci * 64
            g8 = lambda ap_, g: ap_[0:64, g * 8:(g + 1) * 8]
            P.op('dve', lambda e: e.tensor_tensor(out=dA[0:64, :], in0=dtv[0:64, ci, :], in1=srep[0:64, 32:64], op=ALU.mult),
                 r=['dttm', 'srep'], w=['dA'])
            b = bank()
            P.op('pe', lambda e: e.matmul(banks[b][0:64, 0:32], lhsT=trif[0:64, 0:64], rhs=dA[0:64, :], start=True, stop=True),
                 r=['cstf', 'dA'], w=[('bank', b)])
            P.op('pe', lambda e: e.matmul(banks[b][:, 32:64], lhsT=onef[0:64, 0:128], rhs=dA[0:64, :], start=True, stop=True),
                 r=['cstf', 'dA'], w=[('bank', b)])
            P.op('act', lambda e: e.activation(out=cum[0:64, :], in_=banks[b][0:64, 0:32], func=AF.Identity), r=[('bank', b)], w=['cum'])
            P.op('act', lambda e: e.activation(out=tot[:, :], in_=banks[b][:, 32:64], func=AF.Identity), r=[('bank', b)], w=['tot'])
            P.op('dve', lambda e: e.tensor_tensor(out=dend[0:64, :], in0=tot[0:64, :], in1=cum[0:64, :], op=ALU.subtract), r=['tot', 'cum'], w=['dend'])
            P.op('act', lambda e: e.activation(out=dend[0:64, :], in_=dend[0:64, :], func=AF.Exp), r=['dend'], w=['dend'])
            P.op('act', lambda e: e.activation(out=cd[:, :], in_=tot[:, :], func=AF.Exp), r=['tot'], w=['cd'])
            if full:
                P.op('act', lambda e: e.activation(out=ecum[0:64, :], in_=cum[0:64, :], func=AF.Exp), r=['cum'], w=['ecum'])
            else:
                P.op('dve', lambda e: e.tensor_tensor(out=totall[:, :], in0=totall[:, :], in1=tot[:, :], op=ALU.add), r=['tot', 'totall'], w=['totall'])
            for half in range(2):
                b = bank()
                for i in range(8):
                    xc = half * 8 + i
                    P.op('pe', lambda e: e.transpose(tpb[b][0:64, i * 128:(i + 1) * 128], xbc[:, xc, t0:t0 + 64], ident[:, :]),
                         r=keys('xbc', xc, t0, t0 + 64) + ['ident'], w=[('bank', b)])
                P.op('act', lambda e: e.activation(out=xtm[0:64, half * 1024:(half + 1) * 1024], in_=tpb[b][0:64, 0:1024], func=AF.Identity),
                     r=[('bank', b)], w=['xtm'])
            b = bank()
            for g in range(4):
                P.op('pe', lambda e: e.transpose(tpb[b][0:64, g * 128:(g + 1) * 128], xbc[:, 16 + g, t0:t0 + 64], ident[:, :]),
                     r=keys('xbc', 16 + g, t0, t0 + 64) + ['ident'], w=[('bank', b)])
            P.op('act', lambda e: e.activation(out=Btm[0:64, :], in_=tpb[b][0:64, 0:512], func=AF.Identity), r=[('bank', b)], w=['Btm'])
            x3 = lambda ap_: ap_[0:64, :].rearrange("p (e q) -> p e q", q=64)
            P.op('dve', lambda e: e.tensor_tensor(out=x3(xdt), in0=x3(xtm), in1=br3(dtv[0:64, ci, :], 64, 32, 64), op=ALU.mult),
                 r=['xtm', 'dttm'], w=['xdt'])
            P.op('dve', lambda e: e.tensor_tensor(out=x3(xdd), in0=x3(xdt), in1=br3(dend[0:64, :], 64, 32, 64), op=ALU.mult),
                 r=['xdt', 'dend'], w=['xdd'])
            for g in range(4):
                gs = slice(g * 512, (g + 1) * 512)
                if full:
                    bg = bank()
                    P.op('pe', lambda e: e.matmul(banks[bg][0:64, 0:64], lhsT=xbc[:, 16 + g, t0:t0 + 64], rhs=xbc[:, 20 + g, t0:t0 + 64],
                                                  start=True, stop=True),
                         r=keys('xbc', 16 + g, t0, t0 + 64) + keys('xbc', 20 + g, t0, t0 + 64), w=[('bank', bg)])
                    P.op('dve', lambda e: e.tensor_tensor(out=mG[0:64, :], in0=banks[bg][0:64, 0:64], in1=trif[0:64, 0:64], op=ALU.mult),
                         r=[('bank', bg), 'cstf'], w=['mG'])
                    d3 = dgx[0:64, :].rearrange("p (e q) -> p e q", q=64)
                    P.op('dve', lambda e: e.tensor_tensor(out=d3, in0=idf[0:64, 0:64].unsqueeze(1).to_broadcast([64, 8, 64]),
                                                          in1=br3(g8(cum, g), 64, 8, 64), op=ALU.mult), r=['cstf', 'cum'], w=['dgx'])
                    bc = bank()
                    P.op('pe', lambda e: e.matmul(banks[bc][0:64, 0:512], lhsT=onef[0:64, 0:64], rhs=dgx[0:64, :], start=True, stop=True),
                         r=['cstf', 'dgx'], w=[('bank', bc)])
                    L3 = Lt[0:64, :].rearrange("p (e q) -> p e q", q=64)
                    P.op('dve', lambda e: e.tensor_tensor(out=L3, in0=banks[bc][0:64, 0:512].rearrange("p (e q) -> p e q", q=64),
                                                          in1=br3(g8(cum, g), 64, 8, 64), op=ALU.subtract), r=[('bank', bc), 'cum'], w=['Lt'])
                    P.op('dve', lambda e: e.tensor_scalar_min(out=Lt[0:64, :], in0=Lt[0:64, :], scalar1=0.0), r=['Lt'], w=['Lt'])
                    P.op('act', lambda e: e.activation(out=Lt[0:64, :], in_=Lt[0:64, :], func=AF.Exp), r=['Lt'], w=['Lt'])
                    P.op('dve', lambda e: e.tensor_tensor(out=Mt[0:64, :].rearrange("p (e q) -> p e q", q=64), in0=L3,
                                                          in1=mG[0:64, :].unsqueeze(1).to_broadcast([64, 8, 64]), op=ALU.mult),
                         r=['Lt', 'mG'], w=['Mt'])
                    byd = bank()
                    for e_ in range(8):
                        P.op('pe', lambda e: e.matmul(banks[byd][0:64, e_ * 64:(e_ + 1) * 64], lhsT=Mt[0:64, e_ * 64:(e_ + 1) * 64],
                                                      rhs=xdt[0:64, g * 512 + e_ * 64:g * 512 + (e_ + 1) * 64], start=True, stop=True),
                             r=['Mt', 'xdt'], w=[('bank', byd)])
                    byo = bank()
                    P.op('pe', lambda e: e.matmul(banks[byo][0:64, 0:512], lhsT=xbc[:, 20 + g, t0:t0 + 64], rhs=Sb[:, gs], start=True, stop=True),
                         r=keys('xbc', 20 + g, t0, t0 + 64) + ['Sb'], w=[('bank', byo)])
                    y3 = ytm[0:64, gs].rearrange("p (e q) -> p e q", q=64)
                    P.op('dve', lambda e: e.tensor_tensor(out=y3, in0=banks[byo][0:64, 0:512].rearrange("p (e q) -> p e q", q=64),
                                                          in1=br3(g8(ecum, g), 64, 8, 64), op=ALU.mult), r=[('bank', byo), 'ecum'], w=['ytm'])
                    P.op('dve', lambda e: e.tensor_tensor(out=ytm[0:64, gs], in0=ytm[0:64, gs], in1=banks[byd][0:64, 0:512], op=ALU.add),
                         r=[('bank', byd), 'ytm'], w=['ytm'])
                bs = bank()
                P.op('pe', lambda e: e.matmul(banks[bs][:, 0:512], lhsT=Btm[0:64, g * 128:(g + 1) * 128], rhs=xdd[0:64, gs], start=True, stop=True),
                     r=['Btm', 'xdd'], w=[('bank', bs)])
                S3 = ST[:, gs].rearrange("p (e q) -> p e q", q=64)
                P.op('dve', lambda e: e.tensor_tensor(out=S3, in0=S3, in1=br3(cd[:, g * 8:(g + 1) * 8], 128, 8, 64), op=ALU.mult),
                     r=['ST', 'cd', 'Sb'], w=['ST'])
                P.op('dve', lambda e: e.tensor_tensor(out=ST[:, gs], in0=ST[:, gs], in1=banks[bs][:, 0:512], op=ALU.add),
                     r=['ST', ('bank', bs)], w=['ST'])
            if full:
                P.op('act', lambda e: e.activation(out=Sb[:, :], in_=ST[:, :], func=AF.Identity), r=['ST', 'Sb'], w=['Sb'])
            if full:
                P.dma('sp', lambda e: e.dma_start(out=zch[0:64, :], in_=zd[t0:t0 + 64, :]), r=['zd', 'zch'], w=['zch'])
                P.op('dve', lambda e: e.tensor_tensor(out=x3(xdd), in0=x3(xtm), in1=br3(srep[0:64, 64:96], 64, 32, 64), op=ALU.mult),
                     r=['xtm', 'srep', 'xdd'], w=['xdd'])
                P.op('dve', lambda e: e.tensor_tensor(out=ytm[0:64, :], in0=ytm[0:64, :], in1=xdd[0:64, :], op=ALU.add), r=['ytm', 'xdd'], w=['ytm'])
                P.op('dve', lambda e: e.tensor_tensor(out=ytm[0:64, :], in0=ytm[0:64, :], in1=zch[0:64, :], op=ALU.mult), r=['ytm', 'zch'], w=['ytm'])
                P.op('act', lambda e: e.activation(out=xdd[0:64, :], in_=ytm[0:64, :], func=AF.Square, accum_out=rs[0:64, 0:1]),
                     r=['ytm', 'xdd'], w=['xdd', 'rs'])
                P.op('act', lambda e: e.activation(out=rs[0:64, 0:1], in_=rs[0:64, 0:1], func=AF.Sqrt, scale=1.0 / 2048.0, bias=small[0:64, 0:1]),
                     r=['rs', 'small'], w=['rs'])
                P.op('dve', lambda e: e.reciprocal(out=rs[0:64, 0:1], in_=rs[0:64, 0:1]), r=['rs'], w=['rs'])
                P.op('dve', lambda e: e.tensor_scalar(out=wn[0:64, :], in0=ytm[0:64, :], scalar1=rs[0:64, 0:1], scalar2=None, op0=ALU.mult),
                     r=['ytm', 'rs', 'xdt'], w=['xdt'])
                for half in range(2):
                    b = bank()
                    for i in range(8):
                        xc = half * 8 + i
                        P.op('pe', lambda e: e.transpose(tpb[b][:, i * 64:(i + 1) * 64], wn[0:64, xc * 128:(xc + 1) * 128], ident[0:64, 0:64]),
                             r=['xdt', 'ident'], w=[('bank', b)])
                    for i in range(8):
                        xc = half * 8 + i
                        P.op('act', lambda e: e.activation(out=xbc[:, xc, t0:t0 + 64], in_=tpb[b][:, i * 64:(i + 1) * 64], func=AF.Identity,
                                                           scale=sng[:, xc:xc + 1]),
                             r=[('bank', b), 'sng'], w=keys('xbc', xc, t0, t0 + 64))

        P.op('dve', lambda e: e.memset(ST[:, :], 0.0), w=['ST'])
        P.op('dve', lambda e: e.memset(totall[:, :], 0.0), w=['totall'])
        P.op('act', lambda e: e.activation(out=Sb[:, :], in_=ST[:, :], func=AF.Identity), r=['ST'], w=['Sb'])
        for ci in range(TP // 64):
            chunk(ci, False)
        P.op('dve', lambda e: e.tensor_copy(out=Sx[:, 0:2048], in_=ST[:, :]), r=['ST'], w=['Sx'])
        gath, gkey = exchange_big(Sx[:, 0:2048], 2048, ['Sx'], 'ssd')
        exchange(totall[:, :], 32, ['totall'])
        Sacc = ytm
        P.op('dve', lambda e: e.memset(Sacc[:, :], 0.0), w=['ytm'])
        for r_ in range(4):
            P.dma('sp', lambda e: e.dma_start(out=Sx[:, 0:2048], in_=gath[r_ * 128:(r_ + 1) * 128, :]), r=[gkey, 'Sx'], w=['Sx'])
            P.op('act', lambda e: e.activation(out=cd[:, :], in_=xg[:, r_, 0:32], func=AF.Exp), r=['xg'], w=['cd'])
            S3 = ST[:, :].rearrange("p (e q) -> p e q", q=64)
            A3 = Sacc[:, :].rearrange("p (e q) -> p e q", q=64)
            P.op('dve', lambda e: e.tensor_tensor(out=S3, in0=A3, in1=br3(cd[:, :], 128, 32, 64), op=ALU.mult), r=['ytm', 'cd', 'ST'], w=['ST'])
            P.op('dve', lambda e: e.tensor_tensor(out=ST[:, :], in0=ST[:, :], in1=Sx[:, 0:2048], op=ALU.add), r=['ST', 'Sx'], w=['ST'])
            P.op('dve', lambda e: e.tensor_tensor(out=ST[:, :], in0=ST[:, :], in1=Sacc[:, :], op=ALU.subtract), r=['ST', 'ytm'], w=['ST'])
            P.op('dve', lambda e: e.scalar_tensor_tensor(out=Sacc[:, :], in0=ST[:, :], scalar=mk[:, 4 + r_:5 + r_], in1=Sacc[:, :],
                                                         op0=ALU.mult, op1=ALU.add), r=['ST', 'mk', 'ytm'], w=['ytm'])
        P.op('dve', lambda e: e.tensor_copy(out=ST[:, :], in_=Sacc[:, :]), r=['ytm'], w=['ST'])
        P.op('act', lambda e: e.activation(out=Sb[:, :], in_=ST[:, :], func=AF.Identity), r=['ST'], w=['Sb'])
        for ci in range(TP // 64):
            chunk(ci, True)
        P.dma('sp', lambda e: e.dma_start(out=o_ssm[:, 0, :], in_=ST[:, :]), r=['ST'], w=['o_ssm0'])
        for s_ in (1, 2):
            P.dma('sp', lambda e: e.dma_start(out=ST[:, :], in_=st_ssm[:, s_ - 1, :]), r=['ST'], w=['ST'])
            P.op('act', lambda e: e.activation(out=Sb[:, :], in_=ST[:, :], func=AF.Identity), r=['ST'], w=['Sb'])
            chunk(TP // 64 + s_ - 1, True)
            P.dma('sp', lambda e: e.dma_start(out=o_ssm[:, s_, :], in_=ST[:, :]), r=['ST'], w=['o_ssm%d' % s_])
        P.fence()
        out_proj_residual(ssd_w_out, 16, xbc, 'xbc', 1, pw=256)


    k.bg = {'s': 0, 'o': 0, 'd': 0}

    def gbank(grp):
        base, n = {'s': (0, 4), 'o': (4, 2), 'd': (6, 2)}[grp]
        i = k.bg[grp]
        k.bg[grp] = (i + 1) % n
        return base + i

    def fox_layer(l):
        hflat = h[:, :, :].rearrange("p c t -> p (c t)")
        bflat = blk16[:, :, :].rearrange("p c n -> p (c n)")
        qT = big[:, 0:NCH * T].rearrange("p (c t) -> p c t", t=T)
        oT = big[:, NCH * T:2 * NCH * T].rearrange("p (c t) -> p c t", t=T)
        kvb = big[:, 2 * NCH * T:3 * NCH * T]
        Kc = [kvb[:, r_ * 2048:(r_ + 1) * 2048] for r_ in range(4)]
        Vc = [kvb[:, 8192 + r_ * 2048:8192 + (r_ + 1) * 2048].rearrange("p (t n) -> p t n", n=128) for r_ in range(4)]
        kst = bflat[:, 0:512]
        kb = bflat[:, 512:768].bitcast(BF16)
        vst = bflat[:, 768:1280]
        vb = bflat[:, 1280:1536].bitcast(BF16)
        lfa = bflat[:, 1536:1808]
        lfb = bflat[:, 1808:2080]
        import os
        FST = float(os.environ.get('FOX_STAGE', '9'))
        if FST <= 0.05:
            return
        def evac_q(oc, lo, n, b):
            P.op('act', lambda e: e.activation(out=qT[:, oc, lo:lo + n], in_=banks[b][:, 0:n], func=AF.Identity), r=[('bank', b)],
                 w=keys('qT', oc, lo, lo + n))
        project(fox_w_qkv[:, 0:D], NCH, D, h, 'h', evac_q)
        if FST <= 0.08:
            return

        def evac_k(oc, lo, n, b):
            P.op('act', lambda e: e.activation(out=kst[:, 0:n], in_=banks[b][:, 0:n], func=AF.Identity), r=[('bank', b)], w=['kst'])
            P.dma('sp', lambda e: e.dma_start(out=o_kT[:, oc, lo:lo + n], in_=kst[:, 0:n]), r=['kst'], w=['o_kT'])
            if FST <= 0.09:
                return
            if lo < TP:
                P.op('act', lambda e: e.activation(out=kb[:, 0:n], in_=banks[b][:, 0:n], func=AF.Identity), r=[('bank', b)], w=['kb'])
                if FST > 0.095:
                    P.dma('sp', lambda e: e.dma_start(out=kv_in[oc].ap()[:, lo:lo + n], in_=kb[:, 0:n]), r=['kb'], w=['kv_in%d' % oc])
            else:
                P.op('act', lambda e: e.activation(out=ksm[:, oc, :], in_=banks[b][:, 0:n], func=AF.Identity), r=[('bank', b)], w=['ksm'])
        project(fox_w_qkv[:, D:2 * D], NCH, D, h, 'h', evac_k)
        if FST <= 0.4:
            return
        for cg in range(2):
            t, key = load_panel(fox_w_qkv[:, 2 * D:3 * D], NCH, cg * 512, 512)
            for tt in range(17):
                b = bank()
                for q in range(NCH):
                    P.op('pe', lambda e: e.matmul(banks[b][:, 0:512], lhsT=h[:, q, tt * 128:(tt + 1) * 128], rhs=t[:, q, :],
                                                  start=(q == 0), stop=(q == NCH - 1)),
                         r=[key] + keys('h', q, tt * 128, tt * 128 + 128), w=[('bank', b)])
                P.op('act', lambda e: e.activation(out=vst, in_=banks[b][:, 0:512], func=AF.Identity), r=[('bank', b)], w=['vst'])
                P.dma('sp', lambda e: e.dma_start(out=o_v[tt * 128:(tt + 1) * 128, cg * 512:(cg + 1) * 512], in_=vst), r=['vst'], w=['o_v'])
                if tt < 16:
                    P.op('act', lambda e: e.activation(out=vb, in_=banks[b][:, 0:512], func=AF.Identity), r=[('bank', b)], w=['vb'])
                    for cl in range(4):
                        c_ = cg * 4 + cl
                        P.dma('sp', lambda e: e.dma_start(out=kv_in[c_].ap()[:, 2048 + tt * 128:2048 + (tt + 1) * 128],
                                                          in_=vb[:, cl * 128:(cl + 1) * 128]), r=['vb'], w=['kv_in%d' % c_])
                else:
                    P.op('act', lambda e: e.activation(out=vsm[:, cg * 512:(cg + 1) * 512], in_=banks[b][:, 0:512], func=AF.Identity), r=[('bank', b)], w=['vsm'])
        if FST <= 0.6:
            return
        P.dma('pool', lambda e: e.dma_start(out=wf[:, :, :], in_=fox_w_f.rearrange("(kc p) n -> p kc n", p=128)), w=['wf'])
        b = bank()
        for tt in range(17):
            for q in range(NCH):
                P.op('pe', lambda e: e.matmul(banks[b][:, tt * 16:(tt + 1) * 16], lhsT=h[:, q, tt * 128:(tt + 1) * 128], rhs=wf[:, q, :],
                                              start=(q == 0), stop=(q == NCH - 1)),
                     r=['wf'] + keys('h', q, tt * 128, tt * 128 + 128), w=[('bank', b)])
        lf2 = lftm[:, :, :].rearrange("p t e -> p (t e)")
        P.op('dve', lambda e: e.tensor_tensor(out=lftm[:, :, :], in0=banks[b][:, 0:272].rearrange("p (t e) -> p t e", e=16),
                                              in1=fbf[:, :].unsqueeze(1).to_broadcast([128, 17, 16]), op=ALU.add), r=[('bank', b), 'fbf'], w=['lftm'])
        P.op('act', lambda e: e.activation(out=lfa, in_=lf2, func=AF.Abs), r=['lftm'], w=['lfa'])
        P.op('act', lambda e: e.activation(out=lfa, in_=lfa, func=AF.Exp, scale=-1.0), r=['lfa'], w=['lfa'])
        P.op('act', lambda e: e.activation(out=lfa, in_=lfa, func=AF.Ln, bias=small[:, 1:2]), r=['lfa', 'small'], w=['lfa'])
        P.op('dve', lambda e: e.tensor_scalar_min(out=lfb, in0=lf2, scalar1=0.0), r=['lftm'], w=['lfb'])
        P.op('dve', lambda e: e.tensor_tensor(out=lf2, in0=lfb, in1=lfa, op=ALU.subtract), r=['lfa', 'lfb'], w=['lftm'])
        P.dma('sp', lambda e: e.dma_start(out=o_lf.rearrange("(t p) e -> p t e", p=128), in_=lftm[:, :, :]), r=['lftm'], w=['o_lf'])
        if FST <= 0.8:
            return
        for c in range(NCH):
            P.coll(lambda e: e.collective_compute("AllGather", ALU.bypass, replica_groups=GROUPS,
                                                  ins=[kv_in[c].ap().opt()], outs=[kv_out[c].ap().opt()]),
                   r=['kv_in%d' % c], w=['kv_out%d' % c])
        P.fence()
        import os
        if FST <= 1:
            return
        idf, trif, onef, tri2 = cstf[:, 0, :], cstf[:, 1, :], cstf[:, 2, :], cstf[:, 3, :]
        HF = hflat[:, :].bitcast(F32)
        biasAll = HF[:, 0:4096].rearrange("p (r q x) -> p r q x", r=4, q=4)
        Mband = hflat[:, 8192:16384].rearrange("p (r m x) -> p r m x", r=4, m=4)
        Fg = bflat[:, 0:1024].rearrange("p (r x) -> p r x", r=4)
        pay = bflat[:, 1024:1296]
        PTq = bflat[:, 1296:1360].rearrange("p (q e) -> p q e", e=16)
        base = bflat[:, 1360:1424].rearrange("p (r e) -> p r e", e=16)
        bme = bflat[:, 1424:1440]
        cA = bflat[:, 1440:1696].rearrange("p (r q e) -> p r q e", r=4, q=4)
        cC = bflat[:, 1696:1952].rearrange("p (r q e) -> p r q e", r=4, q=4)
        tmf = bflat[:, 1952:2464]
        bs_s = bflat[:, 2464:2976].rearrange("p (s x) -> p s x", s=2)
        bn_s = bflat[:, 2976:2992]
        lfp = bflat[:, 2992:3504].rearrange("p (s t e) -> p s t e", s=2, t=16)
        totp = bflat[:, 3504:3536].rearrange("p (s e) -> p s e", s=2)
        ptile = [bflat[:, 3536:3792].bitcast(BF16), bflat[:, 3792:4048].bitcast(BF16)]
        rden = tmpf0[:, 0:512]
        otmp = tmpf0[:, 512:1024]

        def cumsum_tiles(src3, ntile, dst_cols, tot_dst):
            b = bank()
            for tt in range(ntile):
                P.op('pe', lambda e: e.matmul(banks[b][:, tt * 16:(tt + 1) * 16], lhsT=trif, rhs=src3[:, tt, :], start=True, stop=(tt == 0)),
                     r=['cstf', 'lftm', 'lfp'], w=[('bank', b)])
                for t2 in range(tt):
                    P.op('pe', lambda e: e.matmul(banks[b][:, tt * 16:(tt + 1) * 16], lhsT=onef, rhs=src3[:, t2, :], start=False, stop=(t2 == tt - 1)),
                         r=['cstf', 'lftm', 'lfp'], w=[('bank', b)])
            for t2 in range(ntile):
                P.op('pe', lambda e: e.matmul(banks[b][:, 496:512], lhsT=onef, rhs=src3[:, t2, :], start=(t2 == 0), stop=(t2 == ntile - 1)),
                     r=['cstf', 'lftm', 'lfp'], w=[('bank', b)])
            P.op('act', lambda e: e.activation(out=dst_cols, in_=banks[b][:, 0:ntile * 16], func=AF.Identity), r=[('bank', b)], w=['pay', 'Fg', 'bs_s'])
            P.op('act', lambda e: e.activation(out=tot_dst, in_=banks[b][:, 496:512], func=AF.Identity), r=[('bank', b)], w=['pay', 'totp'])

        cumsum_tiles(lftm, 16, pay[:, 0:256], pay[:, 256:272])
        P.op('dve', lambda e: e.memset(PTq[:, 0, :], 0.0), w=['PTq'])
        b = bank()
        for qb in range(1, 4):
            for t2 in range(4 * qb):
                P.op('pe', lambda e: e.matmul(banks[b][:, qb * 16:(qb + 1) * 16], lhsT=onef, rhs=lftm[:, t2, :], start=(t2 == 0), stop=(t2 == 4 * qb - 1)),
                     r=['cstf', 'lftm'], w=[('bank', b)])
        P.op('act', lambda e: e.activation(out=PTq[:, 1:4, :], in_=banks[b][:, 16:64].rearrange("p (q e) -> p q e", e=16), func=AF.Identity),
             r=[('bank', b)], w=['PTq'])
        xgf = tmpf0[:, 0:1088].rearrange('p (r n) -> p r n', r=4)
        exchange(pay, 272, ['pay'], W=272, gdst=xgf, gkeys=['xgf'])
        P.op('dve', lambda e: e.memset(base[:, 0, :], 0.0), w=['base'])
        for r_ in range(1, 4):
            P.op('dve', lambda e: e.tensor_tensor(out=base[:, r_, :], in0=base[:, r_ - 1, :], in1=xgf[:, r_ - 1, 256:272], op=ALU.add),
                 r=['xgf', 'base'], w=['base'])
        for r_ in range(4):
            P.op('dve', lambda e: e.tensor_tensor(out=Fg[:, r_, :].rearrange("p (t e) -> p t e", e=16),
                                                  in0=xgf[:, r_, 0:256].rearrange("p (t e) -> p t e", e=16),
                                                  in1=base[:, r_, :].unsqueeze(1).to_broadcast([128, 16, 16]), op=ALU.add),
                 r=['xgf', 'base'], w=['Fg'])
        P.op('dve', lambda e: e.tensor_scalar(out=bme, in0=xgf[:, 0, 256:272], scalar1=mk[:, 4:5], scalar2=None, op0=ALU.mult), r=['xgf', 'mk'], w=['bme'])
        for r_ in range(1, 4):
            P.op('dve', lambda e: e.scalar_tensor_tensor(out=bme, in0=xgf[:, r_, 256:272], scalar=mk[:, 4 + r_:5 + r_], in1=bme, op0=ALU.mult, op1=ALU.add),
                 r=['xgf', 'mk', 'bme'], w=['bme'])
        for qb in range(4):
            P.op('dve', lambda e: e.tensor_tensor(out=PTq[:, qb, :], in0=PTq[:, qb, :], in1=bme, op=ALU.add), r=['PTq', 'bme'], w=['PTq'])
        for r_ in range(4):
            for qb in range(4):
                P.op('dve', lambda e: e.tensor_scalar(out=cA[:, r_, qb, :], in0=PTq[:, qb, :], scalar1=nbc[:, r_:r_ + 1], scalar2=None, op0=ALU.add),
                     r=['PTq', 'nbc'], w=['cA'])
                P.op('dve', lambda e: e.tensor_scalar(out=cC[:, r_, qb, :], in0=PTq[:, qb, :], scalar1=nbc[:, 4 + r_:5 + r_], scalar2=None, op0=ALU.add),
                     r=['PTq', 'nbc'], w=['cC'])
                na = min(16, 4 * qb + 4)
                bA = biasAll[:, r_, qb, 0:na * 16].rearrange("p (t e) -> p t e", e=16)
                P.op('dve', lambda e: e.tensor_tensor(out=bA, in0=cA[:, r_, qb, :].unsqueeze(1).to_broadcast([128, na, 16]),
                                                      in1=Fg[:, r_, 0:na * 16].rearrange("p (t e) -> p t e", e=16), op=ALU.subtract),
                     r=['cA', 'Fg'], w=['biasAll'])
                if na < 16:
                    bC = biasAll[:, r_, qb, na * 16:256].rearrange("p (t e) -> p t e", e=16)
                    P.op('dve', lambda e: e.tensor_tensor(out=bC, in0=cC[:, r_, qb, :].unsqueeze(1).to_broadcast([128, 16 - na, 16]),
                                                          in1=Fg[:, r_, na * 16:256].rearrange("p (t e) -> p t e", e=16), op=ALU.subtract),
                         r=['cC', 'Fg'], w=['biasAll'])
        for m_ in range(4):
            P.dma('sp', lambda e: e.dma_start(out=tmf, in_=tmk[:, m_, :]), r=['tmf'], w=['tmf'])
            for r_ in range(4):
                P.op('dve', lambda e: e.tensor_scalar(out=Mband[:, r_, m_, :], in0=tmf, scalar1=mk[:, 8 + r_:9 + r_], scalar2=mk[:, 4 + r_:5 + r_],
                                                      op0=ALU.mult, op1=ALU.add), r=['tmf', 'mk'], w=['Mband'])
        P.dma('sp', lambda e: e.dma_start(out=lfp.rearrange("p s t e -> p (s t e)"), in_=lfc[:, :, :, :].rearrange("p s t e -> p (s t e)")), w=['lfp'])
        for s_ in range(2):
            cumsum_tiles(lfp[:, s_, :, :], 16, bs_s[:, s_, :], totp[:, s_, :])
            P.op('dve', lambda e: e.tensor_tensor(out=bs_s[:, s_, :].rearrange("p (t e) -> p t e", e=16),
                                                  in0=totp[:, s_, :].unsqueeze(1).to_broadcast([128, 16, 16]),
                                                  in1=bs_s[:, s_, :].rearrange("p (t e) -> p t e", e=16), op=ALU.subtract),
                 r=['bs_s', 'totp'], w=['bs_s'])
        b = bank()
        P.op('pe', lambda e: e.matmul(banks[b][:, 0:16], lhsT=tri2, rhs=lftm[:, 16, :], start=True, stop=True), r=['cstf', 'lftm'], w=[('bank', b)])
        P.op('act', lambda e: e.activation(out=bn_s, in_=banks[b][:, 0:16], func=AF.Identity, scale=-1.0), r=[('bank', b)], w=['bn_s'])

        if FST <= 2:
            return
        ptile.append(tmpf0[:, 1024:1280].bitcast(BF16))
        NPT = 3
        pts = [ptile, [sq[:, i_, :] for i_ in range(3)]]
        daccs = [tmpf0[:, 1280:1792], rstd[:, 0:512]]
        SQK = [('sq', i_) for i_ in range(NSQ)]

        def attend2(keytiles, c, qlo, qn):
            bo = gbank('o')
            bd = gbank('d')
            nt = len(keytiles)
            sbk = [None] * nt

            def emit_s(i):
                Ks, bias2, Vs, band, kp0, nk = keytiles[i]
                bb = (gbank('s'), gbank('s'))
                sbk[i] = bb
                for hh in range(2):
                    pb = 64 * hh
                    P.op('pe', lambda e: e.matmul(banks[bb[hh]][kp0:kp0 + nk, 0:qn], lhsT=Ks[pb:pb + 64, :], rhs=qT[pb:pb + 64, c, qlo:qlo + qn],
                                                  start=True, stop=True),
                         r=['Kc', 'ksm'] + keys('qT', c, qlo, qlo + qn), w=[('bank', bb[hh])])
            emit_s(0)
            for i, (Ks, bias2, Vs, band, kp0, nk) in enumerate(keytiles):
                bb = sbk[i]
                for hh in range(2):
                    pt_ = pts[hh][i % NPT]
                    pk = [('pt', hh, i % NPT)] + (SQK if hh == 1 else [])
                    P.op('act', lambda e: e.activation(out=pt_[kp0:kp0 + nk, 0:qn], in_=banks[bb[hh]][kp0:kp0 + nk, 0:qn], func=AF.Exp, scale=0.125,
                                                       bias=bias2[hh]),
                         r=[('bank', bb[hh]), 'biasAll', 'bs_s', 'bn_s'], w=pk)
                    if band is not None:
                        P.op('dve', lambda e: e.tensor_tensor(out=pt_[kp0:kp0 + nk, 0:qn], in0=pt_[kp0:kp0 + nk, 0:qn], in1=band, op=ALU.mult),
                             r=pk + ['Mband', 't64b'], w=pk)
                if i + 1 < nt:
                    emit_s(i + 1)
                for hh in range(2):
                    pb = 64 * hh
                    pt_ = pts[hh][i % NPT]
                    pk = [('pt', hh, i % NPT)] + (SQK if hh == 1 else [])
                    P.op('pe', lambda e: e.matmul(banks[bo][pb:pb + 64, 0:qn], lhsT=Vs[:, pb:pb + 64], rhs=pt_[kp0:kp0 + nk, 0:qn],
                                                  start=(i == 0), stop=(i == nt - 1)),
                         r=pk + ['Vc', 'vsm'], w=[('bank', bo)])
                    da = daccs[hh]
                    dk = ['dacc%d' % hh] + (['rstd'] if hh == 1 else [])
                    if kp0 == 0 and nk == 128 and i == 0:
                        P.op('dve', lambda e: e.tensor_copy(out=da[:, 0:qn], in_=pt_[:, 0:qn]), r=pk, w=dk)
                    else:
                        P.op('dve', lambda e: e.tensor_tensor(out=da[kp0:kp0 + nk, 0:qn], in0=da[kp0:kp0 + nk, 0:qn], in1=pt_[kp0:kp0 + nk, 0:qn], op=ALU.add),
                             r=pk + dk, w=dk)
            for hh in range(2):
                pb = 64 * hh
                dk = ['dacc%d' % hh] + (['rstd'] if hh == 1 else [])
                P.op('pe', lambda e: e.matmul(banks[bd][pb:pb + 64, 0:qn], lhsT=onef[:, 0:64], rhs=daccs[hh][:, 0:qn], start=True, stop=True),
                     r=dk + ['cstf'], w=[('bank', bd)])
            P.op('dve', lambda e: e.reciprocal(out=rden[:, 0:qn], in_=banks[bd][:, 0:qn]), r=[('bank', bd)], w=['rden'])
            P.op('act', lambda e: e.activation(out=otmp[:, 0:qn], in_=banks[bo][:, 0:qn], func=AF.Identity), r=[('bank', bo)], w=['otmp'])
            P.op('dve', lambda e: e.tensor_tensor(out=oT[:, c, qlo:qlo + qn], in0=otmp[:, 0:qn], in1=rden[:, 0:qn], op=ALU.mult),
                 r=['rden', 'otmp'], w=keys('oT', c, qlo, qlo + qn))

        for c in range(NCH):
            for r_ in range(4):
                P.dma('sp', lambda e: e.dma_start(out=Kc[r_], in_=kv_out[c].ap()[r_ * 128:(r_ + 1) * 128, 0:2048]), r=['kv_out%d' % c], w=['Kc'])
                P.dma('sp', lambda e: e.dma_start(out=Vc[r_], in_=kv_out[c].ap()[r_ * 128:(r_ + 1) * 128, 2048:4096].rearrange("p (t n) -> p t n", n=128)),
                      r=['kv_out%d' % c], w=['Vc'])
            for qb in range(4):
                tiles = []
                for r_ in range(4):
                    for kt in range(16):
                        m_ = kt - 4 * qb
                        if r_ == 3 and m_ >= 4:
                            continue
                        band = Mband[:, r_, m_, :] if 0 <= m_ < 4 else None
                        bcols = tuple(biasAll[:, r_, qb, kt * 16 + 2 * c + hh:kt * 16 + 2 * c + hh + 1] for hh in range(2))
                        tiles.append((Kc[r_][:, kt * 128:(kt + 1) * 128], bcols, Vc[r_][:, kt, :], band, 0, 128))
                attend2(tiles, c, qb * 512, 512)
            for s_ in range(2):
                P.dma('pool', lambda e: e.dma_start(out=Kc[0], in_=kcT[:, s_, c, :]), w=['Kc'])
                P.dma('pool', lambda e: e.dma_start(out=Vc[0], in_=vc[s_, :, c * 128:(c + 1) * 128].rearrange("(t p) n -> p t n", p=128)), w=['Vc'])
                kp0 = 64 * s_
                tiles = []
                for kt in range(16):
                    bcols = tuple(bs_s[:, s_, kt * 16 + 2 * c + hh:kt * 16 + 2 * c + hh + 1] for hh in range(2))
                    tiles.append((Kc[0][:, kt * 128:(kt + 1) * 128], bcols, Vc[0][:, kt, :], None, 0, 128))
                bcols = tuple(bn_s[kp0:kp0 + 64, 2 * c + hh:2 * c + hh + 1] for hh in range(2))
                tiles.append((ksm[:, c, kp0:kp0 + 64], bcols, vsm[kp0:kp0 + 64, c * 128:(c + 1) * 128], t64b[kp0:kp0 + 64, :], kp0, 64))
                attend2(tiles, c, TP + s_ * TS, TS)
        P.fence()
        out_proj_residual(fox_w_o, NCH, oT, 'oT', 1)

    def ffn_layer(l):
        FV = lambda idx, fc: ffv[:, l, idx, fc:fc + 1]
        BK = [('blk16', c) for c in range(NCH)]
        gpre = tmpf0[:, 0:T]
        hal = small[:, 24:32]
        stf = fix
        P.dma('sp', lambda e: e.dma_start(out=ftmp[:, :, :].rearrange("p a c -> p (a c)"),
                                          in_=st_ffn[:, l, :, :, :].rearrange("p c s k -> p (c s k)")), w=['ftmp'])
        stv = ftmp[:, :, :].rearrange("p a c -> p (a c)").rearrange("p (c s k) -> p c s k", s=2, k=2)
        for fc in range(NFF):
            pn, o = fc // 4, fc % 4
            if o == 0:
                ncols = min(512, DFF - pn * 512)
                tg, kg = load_panel(ffn_w_gate[l], NCH, pn * 512, ncols)
                tu, ku = load_panel(ffn_w_up[l], NCH, pn * 512, ncols)
            for (lo, n) in BLKS:
                b = bank()
                for q in range(NCH):
                    P.op('pe', lambda e, b=b, o=o, q=q, tg=tg, lo=lo, n=n: e.matmul(
                        banks[b][:, 0:n], lhsT=tg[:, q, o * 128:(o + 1) * 128], rhs=h[:, q, lo:lo + n],
                        start=(q == 0), stop=(q == NCH - 1)), r=[kg] + keys('h', q, lo, lo + n), w=[('bank', b)])
                P.op('act', lambda e, b=b, lo=lo, n=n: e.activation(out=gpre[:, lo:lo + n], in_=banks[b][:, 0:n], func=AF.Identity),
                     r=[('bank', b)], w=tk(0))
            for (s, (so, ln)) in enumerate(SEQS):
                P.op('dve', lambda e, fc=fc, s=s, so=so, ln=ln: e.tensor_copy(out=gt[:, fc, s, :], in_=gpre[:, so + ln - 2:so + ln]),
                     r=tk(0), w=['gt'])
            P.op('dve', lambda e, fc=fc: e.tensor_copy(out=fix[:, fc, 0:2], in_=gpre[:, 0:2]), r=tk(0), w=['fix'])
            gc = blk16[:, :, :].rearrange('p c n -> p (c n)')[:, 0:T]
            P.op('dve', lambda e, fc=fc: e.tensor_scalar(out=gc[:, :], in0=gpre[:, :], scalar1=FV(2, fc), scalar2=FV(3, fc),
                                                        op0=ALU.mult, op1=ALU.add), r=tk(0) + ['ffv'], w=BK)
            for (s, (so, ln)) in enumerate(SEQS):
                for d_, wi in ((1, 1), (2, 0)):
                    P.op('dve', lambda e, fc=fc, so=so, ln=ln, d_=d_, wi=wi: e.scalar_tensor_tensor(
                        out=gc[:, so + d_:so + ln], in0=gpre[:, so:so + ln - d_], scalar=FV(wi, fc), in1=gc[:, so + d_:so + ln],
                        op0=ALU.mult, op1=ALU.add), r=tk(0) + ['ffv'] + BK, w=BK)
                if s > 0:
                    for (tcol, terms) in ((0, ((0, 0), (1, 1))), (1, ((0, 1),))):
                        for (wi, pk) in terms:
                            P.op('dve', lambda e, fc=fc, s=s, so=so, tcol=tcol, wi=wi, pk=pk: e.scalar_tensor_tensor(
                                out=gc[:, so + tcol:so + tcol + 1], in0=stv[:, fc, s - 1, pk:pk + 1], scalar=FV(wi, fc),
                                in1=gc[:, so + tcol:so + tcol + 1], op0=ALU.mult, op1=ALU.add),
                                r=['ftmp', 'ffv'] + BK, w=BK)
            P.op('act', lambda e: e.activation(out=gc[:, :], in_=gc[:, :], func=AF.Gelu), r=BK, w=BK)
            for (lo, n) in BLKS:
                b = bank()
                for q in range(NCH):
                    P.op('pe', lambda e, b=b, o=o, q=q, tu=tu, lo=lo, n=n: e.matmul(
                        banks[b][:, 0:n], lhsT=tu[:, q, o * 128:(o + 1) * 128], rhs=h[:, q, lo:lo + n],
                        start=(q == 0), stop=(q == NCH - 1)), r=[ku] + keys('h', q, lo, lo + n), w=[('bank', b)])
                if lo == 0:
                    P.op('act', lambda e, b=b, fc=fc: e.activation(out=fix[:, fc, 2:4], in_=banks[b][:, 0:2], func=AF.Identity),
                         r=[('bank', b)], w=['fix'])
                P.op('dve', lambda e, b=b, fc=fc, lo=lo, n=n: e.tensor_tensor(out=hid[:, fc, lo:lo + n], in0=gc[:, lo:lo + n], in1=banks[b][:, 0:n],
                                                                           op=ALU.mult),
                     r=[('bank', b)] + BK, w=keys('hid', fc, lo, lo + n))
        P.dma('sp', lambda e: e.dma_start(out=o_ffn[:, l, :, :, :], in_=gt[:, :, :, :]), r=['gt'], w=['o_ffn%d' % l])
        exchange(gt[:, :, 0, :], 2 * NFF, ['gt'], dst_view=lambda d: d.rearrange('p (c k) -> p c k', k=2))
        masked_sum(fprev[:, :, :].rearrange("p c k -> p (c k)"), ['fprev'], 2 * NFF, 0)
        W0, W1, W2, BB = ffv[:, l, 0, :], ffv[:, l, 1, :], ffv[:, l, 2, :], ffv[:, l, 3, :]
        g0, g1, t_ = ftmp[:, 0, :], ftmp[:, 1, :], ftmp[:, 2, :]
        TT = lambda out, a, b_, op, rk=(), wk=('ftmp',): P.op('dve', lambda e: e.tensor_tensor(out=out, in0=a, in1=b_, op=op),
                                                               r=['ftmp', 'fprev', 'fix', 'ffv'], w=list(wk))
        TT(g0, W0, fprev[:, :, 0], ALU.mult); TT(g0, g0, BB, ALU.add)
        TT(t_, W1, fprev[:, :, 1], ALU.mult); TT(g0, g0, t_, ALU.add)
        TT(t_, W2, fix[:, :, 0], ALU.mult); TT(g0, g0, t_, ALU.add)
        TT(g1, W0, fprev[:, :, 1], ALU.mult); TT(g1, g1, BB, ALU.add)
        TT(t_, W1, fix[:, :, 0], ALU.mult); TT(g1, g1, t_, ALU.add)
        TT(t_, W2, fix[:, :, 1], ALU.mult); TT(g1, g1, t_, ALU.add)
        P.op('act', lambda e: e.activation(out=ftmp[:, 0:2, :], in_=ftmp[:, 0:2, :], func=AF.Gelu), r=['ftmp'], w=['ftmp'])
        hkeys = [kk for fc in range(NFF) for kk in keys('hid', fc, 0, 2)]
        P.op('dve', lambda e: e.tensor_tensor(out=hid[:, :, 0], in0=g0, in1=fix[:, :, 2], op=ALU.mult), r=['ftmp', 'fix'], w=hkeys)
        P.op('dve', lambda e: e.tensor_tensor(out=hid[:, :, 1], in0=g1, in1=fix[:, :, 3], op=ALU.mult), r=['ftmp', 'fix'], w=hkeys)
        out_proj_residual(ffn_w_down[l], NFF, hid, 'hid', 3, pw=128)

    ada_precompute()
    P.fence()
    for l in range(nlayers):
        ada_layer(l)
        prenorm(0, 0)
        P.fence()
        if l in (0, 3):
            lru_layer(l, l // 3)
        elif l == 1:
            ssd_layer(l)
        elif l == 2:
            fox_layer(l)
        P.fence()
        prenorm(2, 3)
        ffn_layer(l)
        P.fence()

    for c in range(NCH):
        P.dma('sp', lambda e, c=c: e.dma_start(out=yT[:, c, :], in_=xd[:, c, :]), r=keys('xd', c, 0, T), w=['yT%d' % c])
    P.finish()
    st.close()
    return nc


def fm(a):
    sh = a.shape
    nch = sh[-1] // 128
    b = a.reshape(sh[:-1] + (nch, 128))
    return np.ascontiguousarray(np.moveaxis(b, -1, 0))


def unfm(a):
    b = np.moveaxis(a, 0, -1)
    return np.ascontiguousarray(b.reshape(b.shape[:-2] + (b.shape[-2] * 128,)))


NLAYERS = 4


def kernel(**inp):
    f = lambda n: np.asarray(inp[n], dtype=np.float32)
    nc = build(nlayers=NLAYERS)
    x_prompt, x_sample = f('x_prompt'), f('x_sample')
    lru_vec = np.stack([np.stack([f('lru_b_x')[j], f('lru_b_y')[j], f('lru_conv_w')[j][0], f('lru_conv_w')[j][1], f('lru_conv_w')[j][2],
                                   f('lru_conv_w')[j][3], f('lru_conv_b')[j], f('lru_b_a')[j], f('lru_b_i')[j], f('lru_lambda')[j],
                                   f('lru_b_o')[j]]) for j in range(2)])
    ffn_vec = np.concatenate([f('ffn_conv_w'), f('ffn_conv_b')[:, None, :]], 1)
    shared = dict(norm_gT=fm(f('norm_g')),
                  lru_w_x=f('lru_w_x'), lru_w_y=f('lru_w_y'), lru_w_o=f('lru_w_o'), lru_w_a=f('lru_w_a'), lru_w_i=f('lru_w_i'),
                  ssd_w_in=f('ssd_w_in'), ssd_w_out=f('ssd_w_out'),
                  ssd_cvT=fm(np.concatenate([f('ssd_conv_w'), f('ssd_conv_b')[None, :]], 0)),
                  ssd_rep=np.ascontiguousarray(np.tile(np.concatenate([f('ssd_dt_bias'), f('ssd_a_log'), f('ssd_d')])[None, :], (128, 1))),
                  ssd_ngT=fm(f('ssd_norm_g')),
                  cst=np.ascontiguousarray(np.stack([np.eye(128, dtype=np.float32), np.triu(np.ones((128, 128), np.float32)),
                                                     np.ones((128, 128), np.float32),
                                                     np.kron(np.eye(2, dtype=np.float32), np.triu(np.ones((64, 64), np.float32)))], 1)),
                  fox_w_qkv=f('fox_w_qkv'), fox_w_o=f('fox_w_o'), fox_w_f=f('fox_w_f'),
                  fox_bf=np.ascontiguousarray(np.tile(f('fox_b_f')[None, :], (128, 1))),
                  tmk=np.ascontiguousarray(np.stack([(m * 128 + np.arange(128)[:, None] <= np.arange(512)[None, :]).astype(np.float32)
                                                     for m in range(4)], 1)),
                  t64=np.ascontiguousarray(((np.arange(128)[:, None] % 64) <= np.arange(64)[None, :]).astype(np.float32)),
                  lru_vT=fm(lru_vec), ffn_vT=fm(ffn_vec))
    cs_all = np.concatenate([np.stack([f('c_prompt')[cc // 4], f('c_sample')[2 * cc], f('c_sample')[2 * cc + 1]], 0) for cc in range(NCORES)], 0)
    shared['cT_all'] = np.ascontiguousarray(cs_all.reshape(24, NCH, 128).transpose(2, 1, 0))
    for i in range(4):
        shared['ffn_w_gate%d' % i] = np.ascontiguousarray(f('ffn_w_gate')[i])
        shared['ffn_w_up%d' % i] = np.ascontiguousarray(f('ffn_w_up')[i])
        shared['ffn_w_down%d' % i] = np.ascontiguousarray(f('ffn_w_down')[i])
    in_maps = []
    for c in range(NCORES):
        b, j = c // 4, c % 4
        xs = np.concatenate([x_prompt[b, j * TP:(j + 1) * TP], x_sample[2 * c], x_sample[2 * c + 1]], 0)
        xT = np.ascontiguousarray(xs.reshape(T, NCH, 128).transpose(2, 1, 0))
        cs = np.stack([f('c_prompt')[b], f('c_sample')[2 * c], f('c_sample')[2 * c + 1]], 0)
        cT = np.ascontiguousarray(cs.reshape(3, NCH, 128).transpose(2, 1, 0))
        msk = np.zeros((128, 16), np.float32)
        for r in range(4):
            msk[:, r] = 1.0 if r == j - 1 else 0.0
            msk[:, 4 + r] = 1.0 if r < j else 0.0
            msk[:, 8 + r] = 1.0 if r == j else 0.0
            msk[:, 12 + r] = 1.0 if r <= j else 0.0
        ck = f('cache_l2_k')[2 * c:2 * c + 2].reshape(2, 2048, D)
        kcT = np.ascontiguousarray(ck.transpose(0, 2, 1).reshape(2, NCH, 128, 2048).transpose(2, 0, 1, 3))
        vc_ = np.ascontiguousarray(f('cache_l2_v')[2 * c:2 * c + 2].reshape(2, 2048, D))
        lfc = np.ascontiguousarray(f('cache_l2_logf')[2 * c:2 * c + 2].reshape(2, 16, 128, 16).transpose(2, 0, 1, 3))
        lc = np.stack([f('state_l0_conv')[2 * c:2 * c + 2], f('state_l3_conv')[2 * c:2 * c + 2]])
        st_lconv = np.ascontiguousarray(lc.reshape(2, 2, 3, NCH, 128).transpose(4, 0, 3, 1, 2))
        lh = np.stack([f('state_l0_h')[2 * c:2 * c + 2], f('state_l3_h')[2 * c:2 * c + 2]])
        st_lh = np.ascontiguousarray(lh.reshape(2, 2, NCH, 128).transpose(3, 0, 2, 1))
        sf = f('state_ffn_conv')[:, 2 * c:2 * c + 2]
        st_ffn = np.ascontiguousarray(sf.reshape(4, 2, 2, NFF, 128).transpose(4, 0, 3, 1, 2))
        sc = f('state_l1_conv')[2 * c:2 * c + 2]
        st_sconv = np.ascontiguousarray(sc.reshape(2, 3, 24, 128).transpose(3, 2, 0, 1))
        ss = f('state_l1_ssm')[2 * c:2 * c + 2]
        st_ssm = np.ascontiguousarray(ss.transpose(3, 0, 1, 2).reshape(128, 2, 2048))
        m = dict(shared)
        m.update(st_sconv=st_sconv, st_ssm=st_ssm, kcT=kcT, vc=vc_, lfc=lfc)
        asel = np.zeros((128, 3, 24), np.float32)
        for si in range(3):
            asel[:, si, 3 * c + si] = 1.0
        m.update(ada_ws=np.ascontiguousarray(f('ada_w')[:, :, c * 768:(c + 1) * 768]),
                 ada_bs=fm(np.ascontiguousarray(f('ada_b')[:, c * 768:(c + 1) * 768])), asel=asel)
        m.update(xT=xT, msk=msk, st_lconv=st_lconv, st_lh=st_lh, st_ffn=st_ffn)
        in_maps.append(m)
    res = run_bass_kernel_spmd(nc, in_maps, core_ids=list(range(NCORES)))
    R = res.results
    B, DB = 2, 16
    y_prompt = np.zeros((B, 4 * TP, D), np.float32)
    y_sample = np.zeros((DB, TS, D), np.float32)
    p_lconv = [np.zeros((B, 3, D), np.float32) for _ in range(2)]
    p_lh = [np.zeros((B, D), np.float32) for _ in range(2)]
    s_lconv = [np.zeros((DB, 3, D), np.float32) for _ in range(2)]
    s_lh = [np.zeros((DB, D), np.float32) for _ in range(2)]
    p_sconv = np.zeros((B, 3, 3072), np.float32); s_sconv = np.zeros((DB, 3, 3072), np.float32)
    p_ssm = np.zeros((B, 32, 64, 128), np.float32); s_ssm = np.zeros((DB, 32, 64, 128), np.float32)
    p_k = np.zeros((B, 4 * TP, 16, 64), np.float32); p_v = np.zeros((B, 4 * TP, 16, 64), np.float32); p_lf = np.zeros((B, 4 * TP, 16), np.float32)
    s_k = np.zeros((DB, TS, 16, 64), np.float32); s_v = np.zeros((DB, TS, 16, 64), np.float32); s_lf = np.zeros((DB, TS, 16), np.float32)
    p_ffn = np.zeros((4, B, 2, DFF), np.float32)
    s_ffn = np.zeros((4, DB, 2, DFF), np.float32)
    for c in range(NCORES):
        b, j = c // 4, c % 4
        r = R[c]
        yt = r['yT'].transpose(2, 1, 0).reshape(T, D)
        y_prompt[b, j * TP:(j + 1) * TP] = yt[0:TP]
        y_sample[2 * c] = yt[TP:TP + TS]
        y_sample[2 * c + 1] = yt[TP + TS:T]
        lc = r['o_lconv']
        lh_ = r['o_lh']
        of = r['o_ffn']
        for jj in range(2):
            cv = lc[:, jj].transpose(2, 3, 1, 0).reshape(3, 3, D)
            hv = lh_[:, jj].transpose(2, 1, 0).reshape(3, D)
            if j == 3:
                p_lconv[jj][b] = cv[0]
                p_lh[jj][b] = hv[0]
            s_lconv[jj][2 * c], s_lconv[jj][2 * c + 1] = cv[1], cv[2]
            s_lh[jj][2 * c], s_lh[jj][2 * c + 1] = hv[1], hv[2]
        kk_ = r['o_kT'].transpose(2, 1, 0).reshape(T, 16, 64)
        vv_ = r['o_v'].reshape(T, 16, 64)
        ll_ = r['o_lf']
        p_k[b, j * TP:(j + 1) * TP], p_v[b, j * TP:(j + 1) * TP], p_lf[b, j * TP:(j + 1) * TP] = kk_[0:TP], vv_[0:TP], ll_[0:TP]
        for si in range(2):
            sl = slice(TP + si * TS, TP + (si + 1) * TS)
            s_k[2 * c + si], s_v[2 * c + si], s_lf[2 * c + si] = kk_[sl], vv_[sl], ll_[sl]
        scv_ = r['o_sconv'].transpose(2, 3, 1, 0).reshape(3, 3, 3072)
        ssm_ = r['o_ssm'].reshape(128, 3, 32, 64).transpose(1, 2, 3, 0)
        if j == 3:
            p_sconv[b] = scv_[0]
            p_ssm[b] = ssm_[0]
        s_sconv[2 * c], s_sconv[2 * c + 1] = scv_[1], scv_[2]
        s_ssm[2 * c], s_ssm[2 * c + 1] = ssm_[1], ssm_[2]
        fv = of.transpose(1, 3, 4, 2, 0).reshape(4, 3, 2, DFF)
        if j == 3:
            p_ffn[:, b] = fv[:, 0]
        s_ffn[:, 2 * c], s_ffn[:, 2 * c + 1] = fv[:, 1], fv[:, 2]
    z = lambda *sh: np.zeros(sh, np.float32)
    return (y_prompt, y_sample,
            p_lconv[0], p_lh[0], p_sconv, p_ssm, p_k, p_v, p_lf,
            p_lconv[1], p_lh[1], p_ffn,
            s_lconv[0], s_lh[0], s_sconv, s_ssm, s_k, s_v, s_lf,
            s_lconv[1], s_lh[1], s_ffn)
```
